# Optimizing a Trainium2 kernel written in Bass

```python
import jax, jax.numpy as jnp
from jax import lax
import numpy as np

D_MODEL = 4096
BATCH = 2
SEQ = 4096
DEPTH = 2

N_MIXERS = 2
N_HEADS = 32
HEAD_DIM = D_MODEL // N_HEADS
BRANCH = N_HEADS * HEAD_DIM
N_KV_HEADS = 4
GROUP = N_HEADS // N_KV_HEADS
Q_LORA = 1024
IDX_HEADS = 64
IDX_DIM = 128
TOPK_MAX = 256
FOX_HEADS = 32
FOX_HEAD_DIM = BRANCH // FOX_HEADS
BLOCK = 128
ROPE_THETA = 10000.0
LN_EPS = 1e-5
RMS_EPS = 1e-6
ALPHA = (2 * DEPTH) ** 0.25
BETA = (8 * DEPTH) ** -0.25
N_A = (DEPTH + 1) // 2
N_B = DEPTH // 2
A_SIZES = (Q_LORA, N_KV_HEADS * HEAD_DIM, N_KV_HEADS * HEAD_DIM, IDX_DIM, IDX_HEADS, BRANCH)
B_SIZES = (BRANCH, BRANCH, BRANCH, BRANCH, FOX_HEADS)
A_IN = sum(A_SIZES)
B_IN = sum(B_SIZES)

kernel_name = 'hybrid_dsa_fox_deepnorm'


def _offsets(sizes):
    out, acc = [], 0
    for s in sizes[:-1]:
        acc += s
        out.append(acc)
    return out


def _layernorm(x, g, b):
    xf = x.astype(jnp.float32)
    mu = jnp.mean(xf, axis=-1, keepdims=True)
    var = jnp.mean(jnp.square(xf - mu), axis=-1, keepdims=True)
    y = (xf - mu) * lax.rsqrt(var + LN_EPS) * g.astype(jnp.float32) + b.astype(jnp.float32)
    return y.astype(x.dtype)


def _rmsnorm(x, g):
    xf = x.astype(jnp.float32)
    y = xf * lax.rsqrt(jnp.mean(jnp.square(xf), axis=-1, keepdims=True) + RMS_EPS)
    return (y * g.astype(jnp.float32)).astype(x.dtype)


def _rope(x, pos):
    half = x.shape[-1] // 2
    inv = ROPE_THETA ** (-jnp.arange(half, dtype=jnp.float32) / half)
    ang = pos.astype(jnp.float32)[:, None] * inv[None, :]
    cos = jnp.cos(ang)[None, :, None, :]
    sin = jnp.sin(ang)[None, :, None, :]
    xf = x.astype(jnp.float32)
    x1, x2 = xf[..., :half], xf[..., half:]
    return jnp.concatenate([x1 * cos - x2 * sin, x2 * cos + x1 * sin], axis=-1).astype(x.dtype)


def _to_blocks(a):
    b, s = a.shape[:2]
    return a.reshape(b, s // BLOCK, BLOCK, *a.shape[2:]).swapaxes(0, 1)


def _from_blocks(a):
    nb, b = a.shape[:2]
    return a.swapaxes(0, 1).reshape(b, nb * BLOCK, *a.shape[3:])


def _dsa_mixer(x, w_in, q_norm_g, w_uq, kidx_g, kidx_b, w_out):
    b, s, _ = x.shape
    pos = jnp.arange(s)
    top = min(TOPK_MAX, s // 4)
    c_q, k, v, k_idx, w_idx, gate = jnp.split(x @ w_in, _offsets(A_SIZES), axis=-1)
    q_up = _rmsnorm(c_q, q_norm_g) @ w_uq
    q, q_idx = jnp.split(q_up, [BRANCH], axis=-1)
    q = _rope(q.reshape(b, s, N_HEADS, HEAD_DIM), pos)
    k = _rope(k.reshape(b, s, N_KV_HEADS, HEAD_DIM), pos)
    v = v.reshape(b, s, N_KV_HEADS, HEAD_DIM)
    q_idx = _rope(q_idx.reshape(b, s, IDX_HEADS, IDX_DIM), pos)
    k_idx = _rope(_layernorm(k_idx, kidx_g, kidx_b)[:, :, None, :], pos)[:, :, 0, :]
    w_idx = w_idx * (IDX_HEADS ** -0.5 * IDX_DIM ** -0.5)
    scale = HEAD_DIM ** -0.5

    def block(args):
        bi, qb, qib, wb = args
        qpos = bi * BLOCK + jnp.arange(BLOCK)
        dots = jnp.einsum('bqhd,bsd->bqhs', qib, k_idx).astype(jnp.float32)
        score = jnp.einsum('bqhs,bqh->bqs', jax.nn.relu(dots), wb.astype(jnp.float32))
        causal = pos[None, :] <= qpos[:, None]
        score = jnp.where(causal[None], score, -jnp.inf)
        _, idx = lax.top_k(score, top)
        kg = jax.vmap(lambda kk, ii: kk[ii])(k, idx)
        vg = jax.vmap(lambda vv, ii: vv[ii])(v, idx)
        valid = idx <= qpos[None, :, None]
        qg = qb.reshape(b, BLOCK, N_KV_HEADS, GROUP, HEAD_DIM)
        logits = jnp.einsum('bqgrd,bqkgd->bqgrk', qg, kg).astype(jnp.float32) * scale
        logits = jnp.where(valid[:, :, None, None, :], logits, -jnp.inf)
        p = jax.nn.softmax(logits, axis=-1).astype(v.dtype)
        o = jnp.einsum('bqgrk,bqkgd->bqgrd', p, vg)
        return o.reshape(b, BLOCK, BRANCH)

    nb = s // BLOCK
    o = lax.map(block, (jnp.arange(nb), _to_blocks(q), _to_blocks(q_idx), _to_blocks(w_idx)))
    o = _from_blocks(o)
    return (o * jax.nn.silu(gate)) @ w_out


def _fox_mixer(x, w_in, forget_bias, w_out):
    b, s, _ = x.shape
    pos = jnp.arange(s)
    q, k, v, gate, f_logit = jnp.split(x @ w_in, _offsets(B_SIZES), axis=-1)
    q = q.reshape(b, s, FOX_HEADS, FOX_HEAD_DIM)
    k = k.reshape(b, s, FOX_HEADS, FOX_HEAD_DIM)
    v = v.reshape(b, s, FOX_HEADS, FOX_HEAD_DIM)
    log_f = jax.nn.log_sigmoid(f_logit.astype(jnp.float32) + forget_bias.astype(jnp.float32))
    c = jnp.cumsum(log_f, axis=1)
    c_keys = c.transpose(0, 2, 1)
    scale = FOX_HEAD_DIM ** -0.5

    def block(args):
        bi, qb, cb = args
        qpos = bi * BLOCK + jnp.arange(BLOCK)
        logits = jnp.einsum('bqhd,bshd->bhqs', qb, k).astype(jnp.float32) * scale
        logits = logits + (cb.transpose(0, 2, 1)[:, :, :, None] - c_keys[:, :, None, :])
        causal = pos[None, :] <= qpos[:, None]
        logits = jnp.where(causal[None, None], logits, -jnp.inf)
        p = jax.nn.softmax(logits, axis=-1).astype(v.dtype)
        o = jnp.einsum('bhqs,bshd->bqhd', p, v)
        return o.reshape(b, BLOCK, BRANCH)

    nb = s // BLOCK
    o = _from_blocks(lax.map(block, (jnp.arange(nb), _to_blocks(q), _to_blocks(c))))
    return (o * jax.nn.silu(gate)) @ w_out


def setup_inputs(seed: int = 0) -> dict:
    key = jax.random.key(seed)
    ks = jax.random.split(key, 14)
    f32 = jnp.float32
    x = jax.random.normal(ks[0], (BATCH, SEQ, D_MODEL), f32)
    kv = N_KV_HEADS * HEAD_DIM
    a_scale = jnp.concatenate([jnp.ones((Q_LORA + kv,), f32), jnp.full((kv,), BETA, f32),
                               jnp.ones((IDX_DIM + IDX_HEADS + BRANCH,), f32)])
    a_w_in = jax.random.normal(ks[1], (N_A, D_MODEL, A_IN), f32) * D_MODEL ** -0.5 * a_scale
    a_q_norm_g = 1.0 + 0.01 * jax.random.normal(ks[2], (N_A, Q_LORA), f32)
    a_w_uq = jax.random.normal(ks[3], (N_A, Q_LORA, BRANCH + IDX_HEADS * IDX_DIM), f32) * Q_LORA ** -0.5
    a_kidx_norm_g = 1.0 + 0.01 * jax.random.normal(ks[4], (N_A, IDX_DIM), f32)
    a_kidx_norm_b = 0.01 * jax.random.normal(ks[5], (N_A, IDX_DIM), f32)
    a_w_out = jax.random.normal(ks[6], (N_A, BRANCH, D_MODEL), f32) * BRANCH ** -0.5 * BETA
    b_scale = jnp.concatenate([jnp.ones((2 * BRANCH,), f32), jnp.full((BRANCH,), BETA, f32),
                               jnp.ones((BRANCH + FOX_HEADS,), f32)])
    b_w_in = jax.random.normal(ks[7], (N_B, D_MODEL, B_IN), f32) * D_MODEL ** -0.5 * b_scale
    b_forget_bias = jax.random.uniform(ks[8], (N_B, FOX_HEADS), f32, minval=1.0, maxval=4.0)
    b_w_out = jax.random.normal(ks[9], (N_B, BRANCH, D_MODEL), f32) * BRANCH ** -0.5 * BETA
    ln_g = 1.0 + 0.01 * jax.random.normal(ks[10], (DEPTH, D_MODEL), f32)
    ln_b = 0.01 * jax.random.normal(ks[11], (DEPTH, D_MODEL), f32)
    return {'x': x, 'a_w_in': a_w_in, 'a_q_norm_g': a_q_norm_g, 'a_w_uq': a_w_uq,
            'a_kidx_norm_g': a_kidx_norm_g, 'a_kidx_norm_b': a_kidx_norm_b, 'a_w_out': a_w_out,
            'b_w_in': b_w_in, 'b_forget_bias': b_forget_bias, 'b_w_out': b_w_out,
            'ln_g': ln_g, 'ln_b': ln_b}


def reference(x, a_w_in, a_q_norm_g, a_w_uq, a_kidx_norm_g, a_kidx_norm_b, a_w_out,
              b_w_in, b_forget_bias, b_w_out, ln_g, ln_b):
    for i in range(DEPTH):
        j = i // N_MIXERS
        if i % N_MIXERS == 0:
            h = _dsa_mixer(x, a_w_in[j], a_q_norm_g[j], a_w_uq[j], a_kidx_norm_g[j],
                           a_kidx_norm_b[j], a_w_out[j])
        else:
            h = _fox_mixer(x, b_w_in[j], b_forget_bias[j], b_w_out[j])
        x = _layernorm(ALPHA * x + h, ln_g[i], ln_b[i])
    return x
```

```python
from concourse.bass_utils import run_bass_kernel_spmd
from contextlib import ExitStack
import numpy as np
import concourse.bass as bass
import concourse.mybir as mybir

F32 = mybir.dt.float32
BF16 = mybir.dt.bfloat16
AF = mybir.ActivationFunctionType
ALU = mybir.AluOpType
AX = mybir.AxisListType


class _St:
    __slots__ = ("w", "r", "wl")

    def __init__(self):
        self.w = None
        self.r = []
        self.wl = []


class KB:
    def __init__(self, nc, same_engine_sync=("act", "dve", "pool")):
        self.nc = nc
        self.es = ExitStack()
        self.engs = {"pe": nc.tensor, "dve": nc.vector, "act": nc.scalar,
                     "pool": nc.gpsimd, "sp": nc.sync}
        self.sem = {}
        self.cnt = {}
        self.waited = {k: {} for k in self.engs}
        for k in self.engs:
            self.sem[k] = self.es.enter_context(nc.semaphore(f"s_{k}"))
            self.cnt[k] = 0
        self.same = set(same_engine_sync)
        self.st = {}
        self.dsems = {}
        self.dcnt = {}
        self.n_inst = 0
        self.n_wait = 0
        self.scopes = []
        self.ncc = 0

    def _es(self):
        return self.scopes[-1] if self.scopes else self.es

    def sbuf(self, name, shape, dtype):
        self.nalloc = getattr(self, "nalloc", 0) + 1
        return self._es().enter_context(self.nc.sbuf_tensor(f"{name}_{self.nalloc}", list(shape), dtype))

    def psum(self, name, shape, dtype=F32):
        self.nalloc = getattr(self, "nalloc", 0) + 1
        return self._es().enter_context(self.nc.psum_tensor(f"{name}_{self.nalloc}", list(shape), dtype))

    def push(self):
        self.scopes.append(ExitStack())

    def pop(self):
        self.barrier()
        self.scopes.pop().close()
        self.st = {}

    def collective(self, kind, src, dst, groups, after=()):
        name = f"cc{self.ncc}"
        self.ncc += 1
        self.dsem(name)
        self._wait("pool", self._deps(after, []))
        inst = self.nc.gpsimd.collective_compute(kind, ALU.bypass, replica_groups=groups, ins=[src.opt()], outs=[dst.opt()])
        inst.then_inc(self.dsems[name], 1)
        self.dcnt[name] += 1
        self.n_inst += 1

    def dsem(self, name):
        if name not in self.dsems:
            self.dsems[name] = self.es.enter_context(self.nc.semaphore(f"d_{name}"))
            self.dcnt[name] = 0
        return name

    def _deps(self, R, W):
        deps = []
        for k in R:
            s = self.st.get(k)
            if s is not None and s.w is not None:
                deps.append(s.w)
            if s is not None:
                deps.extend(s.wl)
        for k in W:
            s = self.st.get(k)
            if s is not None:
                if s.w is not None:
                    deps.append(s.w)
                deps.extend(s.r)
        return deps

    def _wait(self, eng, deps):
        need = {}
        wt = self.waited[eng]
        for (sname, semh, val) in deps:
            if sname == eng and eng not in self.same:
                continue
            if wt.get(sname, 0) >= val:
                continue
            if need.get(sname, (None, 0))[1] < val:
                need[sname] = (semh, val)
        for sname, (semh, val) in need.items():
            self.engs[eng].wait_ge(semh, val)
            wt[sname] = val
            self.n_wait += 1

    def _commit(self, ev, R, W):
        for k in R:
            self.st.setdefault(k, _St()).r.append(ev)
        for k in W:
            s = self.st.setdefault(k, _St())
            s.w = ev
            s.r = []

    def op(self, eng, fn, R=(), W=()):
        W = list(W) + [k for k in R if k.startswith("pp")]
        R = [k for k in R if not k.startswith("pp")]
        self._wait(eng, self._deps(R, W))
        inst = fn(self.engs[eng])
        self.cnt[eng] += 1
        inst.then_inc(self.sem[eng], 1)
        self.n_inst += 1
        ev = (eng, self.sem[eng], self.cnt[eng])
        self._commit(ev, R, W)
        return inst

    def dma(self, parts, R=(), W=(), sem=None, indep=False):
        assert sem is not None
        self.dsem(sem)
        deps = self._deps(R, () if indep else W)
        if self.dcnt[sem] > 0:
            deps.append(("D" + sem, self.dsems[sem], self.dcnt[sem]))
        for q in dict.fromkeys(p[0] for p in parts):
            self._wait(q, deps)
        for (q, o, i) in parts:
            self.engs[q].dma_start(out=o, in_=i).then_inc(self.dsems[sem], 16)
            self.dcnt[sem] += 16
            self.n_inst += 1
        ev = ("D" + sem, self.dsems[sem], self.dcnt[sem])
        if indep:
            self._commit(ev, R, ())
            for k in W:
                self.st.setdefault(k, _St()).wl.append(ev)
        else:
            self._commit(ev, R, W)

    def finish(self, keys):
        self._wait("sp", self._deps(keys, ()))

    def barrier(self):
        deps = [(k, self.sem[k], self.cnt[k]) for k in self.engs if self.cnt[k] > 0]
        deps += [("D" + s, self.dsems[s], self.dcnt[s]) for s in self.dsems if self.dcnt[s] > 0]
        for e in self.engs:
            same = self.same
            self.same = set(self.engs)
            self._wait(e, deps)
            self.same = same

    def close(self):
        self.es.close()


import ml_dtypes

NPBF = ml_dtypes.bfloat16
D = 4096
A_IN = 6336
B_IN = 16416
SCALE = 128 ** -0.5
WSC = 64 ** -0.5 * 128 ** -0.5
NEG = -30000.0


def own_pos(S, j):
    NBc = S // 512
    return np.concatenate([np.arange(128) + (j + 4 * m) * 128 for m in range(NBc)])


def rope_tables(pos):
    inv = (10000.0 ** (-np.arange(64, dtype=np.float32) / 64)).astype(np.float32)
    ang = pos.astype(np.float32)[:, None] * inv[None, :]
    cos, sin = np.cos(ang).astype(np.float32), np.sin(ang).astype(np.float32)
    cosF = np.concatenate([cos.T, cos.T], 0)
    sinF = np.concatenate([-sin.T, sin.T], 0)
    return dict(cosF=np.ascontiguousarray(cosF), sinF=np.ascontiguousarray(sinF),
                cosT=np.ascontiguousarray(cos), sinT=np.ascontiguousarray(sin))


def tile_w(W, groups, KC, row=8192):
    out = np.zeros((len(groups), 128, row), np.float32)
    for g, (c0, width) in enumerate(groups):
        blk = W[:, c0:c0 + width].reshape(KC, 128, width).transpose(1, 0, 2).reshape(128, KC * width)
        out[g, :, :KC * width] = blk
    return out


A_GROUPS = [(g * 256, 256) for g in range(8)] + [(2048, 192)] + [(2240 + g * 256, 256) for g in range(16)]
UQ_GROUPS = [(g * 1024, 1024) for g in range(12)]
O_GROUPS = [(g * 256, 256) for g in range(16)]
B_GROUPS = [(g * 256, 256) for g in range(32)]
BV_GROUPS = [(8192 + g * 512, 512) for g in range(16)] + [(16384, 32)]


def consts():
    perm = np.zeros((128, 128), np.float32)
    for d in range(128):
        perm[(d + 64) % 128, d] = 1.0
    ident = np.eye(128, dtype=np.float32)
    return dict(perm=perm, ident=ident)


def emit_skewed(items, stage1, stage2, D):
    n = len(items)
    for t in range(n + D):
        if t < n:
            stage1(items[t], t)
        if t - D >= 0:
            stage2(items[t - D], t - D)


def dram_in(nc, name, shape, dt=F32, io=None):
    if io is not None and name in io:
        assert list(io[name].shape) == list(shape), (name, io[name].shape, shape)
        return io[name]
    return nc.dram_tensor(name, list(shape), dt, kind="ExternalInput").ap()


def dram_out(nc, name, shape, dt=F32, io=None):
    if io is not None and name in io:
        assert list(io[name].shape) == list(shape), (name, io[name].shape, shape)
        return io[name]
    return nc.dram_tensor(name, list(shape), dt, kind="ExternalOutput").ap()


def phase_a1(nc, S, stages=('ii', 'v', 'ki', 'gate', 'iv'), kb=None, io=None):
    NBc = S // 512
    T = NBc * 128
    TW = min(T, 512)
    TH = T // TW
    xT = dram_in(nc, "xT", [D, T], F32, io)
    w_in = dram_in(nc, "a_w_in", [25, 128, 8192], F32, io)
    w_uq = dram_in(nc, "a_w_uq", [12, 128, 8192], F32, io)
    qg = dram_in(nc, "qg", [128, 8], F32, io)
    kgb = dram_in(nc, "kgb", [128, 256], F32, io)
    cosF_d = dram_in(nc, "cosF", [128, T], F32, io)
    sinF_d = dram_in(nc, "sinF", [128, T], F32, io)
    cosT_d = dram_in(nc, "cosT", [T, 64], F32, io)
    sinT_d = dram_in(nc, "sinT", [T, 64], F32, io)
    perm_d = dram_in(nc, "perm", [128, 128], F32, io)
    ident_d = dram_in(nc, "ident", [128, 128], F32, io)
    qT_o = dram_out(nc, "qT", [32, 128, T], BF16, io)
    qiT_o = dram_out(nc, "qiT", [64, 128, T], BF16, io)
    kT_o = dram_out(nc, "kT", [4, 128, T], BF16, io)
    v_o = dram_out(nc, "v", [T, 512], BF16, io)
    kiT_o = dram_out(nc, "kiT", [128, T], BF16, io)
    widx_o = dram_out(nc, "widx", [T, 64], F32, io)
    sg_o = dram_out(nc, "sgate", [T, 4096], F32, io)

    own = kb is None
    kb = KB(nc) if own else kb
    kb.push()
    xb = kb.sbuf("xb", [128, 32, T], BF16)
    NWS = 2
    wslot = [kb.sbuf(f"wslot{i}", [128, 8192], BF16) for i in range(NWS)]
    cqg = kb.sbuf("cqg", [128, 8, T], BF16)
    cosF = kb.sbuf("cosFs", [128, T], F32)
    sinF = kb.sbuf("sinFs", [128, T], F32)
    cosT = kb.sbuf("cosTs", [128, NBc, 64], F32)
    sinT = kb.sbuf("sinTs", [128, NBc, 64], F32)
    crq = kb.sbuf("crq", [128, T], F32)
    srq = kb.sbuf("srq", [128, T], F32)
    cri = kb.sbuf("cri", [128, T], F32)
    sri = kb.sbuf("sri", [128, T], F32)
    rstd = kb.sbuf("rstd", [128, T], F32)
    qgs = kb.sbuf("qgs", [128, 8], F32)
    kgbs = kb.sbuf("kgbs", [128, 256], F32)
    permf = kb.sbuf("permf", [128, 128], F32)
    permb = kb.sbuf("permb", [128, 128], BF16)
    identf = kb.sbuf("identf", [128, 128], F32)
    identb = kb.sbuf("identb", [128, 128], BF16)
    onesf = kb.sbuf("onesf", [128, 128], F32)
    NR = 2
    sq = [kb.sbuf(f"sq{i}", [128, 512], F32) for i in range(NR)]
    xbr = [kb.sbuf(f"xbr{i}", [128, 512], BF16) for i in range(NR)]
    ra = [kb.sbuf(f"ra{i}", [128, 512], F32) for i in range(NR)]
    rb = [kb.sbuf(f"rb{i}", [128, 512], F32) for i in range(NR)]
    ro = [kb.sbuf(f"ro{i}", [128, 512], BF16) for i in range(NR)]
    go = [kb.sbuf(f"go{i}", [128, 256], F32) for i in range(NR)]
    vo = [kb.sbuf(f"vo{i}", [128, 256], BF16) for i in range(NR)]
    kis = kb.sbuf("kis", [128, 192], F32)
    kin = kb.sbuf("kin", [128, 128], F32)
    kir = kb.sbuf("kir", [128, 128], F32)
    kit = kb.sbuf("kit", [128, 64], F32)
    kib = kb.sbuf("kib", [128, 128], BF16)
    kiTs = kb.sbuf("kiTs", [128, T], BF16)
    wio = kb.sbuf("wio", [128, NBc, 64], F32)
    bst = kb.sbuf("bst", [128, 6], F32)
    mv = kb.sbuf("mv", [128, 4], F32)
    pp = kb.psum("pp", [128, 8, 512], F32)
    ptb = kb.psum("ptb", [128, 128], BF16) if False else None

    xTr = xT.rearrange("(k p) t -> p k t", p=128)
    for kq in range(4):
        kb.dma([("pool", xb[:, kq * 8:(kq + 1) * 8, :], xTr[:, kq * 8:(kq + 1) * 8, :])], W=[f"xb{kq}"], sem=f"xb{kq}")
    XB = [f"xb{kq}" for kq in range(4)]
    kb.dma([("sp", cosF[:], cosF_d), ("sp", sinF[:], sinF_d), ("sp", qgs[:], qg), ("sp", kgbs[:], kgb),
            ("sp", permf[:], perm_d), ("sp", identf[:], ident_d),
            ("sp", cosT[:], cosT_d.rearrange("(m p) c -> p m c", p=128)),
            ("sp", sinT[:], sinT_d.rearrange("(m p) c -> p m c", p=128))], W=["consts"], sem="consts")
    kb.op("dve", lambda e: e.tensor_copy(out=permb[:], in_=permf[:]), R=["consts"], W=["permb"])
    kb.op("dve", lambda e: e.tensor_copy(out=identb[:], in_=identf[:]), R=["consts"], W=["identb"])
    kb.op("dve", lambda e: e.memset(onesf[:], 1.0), W=["onesf"])

    wstate = {"n": 0}

    def load_w(wt, g, kchunks, ncols):
        i = wstate["n"] % NWS
        wstate["n"] += 1
        n = kchunks * ncols
        view = wslot[i][:, 0:n].rearrange("p (k c) -> p k c", k=kchunks)
        step = min(n, 2048)
        parts = [("pool", wslot[i][:, a:a + step], wt[g][:, a:a + step]) for a in range(0, n, step)]
        kb.dma(parts, W=[f"w{i}"], sem=f"w{i}")
        return view, f"w{i}"

    rr = {"n": 0}

    def rope_tile(src_ps, pskey, N, cr, sr, crkeys, dst_dram, dstkey):
        i = rr["n"] % NR
        rr["n"] += 1
        pb = 4 + (rr["n"] % 2)
        kb.op("act", lambda e: e.activation(out=xbr[i][:, :N], in_=src_ps, func=AF.Copy), R=[pskey], W=[f"xbr{i}"])
        kb.op("pe", lambda e: e.matmul(pp[:, pb, :N], lhsT=permb[:], rhs=xbr[i][:, :N], start=True, stop=True),
              R=[f"xbr{i}", "permb"], W=[f"pp{pb}"])
        kb.op("dve", lambda e: e.tensor_tensor(out=ra[i][:, :N], in0=src_ps, in1=cr, op=ALU.mult),
              R=[pskey] + crkeys, W=[f"ra{i}"])
        kb.op("dve", lambda e: e.tensor_tensor(out=rb[i][:, :N], in0=pp[:, pb, :N], in1=sr, op=ALU.mult),
              R=[f"pp{pb}"] + crkeys, W=[f"rb{i}"])
        kb.op("pool", lambda e: e.tensor_tensor(out=ro[i][:, :N], in0=ra[i][:, :N], in1=rb[i][:, :N], op=ALU.add),
              R=[f"ra{i}", f"rb{i}"], W=[f"ro{i}"])
        kb.dma([("sp", dst_dram, ro[i][:, :N])], R=[f"ro{i}"], W=[dstkey], sem=f"ro{i}")

    mm = {"n": 0}

    def next_bank():
        b = mm["n"] % 3
        mm["n"] += 1
        return b

    for g4 in range(4):
        wv, wk = load_w(w_in, g4, 32, 256)
        for fl in range(2):
            fc = g4 * 2 + fl
            for th in range(TH):
                b = next_bank()
                for k in range(32):
                    kb.op("pe", lambda e: e.matmul(pp[:, b, :TW], lhsT=wv[:, k, fl * 128:(fl + 1) * 128],
                                                   rhs=xb[:, k, th * TW:(th + 1) * TW], start=(k == 0), stop=(k == 31)),
                          R=[wk] + XB, W=[f"pp{b}"])
                kb.op("dve", lambda e: e.tensor_scalar(out=cqg[:, fc, th * TW:(th + 1) * TW], in0=pp[:, b, :TW],
                                                       scalar1=qgs[:, fc:fc + 1], scalar2=None, op0=ALU.mult),
                      R=[f"pp{b}", "consts"], W=[f"cqg{th}"])
                i = (fc * TH + th) % NR
                kb.op("act", lambda e: e.activation(out=sq[i][:, :TW], in_=pp[:, b, :TW], func=AF.Square),
                      R=[f"pp{b}"], W=[f"sq{i}"])
                kb.op("pe", lambda e: e.matmul(pp[:, 6 + th, :TW], lhsT=onesf[:], rhs=sq[i][:, :TW],
                                               start=(fc == 0), stop=(fc == 7)),
                      R=[f"sq{i}", "onesf"], W=[f"pp{6 + th}"])
    for th in range(TH):
        sl = slice(th * TW, (th + 1) * TW)
        kb.op("act", lambda e: e.activation(out=rstd[:, sl], in_=pp[:, 6 + th, :TW], func=AF.Sqrt, scale=1.0 / 1024, bias=1e-6),
              R=[f"pp{6 + th}"], W=["rstd"])
    kb.op("dve", lambda e: e.reciprocal(out=rstd[:], in_=rstd[:]), R=["rstd"], W=["rstd"])
    kb.op("dve", lambda e: e.tensor_tensor(out=cri[:], in0=cosF[:], in1=rstd[:], op=ALU.mult), R=["rstd", "consts"], W=["cri"])
    kb.op("dve", lambda e: e.tensor_tensor(out=sri[:], in0=sinF[:], in1=rstd[:], op=ALU.mult), R=["rstd", "consts"], W=["sri"])
    kb.op("pool", lambda e: e.tensor_scalar(out=crq[:], in0=cri[:], scalar1=SCALE, scalar2=None, op0=ALU.mult), R=["cri"], W=["crq"])
    kb.op("pool", lambda e: e.tensor_scalar(out=srq[:], in0=sri[:], scalar1=SCALE, scalar2=None, op0=ALU.mult), R=["sri"], W=["srq"])

    for g2 in (range(2) if 'ii' in stages else []):
        wv, wk = load_w(w_in, 4 + g2, 32, 256)
        for fl in range(2):
            g = g2 * 2 + fl
            for th in range(TH):
                b = next_bank()
                for k in range(32):
                    kb.op("pe", lambda e: e.matmul(pp[:, b, :TW], lhsT=wv[:, k, fl * 128:(fl + 1) * 128],
                                                   rhs=xb[:, k, th * TW:(th + 1) * TW], start=(k == 0), stop=(k == 31)),
                          R=[wk] + XB, W=[f"pp{b}"])
                sl = slice(th * TW, (th + 1) * TW)
                rope_tile(pp[:, b, :TW], f"pp{b}", TW, cosF[:, sl], sinF[:, sl], ["consts"], kT_o[g, :, sl], f"kT{g}")

    def tokmajor(gidx, ncols, handler):
        wv, wk = load_w(w_in, gidx, 32, ncols)
        for tb in range(NBc):
            b = next_bank()
            for k in range(32):
                kb.op("pe", lambda e: e.matmul(pp[:, b, :ncols], lhsT=xb[:, k, tb * 128:(tb + 1) * 128],
                                               rhs=wv[:, k, :], start=(k == 0), stop=(k == 31)),
                      R=[wk] + XB, W=[f"pp{b}"])
            handler(tb, b)

    tm = {"n": 0}

    def h_v(c0v):
        def f(tb, b):
            i = tm["n"] % NR
            tm["n"] += 1
            kb.op("act", lambda e: e.activation(out=vo[i][:], in_=pp[:, b, :256], func=AF.Copy), R=[f"pp{b}"], W=[f"vo{i}"])
            kb.dma([("sp", v_o[tb * 128:(tb + 1) * 128, c0v:c0v + 256], vo[i][:])], R=[f"vo{i}"], W=["v_o"], sem=f"vo{i}", indep=True)
        return f

    if 'v' in stages:
        tokmajor(6, 256, h_v(0))
        tokmajor(7, 256, h_v(256))

    def h_ki(tb, b):
        kb.op("act", lambda e: e.activation(out=kis[:], in_=pp[:, b, :192], func=AF.Copy), R=[f"pp{b}"], W=["kis"])
        kb.op("dve", lambda e: e.tensor_scalar(out=wio[:, tb, :], in0=kis[:, 128:192], scalar1=WSC, scalar2=None, op0=ALU.mult),
              R=["kis"], W=["wio"])
        kb.op("dve", lambda e: e.bn_stats(out=bst[:], in_=kis[:, 0:128]), R=["kis"], W=["bst"])
        kb.op("dve", lambda e: e.bn_aggr(out=mv[:, 0:2], in_=bst[:]), R=["bst"], W=["mv"])
        kb.op("act", lambda e: e.activation(out=mv[:, 2:3], in_=mv[:, 1:2], func=AF.Sqrt, scale=1.0, bias=1e-5), R=["mv"], W=["mv"])
        kb.op("dve", lambda e: e.reciprocal(out=mv[:, 3:4], in_=mv[:, 2:3]), R=["mv"], W=["mv"])
        kb.op("dve", lambda e: e.tensor_scalar(out=kin[:], in0=kis[:, 0:128], scalar1=mv[:, 0:1], scalar2=mv[:, 3:4],
                                               op0=ALU.subtract, op1=ALU.mult), R=["kis", "mv"], W=["kin"])
        kb.op("dve", lambda e: e.tensor_tensor(out=kin[:], in0=kin[:], in1=kgbs[:, 0:128], op=ALU.mult), R=["kin", "consts"], W=["kin"])
        kb.op("dve", lambda e: e.tensor_tensor(out=kin[:], in0=kin[:], in1=kgbs[:, 128:256], op=ALU.add), R=["kin", "consts"], W=["kin"])
        c, s = cosT[:, tb, :], sinT[:, tb, :]
        kb.op("dve", lambda e: e.tensor_tensor(out=kir[:, 0:64], in0=kin[:, 0:64], in1=c, op=ALU.mult), R=["kin", "consts"], W=["kir"])
        kb.op("dve", lambda e: e.tensor_tensor(out=kit[:], in0=kin[:, 64:128], in1=s, op=ALU.mult), R=["kin", "consts"], W=["kit"])
        kb.op("dve", lambda e: e.tensor_tensor(out=kir[:, 0:64], in0=kir[:, 0:64], in1=kit[:], op=ALU.subtract), R=["kir", "kit"], W=["kir"])
        kb.op("dve", lambda e: e.tensor_tensor(out=kir[:, 64:128], in0=kin[:, 64:128], in1=c, op=ALU.mult), R=["kin", "consts"], W=["kir"])
        kb.op("dve", lambda e: e.tensor_tensor(out=kit[:], in0=kin[:, 0:64], in1=s, op=ALU.mult), R=["kin", "consts", "kir"], W=["kit"])
        kb.op("dve", lambda e: e.tensor_tensor(out=kib[:, 64:128], in0=kir[:, 64:128], in1=kit[:], op=ALU.add), R=["kir", "kit"], W=["kib"])
        kb.op("dve", lambda e: e.tensor_copy(out=kib[:, 0:64], in_=kir[:, 0:64]), R=["kir"], W=["kib"])
        pt = pp[:, 7, 0:64].bitcast(BF16)
        kb.op("pe", lambda e: e.transpose(pt, kib[:], identb[:]), R=["kib", "identb"], W=["pp7"])
        kb.op("act", lambda e: e.activation(out=kiTs[:, tb * 128:(tb + 1) * 128], in_=pt, func=AF.Copy), R=["pp7"], W=["kiTs"])

    if 'ki' in stages:
        tokmajor(8, 192, h_ki)
        kb.dma([("sp", kiT_o, kiTs[:])], R=["kiTs"], W=["kiT_o"], sem="kiTo")
        kb.dma([("sp", widx_o.rearrange("(m p) c -> p m c", p=128), wio[:])], R=["wio"], W=["widx_o"], sem="wio")

    def h_gate(c0g):
        def f(tb, b):
            i = tm["n"] % NR
            tm["n"] += 1
            kb.op("act", lambda e: e.activation(out=go[i][:], in_=pp[:, b, :256], func=AF.Silu), R=[f"pp{b}"], W=[f"go{i}"])
            kb.dma([("sp", sg_o[tb * 128:(tb + 1) * 128, c0g:c0g + 256], go[i][:])], R=[f"go{i}"], W=["sg_o"], sem=f"go{i}", indep=True)
        return f

    for gg in (range(16) if 'gate' in stages else []):
        tokmajor(9 + gg, 256, h_gate(gg * 256))

    for g12 in (range(12) if 'iv' in stages else []):
        wv, wk = load_w(w_uq, g12, 8, 1024)
        for hl in range(8):
            hh = g12 * 8 + hl
            for th in range(TH):
                b = next_bank()
                for k in range(8):
                    kb.op("pe", lambda e: e.matmul(pp[:, b, :TW], lhsT=wv[:, k, hl * 128:(hl + 1) * 128],
                                                   rhs=cqg[:, k, th * TW:(th + 1) * TW], start=(k == 0), stop=(k == 7)),
                          R=[wk] + [f"cqg{t}" for t in range(TH)], W=[f"pp{b}"])
                sl = slice(th * TW, (th + 1) * TW)
                if hh < 32:
                    rope_tile(pp[:, b, :TW], f"pp{b}", TW, crq[:, sl], srq[:, sl], ["crq", "srq"], qT_o[hh, :, sl], f"qT{hh}")
                else:
                    rope_tile(pp[:, b, :TW], f"pp{b}", TW, cri[:, sl], sri[:, sl], ["cri", "sri"], qiT_o[hh - 32, :, sl], f"qiT{hh}")
    outs = [f"kT{g}" for g in range(4)] + [f"qT{h}" for h in range(32)] + [f"qiT{h}" for h in range(32, 96)] + \
           ["v_o", "kiT_o", "widx_o", "sg_o"]
    kb.finish([o for o in outs if o in kb.st])
    print("A1 inst", kb.n_inst, "waits", kb.n_wait)
    kb.pop()
    if own:
        kb.close()
    return nc


def a1_inputs(inp, S, b, j):
    pos = own_pos(S, j)
    rt = rope_tables(pos)
    c = consts()
    qg = np.ascontiguousarray(inp["a_q_norm_g"][0].reshape(8, 128).T)
    kgb = np.concatenate([np.broadcast_to(inp["a_kidx_norm_g"][0][None, :], (128, 128)),
                          np.broadcast_to(inp["a_kidx_norm_b"][0][None, :], (128, 128))], 1)
    return dict(xT=np.ascontiguousarray(inp["x"][b, pos, :].T), a_w_in=inp["a_w_in_t"], a_w_uq=inp["a_w_uq_t"],
                qg=qg, kgb=np.ascontiguousarray(kgb), perm=c["perm"], ident=c["ident"], **rt)


def phase_a2a(nc, S, kb=None, io=None):
    NB = S // 128
    NBc = S // 512
    T = NBc * 128
    qT = dram_in(nc, "qT", [32, 128, T], BF16, io)
    qiT = dram_in(nc, "qiT", [64, 128, T], BF16, io)
    widx = dram_in(nc, "widx", [T, 64], F32, io)
    sgate = dram_in(nc, "sgate", [T, 4096], F32, io)
    kTf = dram_in(nc, "kTg", [4 * 512, T], BF16, io)
    vf = dram_in(nc, "vg", [4 * T, 512], BF16, io)
    kiTf = dram_in(nc, "kiTg", [4 * 128, T], BF16, io)
    ident_d = dram_in(nc, "ident", [128, 128], F32, io)
    cm4_d = dram_in(nc, "cm4", [128, 4, 128], F32, io)
    ogT_o = dram_out(nc, "ogT", [4096, T], BF16, io)

    own = kb is None
    kb = KB(nc) if own else kb
    kb.push()
    kis = kb.sbuf("kiTs", [128, S], BF16)
    kts = kb.sbuf("kTs", [128, 4, S], BF16)
    vs = kb.sbuf("vs", [128, NB, 4, 130], BF16)
    qib = kb.sbuf("qib", [128, 64, 128], BF16)
    Dg = kb.sbuf("Dg", [128, 64, 128], BF16)
    wix = kb.sbuf("wix", [128, 64], F32)
    score2 = [kb.sbuf(f"score{i}", [128, S], F32) for i in range(2)]
    mb = kb.sbuf("mb", [128, S], BF16)
    mbT = kb.sbuf("mbT", [128, NB, 128], BF16)
    qblk = kb.sbuf("qblk", [128, 32, 128], BF16)
    identf = kb.sbuf("identf", [128, 128], F32)
    identb = kb.sbuf("identb", [128, 128], BF16)
    cm4 = kb.sbuf("cm4s", [128, 4, 128], F32)
    half = kb.sbuf("half", [128, 1], F32)
    bs = kb.sbuf("bs", [128, 8], F32)
    KIT = 26
    pw2 = kb.sbuf("pw2", [128, KIT + 1], F32)
    hht = kb.sbuf("hht", [128, KIT + 1], F32)
    rl = [kb.sbuf(f"rl{i}", [128, 512], BF16) for i in range(6)]
    sm = [kb.sbuf(f"sm{i}", [128, 512], F32) for i in range(4)]
    pT = [kb.sbuf(f"pT{i}", [128, 512], BF16) for i in range(4)]
    sgb = [kb.sbuf(f"sgb{i}", [128, 512], F32) for i in range(2)]
    og4 = kb.sbuf("og4", [128, 512], BF16)
    ogT4 = [kb.sbuf(f"ogT4{i}", [128, 4, 128], BF16) for i in range(2)]
    rs = kb.sbuf("rs", [128, 4], F32)
    pp = kb.psum("pp", [128, 8, 512], F32)

    kb.dma([("sp", identf[:], ident_d), ("sp", cm4[:], cm4_d)] +
           [("sp", kis[:].rearrange("d (m j p) -> d m j p", j=4, p=128)[:, :, j, :],
             kiTf[j * 128:(j + 1) * 128, :].rearrange("d (m p) -> d m p", p=128)) for j in range(4)], W=["consts"], sem="consts")
    kb.dma([("sp", kts[:, g, :].rearrange("d (m j p) -> d m j p", j=4, p=128)[:, :, j, :],
             kTf[j * 512 + g * 128:j * 512 + (g + 1) * 128, :].rearrange("d (m p) -> d m p", p=128))
            for g in range(4) for j in range(4)], W=["kts"], sem="kts")
    kb.op("pool", lambda e: e.memset(vs[:], 1.0), W=["vs"])
    kb.dma([("sp", vs[:, :, g, 0:128].rearrange("p (m j) d -> p m j d", j=4)[:, :, j, :],
             vf[j * T:(j + 1) * T, g * 128:(g + 1) * 128].rearrange("(m p) d -> p m d", p=128))
            for g in range(4) for j in range(4)], W=["vs"], sem="vs")
    kb.op("dve", lambda e: e.tensor_copy(out=identb[:], in_=identf[:]), R=["consts"], W=["identb"])
    kb.op("dve", lambda e: e.memset(half[:], 0.5), W=["half"])
    for k in range(KIT + 1):
        kb.op("pool", lambda e: e.memset(pw2[:, k:k + 1], 2.0 ** -(k + 1)), W=["pw2"])
    LO, HI, MID, CNT, GE, D1, D2 = [slice(i, i + 1) for i in range(7)]
    ev = {"n": 0}

    def stage_I(m):
        nk = 4 * m + 4
        SK = nk * 128
        qs = slice(m * 128, (m + 1) * 128)
        score = score2[m % 2]
        skey = f"score{m % 2}"
        junk = mb
        kb.dma([("sp", qib[:], qiT[:, :, qs].rearrange("h d q -> d h q")), ("sp", wix[:], widx[qs, :])], W=["qib", "wix"], sem="qload")
        for h in range(64):
            eng = "dve" if h % 2 == 0 else "pool"
            kb.op(eng, lambda e: e.tensor_scalar(out=Dg[:, h, :], in0=identf[:], scalar1=wix[:, h:h + 1], scalar2=None, op0=ALU.mult),
                  R=["consts", "wix"], W=[f"Dg{h}"])
        DB = (0, 1, 2, 5, 6, 7)
        items = [(c0, min(512, SK - c0), h) for c0 in range(0, SK, 512) for h in range(64)]

        def idx_s1(it, t):
            c0, N, h = it
            b = DB[t % 6]
            kb.op("pe", lambda e: e.matmul(pp[:, b, :N], lhsT=qib[:, h, :], rhs=kis[:, c0:c0 + N], start=True, stop=True),
                  R=["qib", "consts"], W=[f"pp{b}"])

        def idx_s2(it, t):
            c0, N, h = it
            b = DB[t % 6]
            i = t % 6
            if True:
                kb.op("act", lambda e: e.activation(out=rl[i][:, :N], in_=pp[:, b, :N], func=AF.Relu), R=[f"pp{b}"], W=[f"rl{i}"])
            else:
                kb.op("dve", lambda e: e.tensor_scalar(out=rl[i][:, :N], in0=pp[:, b, :N], scalar1=0.0, scalar2=None, op0=ALU.max),
                      R=[f"pp{b}"], W=[f"rl{i}"])
            kb.op("pe", lambda e: e.matmul(pp[:, 3, :N], lhsT=Dg[:, h, :], rhs=rl[i][:, :N], start=(h == 0), stop=(h == 63)),
                  R=[f"rl{i}", f"Dg{h}"], W=["pp3"])
            if h == 63:
                kb.op("act", lambda e: e.activation(out=score[:, c0:c0 + N], in_=pp[:, 3, :N], func=AF.Copy), R=["pp3"], W=[skey])

        emit_skewed(items, idx_s1, idx_s2, 4)

    def stage_B(m):
        nk = 4 * m + 4
        SK = nk * 128
        qs = slice(m * 128, (m + 1) * 128)
        score = score2[m % 2]
        skey = f"score{m % 2}"
        junk = mb
        W0 = slice(5, 6)
        SG = slice(4, 5)
        kb.op("dve", lambda e: e.tensor_reduce(out=bs[:, LO], in_=score[:, :SK], op=ALU.min, axis=AX.X), R=[skey], W=["bs"])
        kb.op("dve", lambda e: e.tensor_reduce(out=bs[:, HI], in_=score[:, :SK], op=ALU.max, axis=AX.X), R=[skey], W=["bs"])
        kb.op("dve", lambda e: e.tensor_tensor(out=score[:, SK - 512:SK], in0=score[:, SK - 512:SK],
                                               in1=cm4[:].rearrange("p a b -> p (a b)"), op=ALU.add), R=[skey, "consts"], W=[skey])
        kb.op("dve", lambda e: e.tensor_tensor(out=bs[:, W0], in0=bs[:, HI], in1=bs[:, LO], op=ALU.subtract), R=["bs"], W=["bs"])
        kb.op("dve", lambda e: e.tensor_scalar(out=bs[:, W0], in0=bs[:, W0], scalar1=1.01, scalar2=1e-5, op0=ALU.mult, op1=ALU.add), R=["bs"], W=["bs"])
        kb.op("dve", lambda e: e.tensor_scalar(out=hht[:], in0=pw2[:], scalar1=bs[:, W0], scalar2=None, op0=ALU.mult), R=["bs", "pw2"], W=["hh"])
        kb.op("dve", lambda e: e.tensor_tensor(out=bs[:, MID], in0=bs[:, HI], in1=hht[:, 0:1], op=ALU.subtract), R=["bs", "hh"], W=["bs"])
        for it in range(KIT):
            kb.op("dve", lambda e: e.tensor_scalar(out=junk[:, :SK], in0=score[:, :SK], scalar1=bs[:, MID], scalar2=None,
                                                   op0=ALU.is_ge, op1=ALU.add, accum_out=bs[:, CNT]), R=[skey, "bs"], W=["mb", "bs"])
            kb.op("dve", lambda e: e.tensor_scalar(out=bs[:, SG], in0=bs[:, CNT], scalar1=255.5, scalar2=0.5, op0=ALU.is_ge, op1=ALU.subtract),
                  R=["bs"], W=["bs"])
            kb.op("dve", lambda e: e.scalar_tensor_tensor(out=bs[:, MID], in0=bs[:, SG], scalar=hht[:, it:it + 1], in1=bs[:, MID],
                                                          op0=ALU.mult, op1=ALU.add), R=["bs", "hh"], W=["bs"])
        kb.op("dve", lambda e: e.tensor_tensor(out=bs[:, LO], in0=bs[:, MID], in1=hht[:, KIT:KIT + 1], op=ALU.subtract), R=["bs", "hh"], W=["bs"])
        kb.op("dve", lambda e: e.tensor_scalar(out=mb[:, :SK], in0=score[:, :SK], scalar1=bs[:, LO], scalar2=NEG, op0=ALU.is_lt, op1=ALU.mult),
              R=[skey, "bs"], W=["mb"])
        for k4 in range(0, nk, 4):
            pt = pp[:, 4, 0:256].bitcast(BF16).rearrange("p (a b) -> p a b", a=4)
            for a in range(4):
                kb.op("pe", lambda e: e.transpose(pt[:, a, :], mb[:, (k4 + a) * 128:(k4 + a + 1) * 128], identb[:]), R=["mb", "identb"], W=["pp4"])
            kb.op("act", lambda e: e.activation(out=mbT[:, k4:k4 + 4, :], in_=pt, func=AF.Copy), R=["pp4"], W=["mbT"])

    def stage_A(m):
        nk = 4 * m + 4
        SK = nk * 128
        qs = slice(m * 128, (m + 1) * 128)
        score = score2[m % 2]
        skey = f"score{m % 2}"
        junk = mb
        kb.dma([("sp", qblk[:], qT[:, :, qs].rearrange("h d q -> d h q"))], W=["qblk"], sem="qblk")
        aitems = [(g, hp, kk) for g in range(4) for hp in range(2) for kk in range(nk)]

        def att_s1(it, t):
            g, hp, kk = it
            h0 = g * 8 + hp * 4
            b = t % 4
            if kk == 0:
                sgi = (g * 2 + hp) % 2
                kb.dma([("sp", sgb[sgi][:], sgate[qs, h0 * 128:(h0 + 4) * 128])], W=[f"sgb{sgi}"], sem=f"sgb{sgi}")
            kb.op("pe", lambda e: e.matmul(pp[:, b, :], lhsT=kts[:, g, kk * 128:(kk + 1) * 128],
                                           rhs=qblk[:, h0:h0 + 4, :].rearrange("p h q -> p (h q)"), start=True, stop=True),
                  R=["kts", "qblk"], W=[f"pp{b}"])

        def att_s2(it, t):
            g, hp, kk = it
            h0 = g * 8 + hp * 4
            b = t % 4
            i = t % 4
            sgi = (g * 2 + hp) % 2
            kb.op("dve", lambda e: e.tensor_tensor(out=sm[i][:].rearrange("p (h q) -> p h q", h=4),
                                                   in0=pp[:, b, :].rearrange("p (h q) -> p h q", h=4),
                                                   in1=mbT[:, kk:kk + 1, :].to_broadcast([128, 4, 128]), op=ALU.add),
                  R=[f"pp{b}", "mbT"], W=[f"sm{i}"])
            kb.op("act", lambda e: e.activation(out=pT[i][:], in_=sm[i][:], func=AF.Exp), R=[f"sm{i}"], W=[f"pT{i}"])
            for hh in range(4):
                kb.op("pe", lambda e: e.matmul(pp[:, 4 + hh, 0:129], lhsT=pT[i][:, hh * 128:(hh + 1) * 128], rhs=vs[:, kk, g, 0:129],
                                               start=(kk == 0), stop=(kk == nk - 1)), R=[f"pT{i}", "vs"], W=[f"pp{4 + hh}"])
            if kk == nk - 1:
                for hh in range(4):
                    kb.op("dve", lambda e: e.reciprocal(out=rs[:, hh:hh + 1], in_=pp[:, 4 + hh, 128:129]), R=[f"pp{4 + hh}"], W=["rs"])
                    kb.op("dve", lambda e: e.scalar_tensor_tensor(out=og4[:, hh * 128:(hh + 1) * 128], in0=pp[:, 4 + hh, 0:128],
                                                                  scalar=rs[:, hh:hh + 1], in1=sgb[sgi][:, hh * 128:(hh + 1) * 128],
                                                                  op0=ALU.mult, op1=ALU.mult), R=[f"pp{4 + hh}", "rs", f"sgb{sgi}"], W=["og4"])
                i2 = ev["n"] % 2
                ev["n"] += 1
                pt = pp[:, 4, 256:512].bitcast(BF16).rearrange("p (a b) -> p a b", a=4)
                for hh in range(4):
                    kb.op("pe", lambda e: e.transpose(pt[:, hh, :], og4[:, hh * 128:(hh + 1) * 128], identb[:]), R=["og4", "identb"], W=["pp4"])
                kb.op("act", lambda e: e.activation(out=ogT4[i2][:], in_=pt, func=AF.Copy), R=["pp4"], W=[f"ogT4{i2}"])
                kb.dma([("sp", ogT_o[h0 * 128:(h0 + 4) * 128, qs].rearrange("(h d) q -> d h q", h=4), ogT4[i2][:])],
                       R=[f"ogT4{i2}"], W=["ogT_o"], sem=f"ogT4{i2}", indep=True)

        emit_skewed(aitems, att_s1, att_s2, 3)

    stage_I(0)
    for m in range(NBc):
        if m + 1 < NBc:
            stage_I(m + 1)
        stage_B(m)
        stage_A(m)
    kb.finish(["ogT_o"])
    print("A2a inst", kb.n_inst, "waits", kb.n_wait)
    kb.pop()
    if own:
        kb.close()
    return nc


def causal_tables(j):
    q = np.arange(128)[:, None]
    s = np.arange(128)[None, :]
    cm4 = np.zeros((128, 4, 128), np.float32)
    tri4 = np.zeros((128, 4, 128), np.float32)
    for r in range(4):
        if r == j:
            cm4[:, r, :] = np.where(s <= q, 0.0, -1e30)
            tri4[:, r, :] = np.where(s.T <= q.T, 0.0, NEG)
        elif r > j:
            cm4[:, r, :] = -1e30
            tri4[:, r, :] = NEG
    return cm4, tri4


def phase_out(nc, S, final, kb=None, io=None):
    NBc = S // 512
    T = NBc * 128
    HB = min(4, NBc)
    ogT = dram_in(nc, "ogT", [4096, T], BF16, io)
    w_out = dram_in(nc, "w_out", [16, 128, 8192], F32, io)
    x_d = dram_in(nc, "x", [T, 4096], F32, io)
    lng_d = dram_in(nc, "lng", [128, 4096], F32, io)
    lnb_d = dram_in(nc, "lnb", [128, 4096], F32, io)
    ident_d = dram_in(nc, "ident", [128, 128], F32, io)
    y_o = dram_out(nc, "y", [T, 4096], F32, io)
    yT_o = None if final else dram_out(nc, "yT", [4096, T], BF16, io)
    ALPHA = 4 ** 0.25

    own = kb is None
    kb = KB(nc) if own else kb
    kb.push()
    ogb = kb.sbuf("ogb", [128, 32, HB * 128], BF16)
    wslot = [kb.sbuf(f"wslot{i}", [128, 32, 256], BF16) for i in range(2)]
    ybuf = kb.sbuf("ybuf", [128, HB, 4096], F32)
    lng = kb.sbuf("lngs", [128, 4096], F32)
    lnb = kb.sbuf("lnbs", [128, 4096], F32)
    identf = kb.sbuf("identf", [128, 128], F32)
    bst = kb.sbuf("bst", [128, 8, 6], F32)
    mv = kb.sbuf("mv", [128, 4], F32)
    yTs = kb.sbuf("yTs", [128, 32, 128], BF16)
    pp = kb.psum("pp", [128, 8, 512], F32)
    kb.dma([("sp", lng[:], lng_d), ("sp", lnb[:], lnb_d), ("sp", identf[:], ident_d)], W=["consts"], sem="consts")
    ogr = ogT.rearrange("(c p) t -> p c t", p=128)
    wn = {"n": 0}
    for hf in range(NBc // HB):
        t0 = hf * HB * 128
        kb.dma([("sp", ogb[:, a * 8:(a + 1) * 8, :], ogr[:, a * 8:(a + 1) * 8, t0:t0 + HB * 128]) for a in range(4)], W=["ogb"], sem="ogb")
        kb.dma([("sp", ybuf[:, tb, :], x_d[t0 + tb * 128:t0 + (tb + 1) * 128, :]) for tb in range(HB)], W=["ybuf"], sem="ybuf")
        for cg in range(16):
            i = wn["n"] % 2
            wn["n"] += 1
            wfl = wslot[i][:].rearrange("p k c -> p (k c)")
            kb.dma([("pool", wfl[:, a * 2048:(a + 1) * 2048], w_out[cg][:, a * 2048:(a + 1) * 2048]) for a in range(4)], W=[f"w{i}"], sem=f"w{i}")
            for tb in range(HB):
                b = (cg * HB + tb) % 4
                for c in range(32):
                    kb.op("pe", lambda e: e.matmul(pp[:, b, :256], lhsT=ogb[:, c, tb * 128:(tb + 1) * 128], rhs=wslot[i][:, c, :],
                                                   start=(c == 0), stop=(c == 31)), R=["ogb", f"w{i}"], W=[f"pp{b}"])
                ysl = ybuf[:, tb, cg * 256:(cg + 1) * 256]
                kb.op("dve", lambda e: e.scalar_tensor_tensor(out=ysl, in0=ysl, scalar=ALPHA, in1=pp[:, b, :256], op0=ALU.mult, op1=ALU.add),
                      R=[f"pp{b}"], W=["ybuf"])
        for tb in range(HB):
            for c8 in range(8):
                kb.op("dve", lambda e: e.bn_stats(out=bst[:, c8, :], in_=ybuf[:, tb, c8 * 512:(c8 + 1) * 512]), R=["ybuf"], W=["bst"])
            kb.op("dve", lambda e: e.bn_aggr(out=mv[:, 0:2], in_=bst[:].rearrange("p a b -> p (a b)")), R=["bst"], W=["mv"])
            kb.op("act", lambda e: e.activation(out=mv[:, 2:3], in_=mv[:, 1:2], func=AF.Sqrt, scale=1.0, bias=1e-5), R=["mv"], W=["mv"])
            kb.op("dve", lambda e: e.reciprocal(out=mv[:, 3:4], in_=mv[:, 2:3]), R=["mv"], W=["mv"])
            kb.op("dve", lambda e: e.tensor_scalar(out=ybuf[:, tb, :], in0=ybuf[:, tb, :], scalar1=mv[:, 0:1], scalar2=mv[:, 3:4],
                                                   op0=ALU.subtract, op1=ALU.mult), R=["mv"], W=["ybuf"])
            kb.op("pool", lambda e: e.tensor_tensor(out=ybuf[:, tb, :], in0=ybuf[:, tb, :], in1=lng[:], op=ALU.mult), R=["consts"], W=["ybuf"])
            kb.op("dve", lambda e: e.tensor_tensor(out=ybuf[:, tb, :], in0=ybuf[:, tb, :], in1=lnb[:], op=ALU.add), R=["consts"], W=["ybuf"])
            kb.dma([("sp", y_o[t0 + tb * 128:t0 + (tb + 1) * 128, :], ybuf[:, tb, :])], R=["ybuf"], W=["y_o"], sem="yo", indep=True)
            if not final:
                for c4 in range(8):
                    b = 4 + c4 % 2
                    for a in range(4):
                        c = c4 * 4 + a
                        kb.op("pe", lambda e: e.transpose(pp[:, b, a * 128:(a + 1) * 128], ybuf[:, tb, c * 128:(c + 1) * 128], identf[:]),
                              R=["ybuf", "consts"], W=[f"pp{b}"])
                    kb.op("act", lambda e: e.activation(out=yTs[:, c4 * 4:(c4 + 1) * 4, :].rearrange("p a b -> p (a b)"), in_=pp[:, b, :],
                                                        func=AF.Copy), R=[f"pp{b}"], W=["yTs"])
                kb.dma([("sp", yT_o.rearrange("(c p) t -> p c t", p=128)[:, :, t0 + tb * 128:t0 + (tb + 1) * 128], yTs[:])],
                       R=["yTs"], W=["yT_o"], sem="yTo", indep=True)
    kb.finish(["y_o", "yT_o"])
    print("OUT inst", kb.n_inst, "waits", kb.n_wait)
    kb.pop()
    if own:
        kb.close()
    return nc


def phase_b1(nc, S, kb=None, io=None):
    NBc = S // 512
    T = NBc * 128
    TW = min(T, 512)
    TH = T // TW
    x1T = dram_in(nc, "x1T", [4096, T], BF16, io)
    w_in = dram_in(nc, "b_w_in", [32, 128, 8192], F32, io)
    w_vg = dram_in(nc, "b_w_vg", [17, 128, 16384], F32, io)
    fb_d = dram_in(nc, "fb", [128, 32], F32, io)
    qT_o = dram_out(nc, "qT1", [32, 128, T], BF16, io)
    kT_o = dram_out(nc, "kT1", [32, 128, T], BF16, io)
    v_o = dram_out(nc, "v1", [T, 4096], BF16, io)
    sg_o = dram_out(nc, "sgate1", [T, 4096], F32, io)
    lf_o = dram_out(nc, "lf", [T, 32], F32, io)

    own = kb is None
    kb = KB(nc) if own else kb
    kb.push()
    xb = kb.sbuf("xb", [128, 32, T], BF16)
    wslot = [kb.sbuf(f"wslot{i}", [128, 32, 512], BF16) for i in range(2)]
    wstage = kb.sbuf("wstage", [128, 8192], F32)
    fbs = kb.sbuf("fbs", [128, 32], F32)
    NR = 4
    ro = [kb.sbuf(f"ro{i}", [128, 512], BF16) for i in range(NR)]
    go = [kb.sbuf(f"go{i}", [128, 512], F32) for i in range(NR)]
    vo = [kb.sbuf(f"vo{i}", [128, 512], BF16) for i in range(NR)]
    zz = kb.sbuf("zz", [128, 32], F32)
    lfo = kb.sbuf("lfo", [128, NBc, 32], F32)
    pp = kb.psum("pp", [128, 8, 512], F32)
    xr = x1T.rearrange("(k p) t -> p k t", p=128)
    kb.dma([("sp", xb[:, a * 8:(a + 1) * 8, :], xr[:, a * 8:(a + 1) * 8, :]) for a in range(4)], W=["xb"], sem="xb")
    kb.dma([("sp", fbs[:], fb_d)], W=["fbs"], sem="fbs")
    wn = {"n": 0}

    def load_w(gidx, ncols, wt=None):
        wt = w_in if wt is None else wt
        i = wn["n"] % 2
        wn["n"] += 1
        n = 32 * ncols
        step = min(n, 2048)
        wfl = wslot[i][:].rearrange("p k c -> p (k c)")
        if i == 0:
            kb.dma([("pool", wfl[:, a:a + step], wt[gidx][:, a:a + step]) for a in range(0, n, step)], W=[f"w{i}a", f"w{i}b"], sem=f"w{i}")
        else:
            for a in range(0, n, 8192):
                m_ = min(8192, n - a)
                kb.dma([("sp", wstage[:, b_:b_ + min(2048, m_)], wt[gidx][:, a + b_:a + b_ + min(2048, m_)]) for b_ in range(0, m_, 2048)],
                       W=["wstage"], sem="wstage")
                kb.op("dve", lambda e: e.tensor_copy(out=wfl[:, a:a + m_], in_=wstage[:, 0:m_]), R=["wstage"], W=[f"w{i}a" if a == 0 else f"w{i}b"])
        return wfl[:, 0:n].rearrange("p (k c) -> p k c", k=32), f"w{i}"

    mm = {"n": 0}
    cc = io.get("cc") if io is not None else None
    if cc is not None:
        nck = max(1, (4096 * T * 2) >> 20)
        rc = 4096 // nck
    for gi in range(32):
        wv, wk = load_w(gi, 256)
        for fl in range(2):
            hh = gi * 2 + fl
            for th in range(TH):
                b = mm["n"] % 4
                mm["n"] += 1
                for k in range(32):
                    kb.op("pe", lambda e: e.matmul(pp[:, b, :TW], lhsT=wv[:, k, fl * 128:(fl + 1) * 128], rhs=xb[:, k, th * TW:(th + 1) * TW],
                                                   start=(k == 0), stop=(k == 31)), R=[wk + "a", wk + "b", "xb"], W=[f"pp{b}"])
                i = mm["n"] % NR
                sl = slice(th * TW, (th + 1) * TW)
                if hh < 32:
                    kb.op("act", lambda e: e.activation(out=ro[i][:, :TW], in_=pp[:, b, :TW], func=AF.Copy, scale=SCALE), R=[f"pp{b}"], W=[f"ro{i}"])
                    kb.dma([("sp", qT_o[hh, :, sl], ro[i][:, :TW])], R=[f"ro{i}"], W=["qT_o"], sem=f"ro{i}", indep=True)
                else:
                    kb.op("dve", lambda e: e.tensor_copy(out=ro[i][:, :TW], in_=pp[:, b, :TW]), R=[f"pp{b}"], W=[f"ro{i}"])
                    kb.dma([("sp", kT_o[hh - 32, :, sl], ro[i][:, :TW])], R=[f"ro{i}"], W=["kT_o"], sem=f"ro{i}", indep=True)
        if cc is not None and gi >= 16:
            rows_done = (gi - 16 + 1) * 256
            if rows_done % rc == 0:
                c = rows_done // rc - 1
                kb.collective("AllGather", cc["kT1"][c * rc:(c + 1) * rc, :], cc["kT1g"][c * 4 * rc:(c + 1) * 4 * rc, :], cc["groups"], after=["kT_o"])

    def tokmajor(gidx, ncols, handler):
        wv, wk = load_w(gidx, ncols, w_vg)
        for tb in range(NBc):
            b = mm["n"] % 4
            mm["n"] += 1
            for k in range(32):
                kb.op("pe", lambda e: e.matmul(pp[:, b, :ncols], lhsT=xb[:, k, tb * 128:(tb + 1) * 128], rhs=wv[:, k, :],
                                               start=(k == 0), stop=(k == 31)), R=[wk + "a", wk + "b", "xb"], W=[f"pp{b}"])
            handler(tb, b)

    tm = {"n": 0}

    def h_v(c0v):
        def f(tb, b):
            i = tm["n"] % NR
            tm["n"] += 1
            kb.op("dve", lambda e: e.tensor_copy(out=vo[i][:], in_=pp[:, b, :512]), R=[f"pp{b}"], W=[f"vo{i}"])
            kb.dma([("sp", v_o[tb * 128:(tb + 1) * 128, c0v:c0v + 512], vo[i][:])], R=[f"vo{i}"], W=["v_o"], sem=f"vo{i}", indep=True)
        return f

    def h_gate(c0g):
        def f(tb, b):
            i = tm["n"] % NR
            tm["n"] += 1
            kb.op("act", lambda e: e.activation(out=go[i][:], in_=pp[:, b, :512], func=AF.Silu), R=[f"pp{b}"], W=[f"go{i}"])
            kb.dma([("sp", sg_o[tb * 128:(tb + 1) * 128, c0g:c0g + 512], go[i][:])], R=[f"go{i}"], W=["sg_o"], sem=f"go{i}", indep=True)
        return f

    for gi in range(8):
        tokmajor(gi, 512, h_v(gi * 512))
    if cc is not None:
        for m in range(NBc):
            kb.collective("AllGather", cc["v1"][m * 128:(m + 1) * 128, :], cc["v1g"][m * 512:(m + 1) * 512, :], cc["groups"], after=["v_o"])
    for gi in range(8):
        tokmajor(8 + gi, 512, h_gate(gi * 512))

    def h_f(tb, b):
        kb.op("dve", lambda e: e.tensor_tensor(out=zz[:], in0=pp[:, b, :32], in1=fbs[:], op=ALU.add), R=[f"pp{b}", "fbs"], W=["zz"])
        kb.op("act", lambda e: e.activation(out=zz[:], in_=zz[:], func=AF.Exp, scale=-1.0), R=["zz"], W=["zz"])
        kb.op("act", lambda e: e.activation(out=zz[:], in_=zz[:], func=AF.Ln, scale=1.0, bias=1.0), R=["zz"], W=["zz"])
        kb.op("dve", lambda e: e.tensor_scalar(out=lfo[:, tb, :], in0=zz[:], scalar1=-1.0, scalar2=None, op0=ALU.mult), R=["zz"], W=["lfo"])

    tokmajor(16, 32, h_f)
    kb.dma([("sp", lf_o.rearrange("(m p) h -> p m h", p=128), lfo[:])], R=["lfo"], W=["lf_o"], sem="lfo")
    if cc is not None:
        kb.collective("AllGather", cc["lf1"], cc["lf1g"], cc["groups"], after=["lf_o"])
    kb.finish(["qT_o", "kT_o", "v_o", "sg_o", "lf_o"])
    print("B1 inst", kb.n_inst, "waits", kb.n_wait)
    kb.pop()
    if own:
        kb.close()
    return nc


def phase_b2a(nc, S, kb=None, io=None):
    NB = S // 128
    NBc = S // 512
    T = NBc * 128
    qT = dram_in(nc, "qT1", [32, 128, T], BF16, io)
    kTf = dram_in(nc, "kT1g", [4 * 4096, T], BF16, io)
    vf = dram_in(nc, "v1g", [S, 4096], BF16, io)
    nck = max(1, (4096 * T * 2) >> 20)
    rc = 4096 // nck
    lff = dram_in(nc, "lfg", [4 * T, 32], F32, io)
    sgate = dram_in(nc, "sgate1", [T, 4096], F32, io)
    tri4_d = dram_in(nc, "tri4", [128, 4, 128], F32, io)
    ident_d = dram_in(nc, "ident", [128, 128], F32, io)
    sel_d = dram_in(nc, "sel", [128, 4], F32, io)
    selh_d = dram_in(nc, "selh", [32, 32, 128], F32, io)
    ogT_o = dram_out(nc, "ogT", [4096, T], BF16, io)

    own = kb is None
    kb = KB(nc) if own else kb
    kb.push()
    identf = kb.sbuf("identf", [128, 128], F32)
    identb = kb.sbuf("identb", [128, 128], BF16)
    tri4 = kb.sbuf("tri4s", [128, 4, 128], F32)
    tri4b = kb.sbuf("tri4b", [128, 4, 128], BF16)
    sel = kb.sbuf("sels", [128, 4], F32)
    lfs = kb.sbuf("lfs", [128, NB, 32], F32)
    lfT = kb.sbuf("lfT", [32, S], F32)
    onesT = kb.sbuf("onesT", [32, S], F32)
    cT = kb.sbuf("cT", [32, S], F32)
    ocT = kb.sbuf("ocT", [32, NBc, 128], F32)
    ocR = kb.sbuf("ocR", [32, NBc, 128], F32)
    r1 = kb.sbuf("r1", [32, S], F32)
    nhi = kb.sbuf("nhi", [32, S], BF16)
    nmid = kb.sbuf("nmid", [32, S], BF16)
    nlo = kb.sbuf("nlo", [32, S], BF16)
    ohi = kb.sbuf("ohi", [32, T], BF16)
    omid = kb.sbuf("omid", [32, T], BF16)
    olo = kb.sbuf("olo", [32, T], BF16)
    augl = [kb.sbuf(f"augl{i}", [128, S], BF16) for i in range(2)]
    augr = [kb.sbuf(f"augr{i}", [128, T], BF16) for i in range(2)]
    kts = [kb.sbuf(f"kts{i}", [128, S], BF16) for i in range(2)]
    vs = [kb.sbuf(f"vs{i}", [128, NB, 130], BF16) for i in range(2)]
    qhr = [kb.sbuf(f"qhr{i}", [128, T], BF16) for i in range(2)]
    sgh = [kb.sbuf(f"sgh{i}", [128, NBc, 128], F32) for i in range(2)]
    NP = 4
    pT = [kb.sbuf(f"pT{i}", [128, 512], BF16) for i in range(NP)]
    ogs = [kb.sbuf(f"ogs{i}", [128, 128], BF16) for i in range(2)]
    ogTs = [kb.sbuf(f"ogTs{i}", [128, T], BF16) for i in range(2)]
    rs = kb.sbuf("rs", [128, 2], F32)
    pp = kb.psum("pp", [128, 8, 512], F32)

    groups = [list(range(min(g0 + 4, NBc) - 1, g0 - 1, -1)) for g0 in range(0, NBc, 4)]
    slot = {}
    for gi, grp in enumerate(groups):
        for idx, m in enumerate(grp):
            slot[m] = gi * 4 + idx

    kb.dma([("sp", identf[:], ident_d), ("sp", tri4[:], tri4_d), ("sp", sel[:], sel_d)] +
           [("sp", lfs[:].rearrange("p (m j) h -> p m j h", j=4)[:, :, j, :],
             lff[j * T:(j + 1) * T, :].rearrange("(m p) h -> p m h", p=128)) for j in range(4)], W=["consts"], sem="consts")
    kb.op("dve", lambda e: e.tensor_copy(out=identb[:], in_=identf[:]), R=["consts"], W=["identb"])
    kb.op("dve", lambda e: e.tensor_copy(out=tri4b[:], in_=tri4[:]), R=["consts"], W=["tri4b"])
    kb.op("dve", lambda e: e.memset(onesT[:], 1.0), W=["onesT"])
    for i in range(2):
        kb.op("pool", lambda e: e.memset(vs[i][:], 1.0), W=[f"vs{i}"])
        kb.op("pool", lambda e: e.memset(augl[i][:], 0.0), W=[f"augl{i}"])
        kb.op("pool", lambda e: e.memset(augr[i][:], 0.0), W=[f"augr{i}"])
        kb.op("pool", lambda e: e.memset(augl[i][32:35, :], 1.0), W=[f"augl{i}"])
        kb.op("pool", lambda e: e.memset(augr[i][0:3, :], 1.0), W=[f"augr{i}"])
    for k4 in range(0, NB, 4):
        for a in range(4):
            kb.op("pe", lambda e: e.transpose(pp[0:32, 0, a * 128:(a + 1) * 128], lfs[:, k4 + a, :], identf[:]), R=["consts"], W=["pp0"])
        kb.op("act", lambda e: e.activation(out=lfT[:, k4 * 128:(k4 + 4) * 128], in_=pp[0:32, 0, :], func=AF.Copy), R=["pp0"], W=["lfT"])
    kb.op("dve", lambda e: e.tensor_tensor_scan(out=cT[:], data0=onesT[:], data1=lfT[:], initial=0.0, op0=ALU.mult, op1=ALU.add),
          R=["onesT", "lfT"], W=["cT"])
    cT4 = cT[:].rearrange("p (m r q) -> p m r q", r=4, q=128)
    kb.op("dve", lambda e: e.tensor_scalar(out=ocT[:], in0=cT4[:, :, 0, :], scalar1=sel[0:32, 0:1], scalar2=None, op0=ALU.mult), R=["cT", "consts"], W=["ocT"])
    for r in range(1, 4):
        kb.op("dve", lambda e: e.scalar_tensor_tensor(out=ocT[:], in0=cT4[:, :, r, :], scalar=sel[0:32, r:r + 1], in1=ocT[:], op0=ALU.mult, op1=ALU.add),
              R=["cT", "consts"], W=["ocT"])
    for m in range(NBc):
        kb.op("dve", lambda e: e.tensor_copy(out=ocR[:, slot[m], :], in_=ocT[:, m, :]), R=["ocT"], W=["ocR"])

    def split3(src, hi, mid, lo, n, neg, key):
        sg = -1.0 if neg else 1.0
        kb.op("dve", lambda e: e.tensor_scalar(out=hi, in0=src, scalar1=sg, scalar2=None, op0=ALU.mult), R=[key], W=[key + "hi"])
        kb.op("dve", lambda e: e.scalar_tensor_tensor(out=r1[:, :n], in0=src, scalar=sg, in1=hi, op0=ALU.mult, op1=ALU.subtract),
              R=[key, key + "hi"], W=["r1"])
        kb.op("dve", lambda e: e.tensor_copy(out=mid, in_=r1[:, :n]), R=["r1"], W=[key + "mid"])
        kb.op("dve", lambda e: e.tensor_tensor(out=r1[:, :n], in0=r1[:, :n], in1=mid, op=ALU.subtract), R=[key + "mid"], W=["r1"])
        kb.op("dve", lambda e: e.tensor_copy(out=lo, in_=r1[:, :n]), R=["r1"], W=[key + "lo"])

    split3(cT[:], nhi[:], nmid[:], nlo[:], S, True, "cT")
    split3(ocR[:].rearrange("p a b -> p (a b)"), ohi[:], omid[:], olo[:], T, False, "ocR")
    CK = ["cThi", "cTmid", "cTlo", "ocRhi", "ocRmid", "ocRlo"]

    st = {"e": 0}
    bitems = []
    for h in range(32):
        first = True
        for gi, grp in enumerate(groups):
            for kk in range(4 * grp[0] + 4):
                bitems.append((h, gi, kk, first))
                first = False
    last_of_head = {}
    for t, it in enumerate(bitems):
        last_of_head[it[0]] = t

    def b_s1(it, t):
        h, gi, kk, first = it
        i = h % 2
        grp = groups[gi]
        g0 = gi * 4 * 128
        if first:
            kb.dma([("sp", sgh[i][:], sgate[:, h * 128:(h + 1) * 128].rearrange("(m p) d -> p m d", p=128))] +
                   [("sp", qhr[i][:, slot[m] * 128:(slot[m] + 1) * 128], qT[h][:, m * 128:(m + 1) * 128]) for m in range(NBc)] +
                   [("sp", kts[i][:].rearrange("d (m j p) -> d m j p", j=4, p=128)[:, :, j, :],
                     kTf[((h * 128) // rc) * 4 * rc + j * rc + (h * 128) % rc:((h * 128) // rc) * 4 * rc + j * rc + (h * 128) % rc + 128, :]
                     .rearrange("d (m p) -> d m p", p=128)) for j in range(4)] +
                   [("sp", vs[i][:, :, 0:128], vf[:, h * 128:(h + 1) * 128].rearrange("(kb p) d -> p kb d", p=128))] +
                   [("sp", augl[i][a:a + 1, :], t_[h:h + 1, :]) for a, t_ in enumerate((nhi, nmid, nlo))] +
                   [("sp", augr[i][32 + a:33 + a, :], t_[h:h + 1, :]) for a, t_ in enumerate((ohi, omid, olo))],
                   R=CK, W=[f"kts{i}", f"qhr{i}", f"sgh{i}", f"vs{i}", f"augl{i}", f"augr{i}"], sem=f"hl{i}")
        act = [m for m in grp if kk < 4 * m + 4]
        na = len(act)
        N = na * 128
        b = t % 4
        ml = act[-1]
        tri = kk >= 4 * ml
        kb.op("pe", lambda e: e.matmul(pp[:, b, :N], lhsT=kts[i][:, kk * 128:(kk + 1) * 128], rhs=qhr[i][:, g0:g0 + N],
                                       start=True, stop=False), R=[f"kts{i}", f"qhr{i}"], W=[f"pp{b}"])
        kb.op("pe", lambda e: e.matmul(pp[:, b, :N], lhsT=augl[i][:, kk * 128:(kk + 1) * 128], rhs=augr[i][:, g0:g0 + N],
                                       start=False, stop=(not tri)), R=[f"augl{i}", f"augr{i}"], W=[f"pp{b}"])
        if tri:
            kb.op("pe", lambda e: e.matmul(pp[:, b, (na - 1) * 128:na * 128], lhsT=identb[:], rhs=tri4b[:, kk - 4 * ml, :],
                                           start=False, stop=True), R=["identb", "tri4b"], W=[f"pp{b}"])

    def b_s2(it, t):
        h, gi, kk, first = it
        i = h % 2
        grp = groups[gi]
        act = [m for m in grp if kk < 4 * m + 4]
        na = len(act)
        N = na * 128
        b = t % 4
        sp_ = t % NP
        ml = act[-1]
        kb.op("act", lambda e: e.activation(out=pT[sp_][:, :N], in_=pp[:, b, :N], func=AF.Exp), R=[f"pp{b}"], W=[f"pT{sp_}"])
        for idx, m in enumerate(act):
            ob = 4 + idx
            kb.op("pe", lambda e: e.matmul(pp[:, ob, 0:129], lhsT=pT[sp_][:, idx * 128:(idx + 1) * 128], rhs=vs[i][:, kk, 0:129],
                                           start=(kk == 0), stop=(kk == 4 * m + 3)), R=[f"pT{sp_}", f"vs{i}"], W=[f"pp{ob}"])
        if kk == 4 * ml + 3:
            ob = 4 + (na - 1)
            e2 = st["e"] % 2
            st["e"] += 1
            kb.op("dve", lambda e: e.reciprocal(out=rs[:, e2:e2 + 1], in_=pp[:, ob, 128:129]), R=[f"pp{ob}"], W=[f"rs{e2}"])
            kb.op("dve", lambda e: e.scalar_tensor_tensor(out=ogs[e2][:], in0=pp[:, ob, 0:128], scalar=rs[:, e2:e2 + 1], in1=sgh[i][:, ml, :],
                                                          op0=ALU.mult, op1=ALU.mult), R=[f"pp{ob}", f"rs{e2}", f"sgh{i}"], W=[f"ogs{e2}"])
            pt = pp[:, ob, 256:320].bitcast(BF16)
            kb.op("pe", lambda e: e.transpose(pt, ogs[e2][:], identb[:]), R=[f"ogs{e2}", "identb"], W=[f"pp{ob}"])
            kb.op("act", lambda e: e.activation(out=ogTs[i][:, ml * 128:(ml + 1) * 128], in_=pt, func=AF.Copy), R=[f"pp{ob}"], W=[f"ogTs{i}"])
        if t == last_of_head[h]:
            kb.dma([("sp", ogT_o[h * 128:(h + 1) * 128, :], ogTs[i][:])], R=[f"ogTs{i}"], W=["ogT_o"], sem=f"ogTs{i}", indep=True)

    emit_skewed(bitems, b_s1, b_s2, 3)
    kb.finish(["ogT_o"])
    print("B2a inst", kb.n_inst, "waits", kb.n_wait)
    kb.pop()
    if own:
        kb.close()
    return nc


GROUPS = [[0, 1, 2, 3], [4, 5, 6, 7]]


def build_fused(nc, S):
    T = S // 4
    kb = KB(nc)

    def ext(name, shape, dt=F32):
        return nc.dram_tensor(name, list(shape), dt, kind="ExternalInput").ap()

    def scr(name, shape, dt):
        return nc.dram_tensor(name, list(shape), dt).ap()

    E = dict(xT=ext("xT", [D, T]), x=ext("x", [T, D]), a_w_in=ext("a_w_in", [25, 128, 8192]), a_w_uq=ext("a_w_uq", [12, 128, 8192]),
             a_w_out=ext("a_w_out", [16, 128, 8192]), b_w_in=ext("b_w_in", [32, 128, 8192]), b_w_vg=ext("b_w_vg", [17, 128, 16384]), b_w_out=ext("b_w_out", [16, 128, 8192]),
             qg=ext("qg", [128, 8]), kgb=ext("kgb", [128, 256]), fb=ext("fb", [128, 32]),
             lng0=ext("lng0", [128, D]), lnb0=ext("lnb0", [128, D]), lng1=ext("lng1", [128, D]), lnb1=ext("lnb1", [128, D]),
             cosF=ext("cosF", [128, T]), sinF=ext("sinF", [128, T]), cosT=ext("cosT", [T, 64]), sinT=ext("sinT", [T, 64]),
             perm=ext("perm", [128, 128]), ident=ext("ident", [128, 128]), cm4=ext("cm4", [128, 4, 128]),
             tri4=ext("tri4", [128, 4, 128]), sel=ext("sel", [128, 4]), selh=ext("selh", [32, 32, 128]))
    y_out = nc.dram_tensor("y", [T, D], F32, kind="ExternalOutput").ap()
    qT0 = scr("s_qT0", [32, 128, T], BF16)
    qiT0 = scr("s_qiT0", [64, 128, T], BF16)
    kT0 = scr("s_kT0", [512, T], BF16)
    v0 = scr("s_v0", [T, 512], BF16)
    kiT0 = scr("s_kiT0", [128, T], BF16)
    widx0 = scr("s_widx0", [T, 64], F32)
    sg0 = scr("s_sg0", [T, D], F32)
    kT0g = scr("s_kT0g", [4 * 512, T], BF16)
    v0g = scr("s_v0g", [4 * T, 512], BF16)
    kiT0g = scr("s_kiT0g", [4 * 128, T], BF16)
    ogT0 = scr("s_ogT0", [D, T], BF16)
    x1 = scr("s_x1", [T, D], F32)
    x1T = scr("s_x1T", [D, T], BF16)
    qT1 = scr("s_qT1", [32, 128, T], BF16)
    kT1 = scr("s_kT1", [D, T], BF16)
    v1 = scr("s_v1", [T, D], BF16)
    sg1 = scr("s_sg1", [T, D], F32)
    lf1 = scr("s_lf1", [T, 32], F32)
    kT1g = scr("s_kT1g", [4 * D, T], BF16)
    v1g = scr("s_v1g", [S, D], BF16)
    lf1g = scr("s_lf1g", [4 * T, 32], F32)
    ogT1 = scr("s_ogT1", [D, T], BF16)

    phase_a1(nc, S, kb=kb, io=dict(xT=E["xT"], a_w_in=E["a_w_in"], a_w_uq=E["a_w_uq"], qg=E["qg"], kgb=E["kgb"], cosF=E["cosF"],
                                   sinF=E["sinF"], cosT=E["cosT"], sinT=E["sinT"], perm=E["perm"], ident=E["ident"],
                                   qT=qT0, qiT=qiT0, kT=kT0.rearrange("(g d) t -> g d t", g=4), v=v0, kiT=kiT0, widx=widx0, sgate=sg0))
    kb.collective("AllGather", kT0, kT0g, GROUPS)
    kb.collective("AllGather", v0, v0g, GROUPS)
    kb.collective("AllGather", kiT0, kiT0g, GROUPS)
    kb.barrier()
    phase_a2a(nc, S, kb=kb, io=dict(qT=qT0, qiT=qiT0, widx=widx0, sgate=sg0, kTg=kT0g, vg=v0g, kiTg=kiT0g, ident=E["ident"],
                                    cm4=E["cm4"], ogT=ogT0))
    phase_out(nc, S, False, kb=kb, io=dict(ogT=ogT0, w_out=E["a_w_out"], x=E["x"], lng=E["lng0"], lnb=E["lnb0"], ident=E["ident"],
                                           y=x1, yT=x1T))
    phase_b1(nc, S, kb=kb, io=dict(x1T=x1T, b_w_in=E["b_w_in"], b_w_vg=E["b_w_vg"], fb=E["fb"], qT1=qT1, kT1=kT1.rearrange("(h d) t -> h d t", h=32),
                                   v1=v1, sgate1=sg1, lf=lf1,
                                   cc=dict(kT1=kT1, kT1g=kT1g, v1=v1, v1g=v1g, lf1=lf1, lf1g=lf1g, groups=GROUPS)))
    phase_b2a(nc, S, kb=kb, io=dict(qT1=qT1, kT1g=kT1g, v1g=v1g, lfg=lf1g, sgate1=sg1, tri4=E["tri4"], ident=E["ident"],
                                    sel=E["sel"], selh=E["selh"], ogT=ogT1))
    phase_out(nc, S, True, kb=kb, io=dict(ogT=ogT1, w_out=E["b_w_out"], x=x1, lng=E["lng1"], lnb=E["lnb1"], ident=E["ident"], y=y_out))
    print("FUSED inst", kb.n_inst, "waits", kb.n_wait)
    kb.close()
    return nc


def fused_inputs(inp, S, b, j):
    pos = own_pos(S, j)
    d = a1_inputs(inp, S, b, j)
    cm4, tri4 = causal_tables(j)
    sel = np.zeros((128, 4), np.float32)
    sel[:, j] = 1.0
    selh = np.zeros((32, 32, 128), np.float32)
    for h in range(32):
        selh[h, h, :] = 1.0
    bc = lambda v: np.ascontiguousarray(np.broadcast_to(v, (128, v.shape[-1])))
    d.update(x=np.ascontiguousarray(inp["x"][b, pos, :]), a_w_out=inp["a_w_out_t"], b_w_in=inp["b_w_in_t"], b_w_vg=inp["b_w_vg_t"], b_w_out=inp["b_w_out_t"],
             fb=bc(inp["b_forget_bias"][0]), lng0=bc(inp["ln_g"][0]), lnb0=bc(inp["ln_b"][0]), lng1=bc(inp["ln_g"][1]), lnb1=bc(inp["ln_b"][1]),
             cm4=cm4, tri4=tri4, sel=sel, selh=selh)
    return d


def tile_weights(inp):
    inp = dict(inp)
    inp["a_w_in_t"] = tile_w(inp["a_w_in"][0], A_GROUPS, 32)
    inp["a_w_uq_t"] = tile_w(inp["a_w_uq"][0], UQ_GROUPS, 8)
    inp["a_w_out_t"] = tile_w(inp["a_w_out"][0], O_GROUPS, 32)
    inp["b_w_in_t"] = tile_w(inp["b_w_in"][0], B_GROUPS, 32)
    inp["b_w_vg_t"] = tile_w(inp["b_w_in"][0], BV_GROUPS, 32, row=16384)
    inp["b_w_out_t"] = tile_w(inp["b_w_out"][0], O_GROUPS, 32)
    return inp


_S = 4096


def kernel(x, a_w_in, a_q_norm_g, a_w_uq, a_kidx_norm_g, a_kidx_norm_b, a_w_out,
           b_w_in, b_forget_bias, b_w_out, ln_g, ln_b):
    S = _S
    f = lambda a: np.asarray(a, np.float32)
    inp = dict(x=f(x), a_w_in=f(a_w_in), a_q_norm_g=f(a_q_norm_g), a_w_uq=f(a_w_uq), a_kidx_norm_g=f(a_kidx_norm_g),
               a_kidx_norm_b=f(a_kidx_norm_b), a_w_out=f(a_w_out), b_w_in=f(b_w_in), b_forget_bias=f(b_forget_bias),
               b_w_out=f(b_w_out), ln_g=f(ln_g), ln_b=f(ln_b))
    inp = tile_weights(inp)
    cores = [(b, j) for b in range(2) for j in range(4)]
    nc = bass.Bass("TRN2", target_bir_lowering=False)
    build_fused(nc, S)
    ims = [fused_inputs(inp, S, b, j) for (b, j) in cores]
    res = run_bass_kernel_spmd(nc, ims, core_ids=list(range(8))).results
    out = np.zeros((2, S, 4096), np.float32)
    for ci, (b, j) in enumerate(cores):
        out[b, own_pos(S, j), :] = res[ci]["y"]
    return out
```

```python
from concourse.bass_utils import run_bass_kernel_spmd
from contextlib import ExitStack
import numpy as np
import concourse.bass as bass
import concourse.mybir as mybir

F32 = mybir.dt.float32
BF16 = mybir.dt.bfloat16
AF = mybir.ActivationFunctionType
ALU = mybir.AluOpType
AX = mybir.AxisListType


class _St:
    __slots__ = ("w", "r", "wl")

    def __init__(self):
        self.w = None
        self.r = []
        self.wl = []


class KB:
    def __init__(self, nc, same_engine_sync=("act", "dve", "pool")):
        self.nc = nc
        self.es = ExitStack()
        self.engs = {"pe": nc.tensor, "dve": nc.vector, "act": nc.scalar,
                     "pool": nc.gpsimd, "sp": nc.sync}
        self.sem = {}
        self.cnt = {}
        self.waited = {k: {} for k in self.engs}
        for k in self.engs:
            self.sem[k] = self.es.enter_context(nc.semaphore(f"s_{k}"))
            self.cnt[k] = 0
        self.same = set(same_engine_sync)
        self.st = {}
        self.dsems = {}
        self.dcnt = {}
        self.n_inst = 0
        self.n_wait = 0
        self.scopes = []
        self.ncc = 0

    def _es(self):
        return self.scopes[-1] if self.scopes else self.es

    def sbuf(self, name, shape, dtype):
        self.nalloc = getattr(self, "nalloc", 0) + 1
        return self._es().enter_context(self.nc.sbuf_tensor(f"{name}_{self.nalloc}", list(shape), dtype))

    def psum(self, name, shape, dtype=F32):
        self.nalloc = getattr(self, "nalloc", 0) + 1
        return self._es().enter_context(self.nc.psum_tensor(f"{name}_{self.nalloc}", list(shape), dtype))

    def push(self):
        self.scopes.append(ExitStack())

    def pop(self):
        self.barrier()
        self.scopes.pop().close()
        self.st = {}

    def collective(self, kind, src, dst, groups, after=()):
        name = f"cc{self.ncc}"
        self.ncc += 1
        self.dsem(name)
        self._wait("pool", self._deps(after, []))
        inst = self.nc.gpsimd.collective_compute(kind, ALU.bypass, replica_groups=groups, ins=[src.opt()], outs=[dst.opt()])
        inst.then_inc(self.dsems[name], 1)
        self.dcnt[name] += 1
        self.n_inst += 1

    def dsem(self, name):
        if name not in self.dsems:
            self.dsems[name] = self.es.enter_context(self.nc.semaphore(f"d_{name}"))
            self.dcnt[name] = 0
        return name

    def _deps(self, R, W):
        deps = []
        for k in R:
            s = self.st.get(k)
            if s is not None and s.w is not None:
                deps.append(s.w)
            if s is not None:
                deps.extend(s.wl)
        for k in W:
            s = self.st.get(k)
            if s is not None:
                if s.w is not None:
                    deps.append(s.w)
                deps.extend(s.r)
        return deps

    def _wait(self, eng, deps):
        need = {}
        wt = self.waited[eng]
        for (sname, semh, val) in deps:
            if sname == eng and eng not in self.same:
                continue
            if wt.get(sname, 0) >= val:
                continue
            if need.get(sname, (None, 0))[1] < val:
                need[sname] = (semh, val)
        for sname, (semh, val) in need.items():
            self.engs[eng].wait_ge(semh, val)
            wt[sname] = val
            self.n_wait += 1

    def _commit(self, ev, R, W):
        for k in R:
            self.st.setdefault(k, _St()).r.append(ev)
        for k in W:
            s = self.st.setdefault(k, _St())
            s.w = ev
            s.r = []

    def op(self, eng, fn, R=(), W=()):
        W = list(W) + [k for k in R if k.startswith("pp")]
        R = [k for k in R if not k.startswith("pp")]
        self._wait(eng, self._deps(R, W))
        inst = fn(self.engs[eng])
        self.cnt[eng] += 1
        inst.then_inc(self.sem[eng], 1)
        self.n_inst += 1
        ev = (eng, self.sem[eng], self.cnt[eng])
        self._commit(ev, R, W)
        return inst

    def dma(self, parts, R=(), W=(), sem=None, indep=False):
        assert sem is not None
        self.dsem(sem)
        deps = self._deps(R, () if indep else W)
        if self.dcnt[sem] > 0:
            deps.append(("D" + sem, self.dsems[sem], self.dcnt[sem]))
        for q in dict.fromkeys(p[0] for p in parts):
            self._wait(q, deps)
        for (q, o, i) in parts:
            self.engs[q].dma_start(out=o, in_=i).then_inc(self.dsems[sem], 16)
            self.dcnt[sem] += 16
            self.n_inst += 1
        ev = ("D" + sem, self.dsems[sem], self.dcnt[sem])
        if indep:
            self._commit(ev, R, ())
            for k in W:
                self.st.setdefault(k, _St()).wl.append(ev)
        else:
            self._commit(ev, R, W)

    def finish(self, keys):
        self._wait("sp", self._deps(keys, ()))

    def barrier(self):
        deps = [(k, self.sem[k], self.cnt[k]) for k in self.engs if self.cnt[k] > 0]
        deps += [("D" + s, self.dsems[s], self.dcnt[s]) for s in self.dsems if self.dcnt[s] > 0]
        for e in self.engs:
            same = self.same
            self.same = set(self.engs)
            self._wait(e, deps)
            self.same = same

    def close(self):
        self.es.close()


import ml_dtypes

NPBF = ml_dtypes.bfloat16
D = 4096
A_IN = 6336
B_IN = 16416
SCALE = 128 ** -0.5
WSC = 64 ** -0.5 * 128 ** -0.5
NEG = -30000.0


def own_pos(S, j):
    NBc = S // 512
    return np.concatenate([np.arange(128) + (j + 4 * m) * 128 for m in range(NBc)])


def rope_tables(pos):
    inv = (10000.0 ** (-np.arange(64, dtype=np.float32) / 64)).astype(np.float32)
    ang = pos.astype(np.float32)[:, None] * inv[None, :]
    cos, sin = np.cos(ang).astype(np.float32), np.sin(ang).astype(np.float32)
    cosF = np.concatenate([cos.T, cos.T], 0)
    sinF = np.concatenate([-sin.T, sin.T], 0)
    return dict(cosF=np.ascontiguousarray(cosF), sinF=np.ascontiguousarray(sinF),
                cosT=np.ascontiguousarray(cos), sinT=np.ascontiguousarray(sin))


def tile_w(W, groups, KC, row=8192):
    out = np.zeros((len(groups), 128, row), np.float32)
    for g, (c0, width) in enumerate(groups):
        blk = W[:, c0:c0 + width].reshape(KC, 128, width).transpose(1, 0, 2).reshape(128, KC * width)
        out[g, :, :KC * width] = blk
    return out


A_GROUPS = [(g * 256, 256) for g in range(8)] + [(2048, 192)] + [(2240 + g * 256, 256) for g in range(16)]
UQ_GROUPS = [(g * 1024, 1024) for g in range(12)]
O_GROUPS = [(g * 256, 256) for g in range(16)]
B_GROUPS = [(g * 256, 256) for g in range(32)]
BV_GROUPS = [(8192 + g * 512, 512) for g in range(16)] + [(16384, 32)]


def consts():
    perm = np.zeros((128, 128), np.float32)
    for d in range(128):
        perm[(d + 64) % 128, d] = 1.0
    ident = np.eye(128, dtype=np.float32)
    return dict(perm=perm, ident=ident)


def emit_skewed(items, stage1, stage2, D):
    n = len(items)
    for t in range(n + D):
        if t < n:
            stage1(items[t], t)
        if t - D >= 0:
            stage2(items[t - D], t - D)


def dram_in(nc, name, shape, dt=F32, io=None):
    if io is not None and name in io:
        assert list(io[name].shape) == list(shape), (name, io[name].shape, shape)
        return io[name]
    return nc.dram_tensor(name, list(shape), dt, kind="ExternalInput").ap()


def dram_out(nc, name, shape, dt=F32, io=None):
    if io is not None and name in io:
        assert list(io[name].shape) == list(shape), (name, io[name].shape, shape)
        return io[name]
    return nc.dram_tensor(name, list(shape), dt, kind="ExternalOutput").ap()


def phase_a1(nc, S, stages=('ii', 'v', 'ki', 'gate', 'iv'), kb=None, io=None):
    NBc = S // 512
    T = NBc * 128
    TW = min(T, 512)
    TH = T // TW
    xT = dram_in(nc, "xT", [D, T], F32, io)
    w_in = dram_in(nc, "a_w_in", [25, 128, 8192], F32, io)
    w_uq = dram_in(nc, "a_w_uq", [12, 128, 8192], F32, io)
    qg = dram_in(nc, "qg", [128, 8], F32, io)
    kgb = dram_in(nc, "kgb", [128, 256], F32, io)
    cosF_d = dram_in(nc, "cosF", [128, T], F32, io)
    sinF_d = dram_in(nc, "sinF", [128, T], F32, io)
    cosT_d = dram_in(nc, "cosT", [T, 64], F32, io)
    sinT_d = dram_in(nc, "sinT", [T, 64], F32, io)
    perm_d = dram_in(nc, "perm", [128, 128], F32, io)
    ident_d = dram_in(nc, "ident", [128, 128], F32, io)
    qT_o = dram_out(nc, "qT", [32, 128, T], BF16, io)
    qiT_o = dram_out(nc, "qiT", [64, 128, T], BF16, io)
    kT_o = dram_out(nc, "kT", [4, 128, T], BF16, io)
    v_o = dram_out(nc, "v", [T, 512], BF16, io)
    kiT_o = dram_out(nc, "kiT", [128, T], BF16, io)
    widx_o = dram_out(nc, "widx", [T, 64], F32, io)
    sg_o = dram_out(nc, "sgate", [T, 4096], F32, io)

    own = kb is None
    kb = KB(nc) if own else kb
    kb.push()
    xb = kb.sbuf("xb", [128, 32, T], BF16)
    NWS = 2
    wslot = [kb.sbuf(f"wslot{i}", [128, 8192], BF16) for i in range(NWS)]
    cqg = kb.sbuf("cqg", [128, 8, T], BF16)
    cosF = kb.sbuf("cosFs", [128, T], F32)
    sinF = kb.sbuf("sinFs", [128, T], F32)
    cosT = kb.sbuf("cosTs", [128, NBc, 64], F32)
    sinT = kb.sbuf("sinTs", [128, NBc, 64], F32)
    crq = kb.sbuf("crq", [128, T], F32)
    srq = kb.sbuf("srq", [128, T], F32)
    cri = kb.sbuf("cri", [128, T], F32)
    sri = kb.sbuf("sri", [128, T], F32)
    rstd = kb.sbuf("rstd", [128, T], F32)
    qgs = kb.sbuf("qgs", [128, 8], F32)
    kgbs = kb.sbuf("kgbs", [128, 256], F32)
    permf = kb.sbuf("permf", [128, 128], F32)
    permb = kb.sbuf("permb", [128, 128], BF16)
    identf = kb.sbuf("identf", [128, 128], F32)
    identb = kb.sbuf("identb", [128, 128], BF16)
    onesf = kb.sbuf("onesf", [128, 128], F32)
    NR = 2
    sq = [kb.sbuf(f"sq{i}", [128, 512], F32) for i in range(NR)]
    xbr = [kb.sbuf(f"xbr{i}", [128, 512], BF16) for i in range(NR)]
    ra = [kb.sbuf(f"ra{i}", [128, 512], F32) for i in range(NR)]
    rb = [kb.sbuf(f"rb{i}", [128, 512], F32) for i in range(NR)]
    ro = [kb.sbuf(f"ro{i}", [128, 512], BF16) for i in range(NR)]
    go = [kb.sbuf(f"go{i}", [128, 256], F32) for i in range(NR)]
    vo = [kb.sbuf(f"vo{i}", [128, 256], BF16) for i in range(NR)]
    kis = kb.sbuf("kis", [128, 192], F32)
    kin = kb.sbuf("kin", [128, 128], F32)
    kir = kb.sbuf("kir", [128, 128], F32)
    kit = kb.sbuf("kit", [128, 64], F32)
    kib = kb.sbuf("kib", [128, 128], BF16)
    kiTs = kb.sbuf("kiTs", [128, T], BF16)
    wio = kb.sbuf("wio", [128, NBc, 64], F32)
    bst = kb.sbuf("bst", [128, 6], F32)
    mv = kb.sbuf("mv", [128, 4], F32)
    pp = kb.psum("pp", [128, 8, 512], F32)
    ptb = kb.psum("ptb", [128, 128], BF16) if False else None

    xTr = xT.rearrange("(k p) t -> p k t", p=128)
    for kq in range(4):
        kb.dma([("pool", xb[:, kq * 8:(kq + 1) * 8, :], xTr[:, kq * 8:(kq + 1) * 8, :])], W=[f"xb{kq}"], sem=f"xb{kq}")
    XB = [f"xb{kq}" for kq in range(4)]
    kb.dma([("sp", cosF[:], cosF_d), ("sp", sinF[:], sinF_d), ("sp", qgs[:], qg), ("sp", kgbs[:], kgb),
            ("sp", permf[:], perm_d), ("sp", identf[:], ident_d),
            ("sp", cosT[:], cosT_d.rearrange("(m p) c -> p m c", p=128)),
            ("sp", sinT[:], sinT_d.rearrange("(m p) c -> p m c", p=128))], W=["consts"], sem="consts")
    kb.op("dve", lambda e: e.tensor_copy(out=permb[:], in_=permf[:]), R=["consts"], W=["permb"])
    kb.op("dve", lambda e: e.tensor_copy(out=identb[:], in_=identf[:]), R=["consts"], W=["identb"])
    kb.op("dve", lambda e: e.memset(onesf[:], 1.0), W=["onesf"])

    wstate = {"n": 0}

    def load_w(wt, g, kchunks, ncols):
        i = wstate["n"] % NWS
        wstate["n"] += 1
        n = kchunks * ncols
        view = wslot[i][:, 0:n].rearrange("p (k c) -> p k c", k=kchunks)
        step = min(n, 2048)
        parts = [("pool", wslot[i][:, a:a + step], wt[g][:, a:a + step]) for a in range(0, n, step)]
        kb.dma(parts, W=[f"w{i}"], sem=f"w{i}")
        return view, f"w{i}"

    rr = {"n": 0}

    def rope_tile(src_ps, pskey, N, cr, sr, crkeys, dst_dram, dstkey):
        i = rr["n"] % NR
        rr["n"] += 1
        pb = 4 + (rr["n"] % 2)
        kb.op("act", lambda e: e.activation(out=xbr[i][:, :N], in_=src_ps, func=AF.Copy), R=[pskey], W=[f"xbr{i}"])
        kb.op("pe", lambda e: e.matmul(pp[:, pb, :N], lhsT=permb[:], rhs=xbr[i][:, :N], start=True, stop=True),
              R=[f"xbr{i}", "permb"], W=[f"pp{pb}"])
        kb.op("dve", lambda e: e.tensor_tensor(out=ra[i][:, :N], in0=src_ps, in1=cr, op=ALU.mult),
              R=[pskey] + crkeys, W=[f"ra{i}"])
        kb.op("dve", lambda e: e.tensor_tensor(out=rb[i][:, :N], in0=pp[:, pb, :N], in1=sr, op=ALU.mult),
              R=[f"pp{pb}"] + crkeys, W=[f"rb{i}"])
        kb.op("dve", lambda e: e.tensor_tensor(out=ro[i][:, :N], in0=ra[i][:, :N], in1=rb[i][:, :N], op=ALU.add),
              R=[f"ra{i}", f"rb{i}"], W=[f"ro{i}"])
        kb.dma([("sp", dst_dram, ro[i][:, :N])], R=[f"ro{i}"], W=[dstkey], sem=f"ro{i}")

    mm = {"n": 0}

    def next_bank():
        b = mm["n"] % 3
        mm["n"] += 1
        return b

    for g4 in range(4):
        wv, wk = load_w(w_in, g4, 32, 256)
        for fl in range(2):
            fc = g4 * 2 + fl
            for th in range(TH):
                b = next_bank()
                for k in range(32):
                    kb.op("pe", lambda e: e.matmul(pp[:, b, :TW], lhsT=wv[:, k, fl * 128:(fl + 1) * 128],
                                                   rhs=xb[:, k, th * TW:(th + 1) * TW], start=(k == 0), stop=(k == 31)),
                          R=[wk] + XB, W=[f"pp{b}"])
                kb.op("dve", lambda e: e.tensor_scalar(out=cqg[:, fc, th * TW:(th + 1) * TW], in0=pp[:, b, :TW],
                                                       scalar1=qgs[:, fc:fc + 1], scalar2=None, op0=ALU.mult),
                      R=[f"pp{b}", "consts"], W=[f"cqg{th}"])
                i = (fc * TH + th) % NR
                kb.op("act", lambda e: e.activation(out=sq[i][:, :TW], in_=pp[:, b, :TW], func=AF.Square),
                      R=[f"pp{b}"], W=[f"sq{i}"])
                kb.op("pe", lambda e: e.matmul(pp[:, 6 + th, :TW], lhsT=onesf[:], rhs=sq[i][:, :TW],
                                               start=(fc == 0), stop=(fc == 7)),
                      R=[f"sq{i}", "onesf"], W=[f"pp{6 + th}"])
    for th in range(TH):
        sl = slice(th * TW, (th + 1) * TW)
        kb.op("act", lambda e: e.activation(out=rstd[:, sl], in_=pp[:, 6 + th, :TW], func=AF.Sqrt, scale=1.0 / 1024, bias=1e-6),
              R=[f"pp{6 + th}"], W=["rstd"])
    kb.op("dve", lambda e: e.reciprocal(out=rstd[:], in_=rstd[:]), R=["rstd"], W=["rstd"])
    kb.op("dve", lambda e: e.tensor_tensor(out=cri[:], in0=cosF[:], in1=rstd[:], op=ALU.mult), R=["rstd", "consts"], W=["cri"])
    kb.op("dve", lambda e: e.tensor_tensor(out=sri[:], in0=sinF[:], in1=rstd[:], op=ALU.mult), R=["rstd", "consts"], W=["sri"])
    kb.op("pool", lambda e: e.tensor_scalar(out=crq[:], in0=cri[:], scalar1=SCALE, scalar2=None, op0=ALU.mult), R=["cri"], W=["crq"])
    kb.op("pool", lambda e: e.tensor_scalar(out=srq[:], in0=sri[:], scalar1=SCALE, scalar2=None, op0=ALU.mult), R=["sri"], W=["srq"])

    for g2 in (range(2) if 'ii' in stages else []):
        wv, wk = load_w(w_in, 4 + g2, 32, 256)
        for fl in range(2):
            g = g2 * 2 + fl
            for th in range(TH):
                b = next_bank()
                for k in range(32):
                    kb.op("pe", lambda e: e.matmul(pp[:, b, :TW], lhsT=wv[:, k, fl * 128:(fl + 1) * 128],
                                                   rhs=xb[:, k, th * TW:(th + 1) * TW], start=(k == 0), stop=(k == 31)),
                          R=[wk] + XB, W=[f"pp{b}"])
                sl = slice(th * TW, (th + 1) * TW)
                rope_tile(pp[:, b, :TW], f"pp{b}", TW, cosF[:, sl], sinF[:, sl], ["consts"], kT_o[g, :, sl], f"kT{g}")

    def tokmajor(gidx, ncols, handler):
        wv, wk = load_w(w_in, gidx, 32, ncols)
        for tb in range(NBc):
            b = next_bank()
            for k in range(32):
                kb.op("pe", lambda e: e.matmul(pp[:, b, :ncols], lhsT=xb[:, k, tb * 128:(tb + 1) * 128],
                                               rhs=wv[:, k, :], start=(k == 0), stop=(k == 31)),
                      R=[wk] + XB, W=[f"pp{b}"])
            handler(tb, b)

    tm = {"n": 0}

    def h_v(c0v):
        def f(tb, b):
            i = tm["n"] % NR
            tm["n"] += 1
            kb.op("act", lambda e: e.activation(out=vo[i][:], in_=pp[:, b, :256], func=AF.Copy), R=[f"pp{b}"], W=[f"vo{i}"])
            kb.dma([("sp", v_o[tb * 128:(tb + 1) * 128, c0v:c0v + 256], vo[i][:])], R=[f"vo{i}"], W=["v_o"], sem=f"vo{i}", indep=True)
        return f

    if 'v' in stages:
        tokmajor(6, 256, h_v(0))
        tokmajor(7, 256, h_v(256))

    def h_ki(tb, b):
        kb.op("act", lambda e: e.activation(out=kis[:], in_=pp[:, b, :192], func=AF.Copy), R=[f"pp{b}"], W=["kis"])
        kb.op("dve", lambda e: e.tensor_scalar(out=wio[:, tb, :], in0=kis[:, 128:192], scalar1=WSC, scalar2=None, op0=ALU.mult),
              R=["kis"], W=["wio"])
        kb.op("dve", lambda e: e.bn_stats(out=bst[:], in_=kis[:, 0:128]), R=["kis"], W=["bst"])
        kb.op("dve", lambda e: e.bn_aggr(out=mv[:, 0:2], in_=bst[:]), R=["bst"], W=["mv"])
        kb.op("act", lambda e: e.activation(out=mv[:, 2:3], in_=mv[:, 1:2], func=AF.Sqrt, scale=1.0, bias=1e-5), R=["mv"], W=["mv"])
        kb.op("dve", lambda e: e.reciprocal(out=mv[:, 3:4], in_=mv[:, 2:3]), R=["mv"], W=["mv"])
        kb.op("dve", lambda e: e.tensor_scalar(out=kin[:], in0=kis[:, 0:128], scalar1=mv[:, 0:1], scalar2=mv[:, 3:4],
                                               op0=ALU.subtract, op1=ALU.mult), R=["kis", "mv"], W=["kin"])
        kb.op("dve", lambda e: e.tensor_tensor(out=kin[:], in0=kin[:], in1=kgbs[:, 0:128], op=ALU.mult), R=["kin", "consts"], W=["kin"])
        kb.op("dve", lambda e: e.tensor_tensor(out=kin[:], in0=kin[:], in1=kgbs[:, 128:256], op=ALU.add), R=["kin", "consts"], W=["kin"])
        c, s = cosT[:, tb, :], sinT[:, tb, :]
        kb.op("dve", lambda e: e.tensor_tensor(out=kir[:, 0:64], in0=kin[:, 0:64], in1=c, op=ALU.mult), R=["kin", "consts"], W=["kir"])
        kb.op("dve", lambda e: e.tensor_tensor(out=kit[:], in0=kin[:, 64:128], in1=s, op=ALU.mult), R=["kin", "consts"], W=["kit"])
        kb.op("dve", lambda e: e.tensor_tensor(out=kir[:, 0:64], in0=kir[:, 0:64], in1=kit[:], op=ALU.subtract), R=["kir", "kit"], W=["kir"])
        kb.op("dve", lambda e: e.tensor_tensor(out=kir[:, 64:128], in0=kin[:, 64:128], in1=c, op=ALU.mult), R=["kin", "consts"], W=["kir"])
        kb.op("dve", lambda e: e.tensor_tensor(out=kit[:], in0=kin[:, 0:64], in1=s, op=ALU.mult), R=["kin", "consts", "kir"], W=["kit"])
        kb.op("dve", lambda e: e.tensor_tensor(out=kib[:, 64:128], in0=kir[:, 64:128], in1=kit[:], op=ALU.add), R=["kir", "kit"], W=["kib"])
        kb.op("dve", lambda e: e.tensor_copy(out=kib[:, 0:64], in_=kir[:, 0:64]), R=["kir"], W=["kib"])
        pt = pp[:, 7, 0:64].bitcast(BF16)
        kb.op("pe", lambda e: e.transpose(pt, kib[:], identb[:]), R=["kib", "identb"], W=["pp7"])
        kb.op("act", lambda e: e.activation(out=kiTs[:, tb * 128:(tb + 1) * 128], in_=pt, func=AF.Copy), R=["pp7"], W=["kiTs"])

    if 'ki' in stages:
        tokmajor(8, 192, h_ki)
        kb.dma([("sp", kiT_o, kiTs[:])], R=["kiTs"], W=["kiT_o"], sem="kiTo")
        kb.dma([("sp", widx_o.rearrange("(m p) c -> p m c", p=128), wio[:])], R=["wio"], W=["widx_o"], sem="wio")

    def h_gate(c0g):
        def f(tb, b):
            i = tm["n"] % NR
            tm["n"] += 1
            kb.op("act", lambda e: e.activation(out=go[i][:], in_=pp[:, b, :256], func=AF.Silu), R=[f"pp{b}"], W=[f"go{i}"])
            kb.dma([("sp", sg_o[tb * 128:(tb + 1) * 128, c0g:c0g + 256], go[i][:])], R=[f"go{i}"], W=["sg_o"], sem=f"go{i}", indep=True)
        return f

    for gg in (range(16) if 'gate' in stages else []):
        tokmajor(9 + gg, 256, h_gate(gg * 256))

    for g12 in (range(12) if 'iv' in stages else []):
        wv, wk = load_w(w_uq, g12, 8, 1024)
        for hl in range(8):
            hh = g12 * 8 + hl
            for th in range(TH):
                b = next_bank()
                for k in range(8):
                    kb.op("pe", lambda e: e.matmul(pp[:, b, :TW], lhsT=wv[:, k, hl * 128:(hl + 1) * 128],
                                                   rhs=cqg[:, k, th * TW:(th + 1) * TW], start=(k == 0), stop=(k == 7)),
                          R=[wk] + [f"cqg{t}" for t in range(TH)], W=[f"pp{b}"])
                sl = slice(th * TW, (th + 1) * TW)
                if hh < 32:
                    rope_tile(pp[:, b, :TW], f"pp{b}", TW, crq[:, sl], srq[:, sl], ["crq", "srq"], qT_o[hh, :, sl], f"qT{hh}")
                else:
                    rope_tile(pp[:, b, :TW], f"pp{b}", TW, cri[:, sl], sri[:, sl], ["cri", "sri"], qiT_o[hh - 32, :, sl], f"qiT{hh}")
    outs = [f"kT{g}" for g in range(4)] + [f"qT{h}" for h in range(32)] + [f"qiT{h}" for h in range(32, 96)] + \
           ["v_o", "kiT_o", "widx_o", "sg_o"]
    kb.finish([o for o in outs if o in kb.st])
    print("A1 inst", kb.n_inst, "waits", kb.n_wait)
    kb.pop()
    if own:
        kb.close()
    return nc


def a1_inputs(inp, S, b, j):
    pos = own_pos(S, j)
    rt = rope_tables(pos)
    c = consts()
    qg = np.ascontiguousarray(inp["a_q_norm_g"][0].reshape(8, 128).T)
    kgb = np.concatenate([np.broadcast_to(inp["a_kidx_norm_g"][0][None, :], (128, 128)),
                          np.broadcast_to(inp["a_kidx_norm_b"][0][None, :], (128, 128))], 1)
    return dict(xT=np.ascontiguousarray(inp["x"][b, pos, :].T), a_w_in=inp["a_w_in_t"], a_w_uq=inp["a_w_uq_t"],
                qg=qg, kgb=np.ascontiguousarray(kgb), perm=c["perm"], ident=c["ident"], **rt)


def phase_a2a(nc, S, kb=None, io=None):
    NB = S // 128
    NBc = S // 512
    T = NBc * 128
    qT = dram_in(nc, "qT", [32, 128, T], BF16, io)
    qiT = dram_in(nc, "qiT", [64, 128, T], BF16, io)
    widx = dram_in(nc, "widx", [T, 64], F32, io)
    sgate = dram_in(nc, "sgate", [T, 4096], F32, io)
    kTf = dram_in(nc, "kTg", [4 * 512, T], BF16, io)
    vf = dram_in(nc, "vg", [4 * T, 512], BF16, io)
    kiTf = dram_in(nc, "kiTg", [4 * 128, T], BF16, io)
    ident_d = dram_in(nc, "ident", [128, 128], F32, io)
    cm4_d = dram_in(nc, "cm4", [128, 4, 128], F32, io)
    ogT_o = dram_out(nc, "ogT", [4096, T], BF16, io)

    own = kb is None
    kb = KB(nc) if own else kb
    kb.push()
    kis = kb.sbuf("kiTs", [128, S], BF16)
    kts = kb.sbuf("kTs", [128, 4, S], BF16)
    vs = kb.sbuf("vs", [128, NB, 4, 130], BF16)
    qib = kb.sbuf("qib", [128, 64, 128], BF16)
    Dg = kb.sbuf("Dg", [128, 64, 128], BF16)
    wix = kb.sbuf("wix", [128, 64], F32)
    score2 = [kb.sbuf(f"score{i}", [128, S], F32) for i in range(2)]
    mb = kb.sbuf("mb", [128, S], BF16)
    mbT = kb.sbuf("mbT", [128, NB, 128], BF16)
    qblk = kb.sbuf("qblk", [128, 32, 128], BF16)
    identf = kb.sbuf("identf", [128, 128], F32)
    identb = kb.sbuf("identb", [128, 128], BF16)
    cm4 = kb.sbuf("cm4s", [128, 4, 128], F32)
    half = kb.sbuf("half", [128, 1], F32)
    bs = kb.sbuf("bs", [128, 8], F32)
    KIT = 26
    pw2 = kb.sbuf("pw2", [128, KIT + 1], F32)
    hht = kb.sbuf("hht", [128, KIT + 1], F32)
    rl = [kb.sbuf(f"rl{i}", [128, 512], BF16) for i in range(6)]
    sm = [kb.sbuf(f"sm{i}", [128, 512], F32) for i in range(4)]
    pT = [kb.sbuf(f"pT{i}", [128, 512], BF16) for i in range(4)]
    sgb = [kb.sbuf(f"sgb{i}", [128, 512], F32) for i in range(2)]
    og4 = kb.sbuf("og4", [128, 512], BF16)
    ogT4 = [kb.sbuf(f"ogT4{i}", [128, 4, 128], BF16) for i in range(2)]
    rs = kb.sbuf("rs", [128, 4], F32)
    pp = kb.psum("pp", [128, 8, 512], F32)

    kb.dma([("sp", identf[:], ident_d), ("sp", cm4[:], cm4_d)] +
           [("sp", kis[:].rearrange("d (m j p) -> d m j p", j=4, p=128)[:, :, j, :],
             kiTf[j * 128:(j + 1) * 128, :].rearrange("d (m p) -> d m p", p=128)) for j in range(4)], W=["consts"], sem="consts")
    kb.dma([("sp", kts[:, g, :].rearrange("d (m j p) -> d m j p", j=4, p=128)[:, :, j, :],
             kTf[j * 512 + g * 128:j * 512 + (g + 1) * 128, :].rearrange("d (m p) -> d m p", p=128))
            for g in range(4) for j in range(4)], W=["kts"], sem="kts")
    kb.op("pool", lambda e: e.memset(vs[:], 1.0), W=["vs"])
    kb.dma([("sp", vs[:, :, g, 0:128].rearrange("p (m j) d -> p m j d", j=4)[:, :, j, :],
             vf[j * T:(j + 1) * T, g * 128:(g + 1) * 128].rearrange("(m p) d -> p m d", p=128))
            for g in range(4) for j in range(4)], W=["vs"], sem="vs")
    kb.op("dve", lambda e: e.tensor_copy(out=identb[:], in_=identf[:]), R=["consts"], W=["identb"])
    kb.op("dve", lambda e: e.memset(half[:], 0.5), W=["half"])
    for k in range(KIT + 1):
        kb.op("pool", lambda e: e.memset(pw2[:, k:k + 1], 2.0 ** -(k + 1)), W=["pw2"])
    LO, HI, MID, CNT, GE, D1, D2 = [slice(i, i + 1) for i in range(7)]
    ev = {"n": 0}

    def stage_I(m):
        nk = 4 * m + 4
        SK = nk * 128
        qs = slice(m * 128, (m + 1) * 128)
        score = score2[m % 2]
        skey = f"score{m % 2}"
        junk = mb
        kb.dma([("sp", qib[:], qiT[:, :, qs].rearrange("h d q -> d h q")), ("sp", wix[:], widx[qs, :])], W=["qib", "wix"], sem="qload")
        for h in range(64):
            eng = "dve" if h % 2 == 0 else "pool"
            kb.op(eng, lambda e: e.tensor_scalar(out=Dg[:, h, :], in0=identf[:], scalar1=wix[:, h:h + 1], scalar2=None, op0=ALU.mult),
                  R=["consts", "wix"], W=[f"Dg{h}"])
        DB = (0, 1, 2, 5, 6, 7)
        items = [(c0, min(512, SK - c0), h) for c0 in range(0, SK, 512) for h in range(64)]

        def idx_s1(it, t):
            c0, N, h = it
            b = DB[t % 6]
            kb.op("pe", lambda e: e.matmul(pp[:, b, :N], lhsT=qib[:, h, :], rhs=kis[:, c0:c0 + N], start=True, stop=True),
                  R=["qib", "consts"], W=[f"pp{b}"])

        def idx_s2(it, t):
            c0, N, h = it
            b = DB[t % 6]
            i = t % 6
            if True:
                kb.op("act", lambda e: e.activation(out=rl[i][:, :N], in_=pp[:, b, :N], func=AF.Relu), R=[f"pp{b}"], W=[f"rl{i}"])
            else:
                kb.op("dve", lambda e: e.tensor_scalar(out=rl[i][:, :N], in0=pp[:, b, :N], scalar1=0.0, scalar2=None, op0=ALU.max),
                      R=[f"pp{b}"], W=[f"rl{i}"])
            kb.op("pe", lambda e: e.matmul(pp[:, 3, :N], lhsT=Dg[:, h, :], rhs=rl[i][:, :N], start=(h == 0), stop=(h == 63)),
                  R=[f"rl{i}", f"Dg{h}"], W=["pp3"])
            if h == 63:
                kb.op("act", lambda e: e.activation(out=score[:, c0:c0 + N], in_=pp[:, 3, :N], func=AF.Copy), R=["pp3"], W=[skey])

        emit_skewed(items, idx_s1, idx_s2, 4)

    def stage_B(m):
        nk = 4 * m + 4
        SK = nk * 128
        qs = slice(m * 128, (m + 1) * 128)
        score = score2[m % 2]
        skey = f"score{m % 2}"
        junk = mb
        W0 = slice(5, 6)
        SG = slice(4, 5)
        kb.op("dve", lambda e: e.tensor_reduce(out=bs[:, LO], in_=score[:, :SK], op=ALU.min, axis=AX.X), R=[skey], W=["bs"])
        kb.op("dve", lambda e: e.tensor_reduce(out=bs[:, HI], in_=score[:, :SK], op=ALU.max, axis=AX.X), R=[skey], W=["bs"])
        kb.op("dve", lambda e: e.tensor_tensor(out=score[:, SK - 512:SK], in0=score[:, SK - 512:SK],
                                               in1=cm4[:].rearrange("p a b -> p (a b)"), op=ALU.add), R=[skey, "consts"], W=[skey])
        kb.op("dve", lambda e: e.tensor_tensor(out=bs[:, W0], in0=bs[:, HI], in1=bs[:, LO], op=ALU.subtract), R=["bs"], W=["bs"])
        kb.op("dve", lambda e: e.tensor_scalar(out=bs[:, W0], in0=bs[:, W0], scalar1=1.01, scalar2=1e-5, op0=ALU.mult, op1=ALU.add), R=["bs"], W=["bs"])
        kb.op("dve", lambda e: e.tensor_scalar(out=hht[:], in0=pw2[:], scalar1=bs[:, W0], scalar2=None, op0=ALU.mult), R=["bs", "pw2"], W=["hh"])
        kb.op("dve", lambda e: e.tensor_tensor(out=bs[:, MID], in0=bs[:, HI], in1=hht[:, 0:1], op=ALU.subtract), R=["bs", "hh"], W=["bs"])
        for it in range(KIT):
            kb.op("dve", lambda e: e.tensor_scalar(out=junk[:, :SK], in0=score[:, :SK], scalar1=bs[:, MID], scalar2=None,
                                                   op0=ALU.is_ge, op1=ALU.add, accum_out=bs[:, CNT]), R=[skey, "bs"], W=["mb", "bs"])
            kb.op("dve", lambda e: e.tensor_scalar(out=bs[:, SG], in0=bs[:, CNT], scalar1=255.5, scalar2=0.5, op0=ALU.is_ge, op1=ALU.subtract),
                  R=["bs"], W=["bs"])
            kb.op("dve", lambda e: e.scalar_tensor_tensor(out=bs[:, MID], in0=bs[:, SG], scalar=hht[:, it:it + 1], in1=bs[:, MID],
                                                          op0=ALU.mult, op1=ALU.add), R=["bs", "hh"], W=["bs"])
        kb.op("dve", lambda e: e.tensor_tensor(out=bs[:, LO], in0=bs[:, MID], in1=hht[:, KIT:KIT + 1], op=ALU.subtract), R=["bs", "hh"], W=["bs"])
        kb.op("dve", lambda e: e.tensor_scalar(out=mb[:, :SK], in0=score[:, :SK], scalar1=bs[:, LO], scalar2=NEG, op0=ALU.is_lt, op1=ALU.mult),
              R=[skey, "bs"], W=["mb"])
        for k4 in range(0, nk, 4):
            pt = pp[:, 4, 0:256].bitcast(BF16).rearrange("p (a b) -> p a b", a=4)
            for a in range(4):
                kb.op("pe", lambda e: e.transpose(pt[:, a, :], mb[:, (k4 + a) * 128:(k4 + a + 1) * 128], identb[:]), R=["mb", "identb"], W=["pp4"])
            kb.op("act", lambda e: e.activation(out=mbT[:, k4:k4 + 4, :], in_=pt, func=AF.Copy), R=["pp4"], W=["mbT"])

    def stage_A(m):
        nk = 4 * m + 4
        SK = nk * 128
        qs = slice(m * 128, (m + 1) * 128)
        score = score2[m % 2]
        skey = f"score{m % 2}"
        junk = mb
        kb.dma([("sp", qblk[:], qT[:, :, qs].rearrange("h d q -> d h q"))], W=["qblk"], sem="qblk")
        aitems = [(g, hp, kk) for g in range(4) for hp in range(2) for kk in range(nk)]

        def att_s1(it, t):
            g, hp, kk = it
            h0 = g * 8 + hp * 4
            b = t % 4
            if kk == 0:
                sgi = (g * 2 + hp) % 2
                kb.dma([("sp", sgb[sgi][:], sgate[qs, h0 * 128:(h0 + 4) * 128])], W=[f"sgb{sgi}"], sem=f"sgb{sgi}")
            kb.op("pe", lambda e: e.matmul(pp[:, b, :], lhsT=kts[:, g, kk * 128:(kk + 1) * 128],
                                           rhs=qblk[:, h0:h0 + 4, :].rearrange("p h q -> p (h q)"), start=True, stop=True),
                  R=["kts", "qblk"], W=[f"pp{b}"])

        def att_s2(it, t):
            g, hp, kk = it
            h0 = g * 8 + hp * 4
            b = t % 4
            i = t % 4
            sgi = (g * 2 + hp) % 2
            kb.op("dve", lambda e: e.tensor_tensor(out=sm[i][:].rearrange("p (h q) -> p h q", h=4),
                                                   in0=pp[:, b, :].rearrange("p (h q) -> p h q", h=4),
                                                   in1=mbT[:, kk:kk + 1, :].to_broadcast([128, 4, 128]), op=ALU.add),
                  R=[f"pp{b}", "mbT"], W=[f"sm{i}"])
            kb.op("act", lambda e: e.activation(out=pT[i][:], in_=sm[i][:], func=AF.Exp), R=[f"sm{i}"], W=[f"pT{i}"])
            for hh in range(4):
                kb.op("pe", lambda e: e.matmul(pp[:, 4 + hh, 0:129], lhsT=pT[i][:, hh * 128:(hh + 1) * 128], rhs=vs[:, kk, g, 0:129],
                                               start=(kk == 0), stop=(kk == nk - 1)), R=[f"pT{i}", "vs"], W=[f"pp{4 + hh}"])
            if kk == nk - 1:
                for hh in range(4):
                    kb.op("dve", lambda e: e.reciprocal(out=rs[:, hh:hh + 1], in_=pp[:, 4 + hh, 128:129]), R=[f"pp{4 + hh}"], W=["rs"])
                    kb.op("dve", lambda e: e.scalar_tensor_tensor(out=og4[:, hh * 128:(hh + 1) * 128], in0=pp[:, 4 + hh, 0:128],
                                                                  scalar=rs[:, hh:hh + 1], in1=sgb[sgi][:, hh * 128:(hh + 1) * 128],
                                                                  op0=ALU.mult, op1=ALU.mult), R=[f"pp{4 + hh}", "rs", f"sgb{sgi}"], W=["og4"])
                i2 = ev["n"] % 2
                ev["n"] += 1
                pt = pp[:, 4, 256:512].bitcast(BF16).rearrange("p (a b) -> p a b", a=4)
                for hh in range(4):
                    kb.op("pe", lambda e: e.transpose(pt[:, hh, :], og4[:, hh * 128:(hh + 1) * 128], identb[:]), R=["og4", "identb"], W=["pp4"])
                kb.op("act", lambda e: e.activation(out=ogT4[i2][:], in_=pt, func=AF.Copy), R=["pp4"], W=[f"ogT4{i2}"])
                kb.dma([("sp", ogT_o[h0 * 128:(h0 + 4) * 128, qs].rearrange("(h d) q -> d h q", h=4), ogT4[i2][:])],
                       R=[f"ogT4{i2}"], W=["ogT_o"], sem=f"ogT4{i2}", indep=True)

        emit_skewed(aitems, att_s1, att_s2, 3)

    stage_I(0)
    for m in range(NBc):
        if m + 1 < NBc:
            stage_I(m + 1)
        stage_B(m)
        stage_A(m)
    kb.finish(["ogT_o"])
    print("A2a inst", kb.n_inst, "waits", kb.n_wait)
    kb.pop()
    if own:
        kb.close()
    return nc


def causal_tables(j):
    q = np.arange(128)[:, None]
    s = np.arange(128)[None, :]
    cm4 = np.zeros((128, 4, 128), np.float32)
    tri4 = np.zeros((128, 4, 128), np.float32)
    for r in range(4):
        if r == j:
            cm4[:, r, :] = np.where(s <= q, 0.0, -1e30)
            tri4[:, r, :] = np.where(s.T <= q.T, 0.0, NEG)
        elif r > j:
            cm4[:, r, :] = -1e30
            tri4[:, r, :] = NEG
    return cm4, tri4


def phase_out(nc, S, final, kb=None, io=None):
    NBc = S // 512
    T = NBc * 128
    HB = min(4, NBc)
    ogT = dram_in(nc, "ogT", [4096, T], BF16, io)
    w_out = dram_in(nc, "w_out", [16, 128, 8192], F32, io)
    x_d = dram_in(nc, "x", [T, 4096], F32, io)
    lng_d = dram_in(nc, "lng", [128, 4096], F32, io)
    lnb_d = dram_in(nc, "lnb", [128, 4096], F32, io)
    ident_d = dram_in(nc, "ident", [128, 128], F32, io)
    y_o = dram_out(nc, "y", [T, 4096], F32, io)
    yT_o = None if final else dram_out(nc, "yT", [4096, T], BF16, io)
    ALPHA = 4 ** 0.25

    own = kb is None
    kb = KB(nc) if own else kb
    kb.push()
    ogb = kb.sbuf("ogb", [128, 32, HB * 128], BF16)
    wslot = [kb.sbuf(f"wslot{i}", [128, 32, 256], BF16) for i in range(2)]
    ybuf = kb.sbuf("ybuf", [128, HB, 4096], F32)
    lng = kb.sbuf("lngs", [128, 4096], F32)
    lnb = kb.sbuf("lnbs", [128, 4096], F32)
    identf = kb.sbuf("identf", [128, 128], F32)
    bst = kb.sbuf("bst", [128, 8, 6], F32)
    mv = kb.sbuf("mv", [128, 4], F32)
    yTs = kb.sbuf("yTs", [128, 32, 128], BF16)
    pp = kb.psum("pp", [128, 8, 512], F32)
    kb.dma([("sp", lng[:], lng_d), ("sp", lnb[:], lnb_d), ("sp", identf[:], ident_d)], W=["consts"], sem="consts")
    ogr = ogT.rearrange("(c p) t -> p c t", p=128)
    wn = {"n": 0}
    for hf in range(NBc // HB):
        t0 = hf * HB * 128
        kb.dma([("sp", ogb[:, a * 8:(a + 1) * 8, :], ogr[:, a * 8:(a + 1) * 8, t0:t0 + HB * 128]) for a in range(4)], W=["ogb"], sem="ogb")
        kb.dma([("sp", ybuf[:, tb, :], x_d[t0 + tb * 128:t0 + (tb + 1) * 128, :]) for tb in range(HB)], W=["ybuf"], sem="ybuf")
        for cg in range(16):
            i = wn["n"] % 2
            wn["n"] += 1
            wfl = wslot[i][:].rearrange("p k c -> p (k c)")
            kb.dma([("pool", wfl[:, a * 2048:(a + 1) * 2048], w_out[cg][:, a * 2048:(a + 1) * 2048]) for a in range(4)], W=[f"w{i}"], sem=f"w{i}")
            for tb in range(HB):
                b = (cg * HB + tb) % 4
                for c in range(32):
                    kb.op("pe", lambda e: e.matmul(pp[:, b, :256], lhsT=ogb[:, c, tb * 128:(tb + 1) * 128], rhs=wslot[i][:, c, :],
                                                   start=(c == 0), stop=(c == 31)), R=["ogb", f"w{i}"], W=[f"pp{b}"])
                ysl = ybuf[:, tb, cg * 256:(cg + 1) * 256]
                kb.op("dve", lambda e: e.scalar_tensor_tensor(out=ysl, in0=ysl, scalar=ALPHA, in1=pp[:, b, :256], op0=ALU.mult, op1=ALU.add),
                      R=[f"pp{b}"], W=["ybuf"])
        for tb in range(HB):
            for c8 in range(8):
                kb.op("dve", lambda e: e.bn_stats(out=bst[:, c8, :], in_=ybuf[:, tb, c8 * 512:(c8 + 1) * 512]), R=["ybuf"], W=["bst"])
            kb.op("dve", lambda e: e.bn_aggr(out=mv[:, 0:2], in_=bst[:].rearrange("p a b -> p (a b)")), R=["bst"], W=["mv"])
            kb.op("act", lambda e: e.activation(out=mv[:, 2:3], in_=mv[:, 1:2], func=AF.Sqrt, scale=1.0, bias=1e-5), R=["mv"], W=["mv"])
            kb.op("dve", lambda e: e.reciprocal(out=mv[:, 3:4], in_=mv[:, 2:3]), R=["mv"], W=["mv"])
            kb.op("dve", lambda e: e.tensor_scalar(out=ybuf[:, tb, :], in0=ybuf[:, tb, :], scalar1=mv[:, 0:1], scalar2=mv[:, 3:4],
                                                   op0=ALU.subtract, op1=ALU.mult), R=["mv"], W=["ybuf"])
            kb.op("pool", lambda e: e.tensor_tensor(out=ybuf[:, tb, :], in0=ybuf[:, tb, :], in1=lng[:], op=ALU.mult), R=["consts"], W=["ybuf"])
            kb.op("dve", lambda e: e.tensor_tensor(out=ybuf[:, tb, :], in0=ybuf[:, tb, :], in1=lnb[:], op=ALU.add), R=["consts"], W=["ybuf"])
            kb.dma([("sp", y_o[t0 + tb * 128:t0 + (tb + 1) * 128, :], ybuf[:, tb, :])], R=["ybuf"], W=["y_o"], sem="yo", indep=True)
            if not final:
                for c4 in range(8):
                    b = 4 + c4 % 2
                    for a in range(4):
                        c = c4 * 4 + a
                        kb.op("pe", lambda e: e.transpose(pp[:, b, a * 128:(a + 1) * 128], ybuf[:, tb, c * 128:(c + 1) * 128], identf[:]),
                              R=["ybuf", "consts"], W=[f"pp{b}"])
                    kb.op("act", lambda e: e.activation(out=yTs[:, c4 * 4:(c4 + 1) * 4, :].rearrange("p a b -> p (a b)"), in_=pp[:, b, :],
                                                        func=AF.Copy), R=[f"pp{b}"], W=["yTs"])
                kb.dma([("sp", yT_o.rearrange("(c p) t -> p c t", p=128)[:, :, t0 + tb * 128:t0 + (tb + 1) * 128], yTs[:])],
                       R=["yTs"], W=["yT_o"], sem="yTo", indep=True)
    kb.finish(["y_o", "yT_o"])
    print("OUT inst", kb.n_inst, "waits", kb.n_wait)
    kb.pop()
    if own:
        kb.close()
    return nc


def phase_b1(nc, S, kb=None, io=None):
    NBc = S // 512
    T = NBc * 128
    TW = min(T, 512)
    TH = T // TW
    x1T = dram_in(nc, "x1T", [4096, T], BF16, io)
    w_in = dram_in(nc, "b_w_in", [32, 128, 8192], F32, io)
    w_vg = dram_in(nc, "b_w_vg", [17, 128, 16384], F32, io)
    fb_d = dram_in(nc, "fb", [128, 32], F32, io)
    qT_o = dram_out(nc, "qT1", [32, 128, T], BF16, io)
    kT_o = dram_out(nc, "kT1", [32, 128, T], BF16, io)
    v_o = dram_out(nc, "v1", [T, 4096], BF16, io)
    sg_o = dram_out(nc, "sgate1", [T, 4096], F32, io)
    lf_o = dram_out(nc, "lf", [T, 32], F32, io)

    own = kb is None
    kb = KB(nc) if own else kb
    kb.push()
    xb = kb.sbuf("xb", [128, 32, T], BF16)
    wslot = [kb.sbuf(f"wslot{i}", [128, 32, 512], BF16) for i in range(2)]
    wstage = kb.sbuf("wstage", [128, 8192], F32)
    fbs = kb.sbuf("fbs", [128, 32], F32)
    NR = 4
    ro = [kb.sbuf(f"ro{i}", [128, 512], BF16) for i in range(NR)]
    go = [kb.sbuf(f"go{i}", [128, 512], F32) for i in range(NR)]
    vo = [kb.sbuf(f"vo{i}", [128, 512], BF16) for i in range(NR)]
    zz = kb.sbuf("zz", [128, 32], F32)
    lfo = kb.sbuf("lfo", [128, NBc, 32], F32)
    pp = kb.psum("pp", [128, 8, 512], F32)
    xr = x1T.rearrange("(k p) t -> p k t", p=128)
    kb.dma([("sp", xb[:, a * 8:(a + 1) * 8, :], xr[:, a * 8:(a + 1) * 8, :]) for a in range(4)], W=["xb"], sem="xb")
    kb.dma([("sp", fbs[:], fb_d)], W=["fbs"], sem="fbs")
    wn = {"n": 0}

    def load_w(gidx, ncols, wt=None):
        wt = w_in if wt is None else wt
        i = wn["n"] % 2
        wn["n"] += 1
        n = 32 * ncols
        step = min(n, 2048)
        wfl = wslot[i][:].rearrange("p k c -> p (k c)")
        if True:
            kb.dma([("pool", wfl[:, a:a + step], wt[gidx][:, a:a + step]) for a in range(0, n, step)], W=[f"w{i}a", f"w{i}b"], sem=f"w{i}")
        else:
            kb.dma([("sp", wstage[:, a:a + step], w_in[gidx][:, a:a + step]) for a in range(0, n, step)], W=["wstage"], sem="wstage")
            for a in range(0, n, 4096):
                e_ = min(n, a + 4096)
                kb.op("dve", lambda e: e.tensor_copy(out=wfl[:, a:e_], in_=wstage[:, a:e_]), R=["wstage"], W=[f"w{i}a" if a == 0 else f"w{i}b"])
        return wfl[:, 0:n].rearrange("p (k c) -> p k c", k=32), f"w{i}"

    mm = {"n": 0}
    cc = io.get("cc") if io is not None else None
    if cc is not None:
        nck = max(1, (4096 * T * 2) >> 20)
        rc = 4096 // nck
    for gi in range(32):
        wv, wk = load_w(gi, 256)
        for fl in range(2):
            hh = gi * 2 + fl
            for th in range(TH):
                b = mm["n"] % 4
                mm["n"] += 1
                for k in range(32):
                    kb.op("pe", lambda e: e.matmul(pp[:, b, :TW], lhsT=wv[:, k, fl * 128:(fl + 1) * 128], rhs=xb[:, k, th * TW:(th + 1) * TW],
                                                   start=(k == 0), stop=(k == 31)), R=[wk + "a", wk + "b", "xb"], W=[f"pp{b}"])
                i = mm["n"] % NR
                sl = slice(th * TW, (th + 1) * TW)
                if hh < 32:
                    kb.op("act", lambda e: e.activation(out=ro[i][:, :TW], in_=pp[:, b, :TW], func=AF.Copy, scale=SCALE), R=[f"pp{b}"], W=[f"ro{i}"])
                    kb.dma([("sp", qT_o[hh, :, sl], ro[i][:, :TW])], R=[f"ro{i}"], W=["qT_o"], sem=f"ro{i}", indep=True)
                else:
                    kb.op("dve", lambda e: e.tensor_copy(out=ro[i][:, :TW], in_=pp[:, b, :TW]), R=[f"pp{b}"], W=[f"ro{i}"])
                    kb.dma([("sp", kT_o[hh - 32, :, sl], ro[i][:, :TW])], R=[f"ro{i}"], W=["kT_o"], sem=f"ro{i}", indep=True)
        if cc is not None and gi >= 16:
            rows_done = (gi - 16 + 1) * 256
            if rows_done % rc == 0:
                c = rows_done // rc - 1
                kb.collective("AllGather", cc["kT1"][c * rc:(c + 1) * rc, :], cc["kT1g"][c * 4 * rc:(c + 1) * 4 * rc, :], cc["groups"], after=["kT_o"])

    def tokmajor(gidx, ncols, handler):
        wv, wk = load_w(gidx, ncols, w_vg)
        for tb in range(NBc):
            b = mm["n"] % 4
            mm["n"] += 1
            for k in range(32):
                kb.op("pe", lambda e: e.matmul(pp[:, b, :ncols], lhsT=xb[:, k, tb * 128:(tb + 1) * 128], rhs=wv[:, k, :],
                                               start=(k == 0), stop=(k == 31)), R=[wk + "a", wk + "b", "xb"], W=[f"pp{b}"])
            handler(tb, b)

    tm = {"n": 0}

    def h_v(c0v):
        def f(tb, b):
            i = tm["n"] % NR
            tm["n"] += 1
            kb.op("dve", lambda e: e.tensor_copy(out=vo[i][:], in_=pp[:, b, :512]), R=[f"pp{b}"], W=[f"vo{i}"])
            kb.dma([("sp", v_o[tb * 128:(tb + 1) * 128, c0v:c0v + 512], vo[i][:])], R=[f"vo{i}"], W=["v_o"], sem=f"vo{i}", indep=True)
        return f

    def h_gate(c0g):
        def f(tb, b):
            i = tm["n"] % NR
            tm["n"] += 1
            kb.op("act", lambda e: e.activation(out=go[i][:], in_=pp[:, b, :512], func=AF.Silu), R=[f"pp{b}"], W=[f"go{i}"])
            kb.dma([("sp", sg_o[tb * 128:(tb + 1) * 128, c0g:c0g + 512], go[i][:])], R=[f"go{i}"], W=["sg_o"], sem=f"go{i}", indep=True)
        return f

    for gi in range(8):
        tokmajor(gi, 512, h_v(gi * 512))
    if cc is not None:
        for m in range(NBc):
            kb.collective("AllGather", cc["v1"][m * 128:(m + 1) * 128, :], cc["v1g"][m * 512:(m + 1) * 512, :], cc["groups"], after=["v_o"])
    for gi in range(8):
        tokmajor(8 + gi, 512, h_gate(gi * 512))

    def h_f(tb, b):
        kb.op("dve", lambda e: e.tensor_tensor(out=zz[:], in0=pp[:, b, :32], in1=fbs[:], op=ALU.add), R=[f"pp{b}", "fbs"], W=["zz"])
        kb.op("act", lambda e: e.activation(out=zz[:], in_=zz[:], func=AF.Exp, scale=-1.0), R=["zz"], W=["zz"])
        kb.op("act", lambda e: e.activation(out=zz[:], in_=zz[:], func=AF.Ln, scale=1.0, bias=1.0), R=["zz"], W=["zz"])
        kb.op("dve", lambda e: e.tensor_scalar(out=lfo[:, tb, :], in0=zz[:], scalar1=-1.0, scalar2=None, op0=ALU.mult), R=["zz"], W=["lfo"])

    tokmajor(16, 32, h_f)
    kb.dma([("sp", lf_o.rearrange("(m p) h -> p m h", p=128), lfo[:])], R=["lfo"], W=["lf_o"], sem="lfo")
    if cc is not None:
        kb.collective("AllGather", cc["lf1"], cc["lf1g"], cc["groups"], after=["lf_o"])
    kb.finish(["qT_o", "kT_o", "v_o", "sg_o", "lf_o"])
    print("B1 inst", kb.n_inst, "waits", kb.n_wait)
    kb.pop()
    if own:
        kb.close()
    return nc


def phase_b2a(nc, S, kb=None, io=None):
    NB = S // 128
    NBc = S // 512
    T = NBc * 128
    qT = dram_in(nc, "qT1", [32, 128, T], BF16, io)
    kTf = dram_in(nc, "kT1g", [4 * 4096, T], BF16, io)
    vf = dram_in(nc, "v1g", [S, 4096], BF16, io)
    nck = max(1, (4096 * T * 2) >> 20)
    rc = 4096 // nck
    lff = dram_in(nc, "lfg", [4 * T, 32], F32, io)
    sgate = dram_in(nc, "sgate1", [T, 4096], F32, io)
    tri4_d = dram_in(nc, "tri4", [128, 4, 128], F32, io)
    ident_d = dram_in(nc, "ident", [128, 128], F32, io)
    sel_d = dram_in(nc, "sel", [128, 4], F32, io)
    selh_d = dram_in(nc, "selh", [32, 32, 128], F32, io)
    ogT_o = dram_out(nc, "ogT", [4096, T], BF16, io)

    own = kb is None
    kb = KB(nc) if own else kb
    kb.push()
    identf = kb.sbuf("identf", [128, 128], F32)
    identb = kb.sbuf("identb", [128, 128], BF16)
    tri4 = kb.sbuf("tri4s", [128, 4, 128], F32)
    tri4b = kb.sbuf("tri4b", [128, 4, 128], BF16)
    sel = kb.sbuf("sels", [128, 4], F32)
    lfs = kb.sbuf("lfs", [128, NB, 32], F32)
    lfT = kb.sbuf("lfT", [32, S], F32)
    onesT = kb.sbuf("onesT", [32, S], F32)
    cT = kb.sbuf("cT", [32, S], F32)
    ocT = kb.sbuf("ocT", [32, NBc, 128], F32)
    ocR = kb.sbuf("ocR", [32, NBc, 128], F32)
    r1 = kb.sbuf("r1", [32, S], F32)
    nhi = kb.sbuf("nhi", [32, S], BF16)
    nmid = kb.sbuf("nmid", [32, S], BF16)
    nlo = kb.sbuf("nlo", [32, S], BF16)
    ohi = kb.sbuf("ohi", [32, T], BF16)
    omid = kb.sbuf("omid", [32, T], BF16)
    olo = kb.sbuf("olo", [32, T], BF16)
    augl = [kb.sbuf(f"augl{i}", [128, S], BF16) for i in range(2)]
    augr = [kb.sbuf(f"augr{i}", [128, T], BF16) for i in range(2)]
    kts = [kb.sbuf(f"kts{i}", [128, S], BF16) for i in range(2)]
    vs = [kb.sbuf(f"vs{i}", [128, NB, 130], BF16) for i in range(2)]
    qhr = [kb.sbuf(f"qhr{i}", [128, T], BF16) for i in range(2)]
    sgh = [kb.sbuf(f"sgh{i}", [128, NBc, 128], F32) for i in range(2)]
    NP = 4
    pT = [kb.sbuf(f"pT{i}", [128, 512], BF16) for i in range(NP)]
    ogs = [kb.sbuf(f"ogs{i}", [128, 128], BF16) for i in range(2)]
    ogTs = [kb.sbuf(f"ogTs{i}", [128, T], BF16) for i in range(2)]
    rs = kb.sbuf("rs", [128, 2], F32)
    pp = kb.psum("pp", [128, 8, 512], F32)

    groups = [list(range(min(g0 + 4, NBc) - 1, g0 - 1, -1)) for g0 in range(0, NBc, 4)]
    slot = {}
    for gi, grp in enumerate(groups):
        for idx, m in enumerate(grp):
            slot[m] = gi * 4 + idx

    kb.dma([("sp", identf[:], ident_d), ("sp", tri4[:], tri4_d), ("sp", sel[:], sel_d)] +
           [("sp", lfs[:].rearrange("p (m j) h -> p m j h", j=4)[:, :, j, :],
             lff[j * T:(j + 1) * T, :].rearrange("(m p) h -> p m h", p=128)) for j in range(4)], W=["consts"], sem="consts")
    kb.op("dve", lambda e: e.tensor_copy(out=identb[:], in_=identf[:]), R=["consts"], W=["identb"])
    kb.op("dve", lambda e: e.tensor_copy(out=tri4b[:], in_=tri4[:]), R=["consts"], W=["tri4b"])
    kb.op("dve", lambda e: e.memset(onesT[:], 1.0), W=["onesT"])
    for i in range(2):
        kb.op("pool", lambda e: e.memset(vs[i][:], 1.0), W=[f"vs{i}"])
        kb.op("pool", lambda e: e.memset(augl[i][:], 0.0), W=[f"augl{i}"])
        kb.op("pool", lambda e: e.memset(augr[i][:], 0.0), W=[f"augr{i}"])
        kb.op("pool", lambda e: e.memset(augl[i][32:35, :], 1.0), W=[f"augl{i}"])
        kb.op("pool", lambda e: e.memset(augr[i][0:3, :], 1.0), W=[f"augr{i}"])
    for k4 in range(0, NB, 4):
        for a in range(4):
            kb.op("pe", lambda e: e.transpose(pp[0:32, 0, a * 128:(a + 1) * 128], lfs[:, k4 + a, :], identf[:]), R=["consts"], W=["pp0"])
        kb.op("act", lambda e: e.activation(out=lfT[:, k4 * 128:(k4 + 4) * 128], in_=pp[0:32, 0, :], func=AF.Copy), R=["pp0"], W=["lfT"])
    kb.op("dve", lambda e: e.tensor_tensor_scan(out=cT[:], data0=onesT[:], data1=lfT[:], initial=0.0, op0=ALU.mult, op1=ALU.add),
          R=["onesT", "lfT"], W=["cT"])
    cT4 = cT[:].rearrange("p (m r q) -> p m r q", r=4, q=128)
    kb.op("dve", lambda e: e.tensor_scalar(out=ocT[:], in0=cT4[:, :, 0, :], scalar1=sel[0:32, 0:1], scalar2=None, op0=ALU.mult), R=["cT", "consts"], W=["ocT"])
    for r in range(1, 4):
        kb.op("dve", lambda e: e.scalar_tensor_tensor(out=ocT[:], in0=cT4[:, :, r, :], scalar=sel[0:32, r:r + 1], in1=ocT[:], op0=ALU.mult, op1=ALU.add),
              R=["cT", "consts"], W=["ocT"])
    for m in range(NBc):
        kb.op("dve", lambda e: e.tensor_copy(out=ocR[:, slot[m], :], in_=ocT[:, m, :]), R=["ocT"], W=["ocR"])

    def split3(src, hi, mid, lo, n, neg, key):
        sg = -1.0 if neg else 1.0
        kb.op("dve", lambda e: e.tensor_scalar(out=hi, in0=src, scalar1=sg, scalar2=None, op0=ALU.mult), R=[key], W=[key + "hi"])
        kb.op("dve", lambda e: e.scalar_tensor_tensor(out=r1[:, :n], in0=src, scalar=sg, in1=hi, op0=ALU.mult, op1=ALU.subtract),
              R=[key, key + "hi"], W=["r1"])
        kb.op("dve", lambda e: e.tensor_copy(out=mid, in_=r1[:, :n]), R=["r1"], W=[key + "mid"])
        kb.op("dve", lambda e: e.tensor_tensor(out=r1[:, :n], in0=r1[:, :n], in1=mid, op=ALU.subtract), R=[key + "mid"], W=["r1"])
        kb.op("dve", lambda e: e.tensor_copy(out=lo, in_=r1[:, :n]), R=["r1"], W=[key + "lo"])

    split3(cT[:], nhi[:], nmid[:], nlo[:], S, True, "cT")
    split3(ocR[:].rearrange("p a b -> p (a b)"), ohi[:], omid[:], olo[:], T, False, "ocR")
    CK = ["cThi", "cTmid", "cTlo", "ocRhi", "ocRmid", "ocRlo"]

    st = {"e": 0}
    bitems = []
    for h in range(32):
        first = True
        for gi, grp in enumerate(groups):
            for kk in range(4 * grp[0] + 4):
                bitems.append((h, gi, kk, first))
                first = False
    last_of_head = {}
    for t, it in enumerate(bitems):
        last_of_head[it[0]] = t

    def b_s1(it, t):
        h, gi, kk, first = it
        i = h % 2
        grp = groups[gi]
        g0 = gi * 4 * 128
        if first:
            kb.dma([("sp", sgh[i][:], sgate[:, h * 128:(h + 1) * 128].rearrange("(m p) d -> p m d", p=128))] +
                   [("sp", qhr[i][:, slot[m] * 128:(slot[m] + 1) * 128], qT[h][:, m * 128:(m + 1) * 128]) for m in range(NBc)] +
                   [("sp", kts[i][:].rearrange("d (m j p) -> d m j p", j=4, p=128)[:, :, j, :],
                     kTf[((h * 128) // rc) * 4 * rc + j * rc + (h * 128) % rc:((h * 128) // rc) * 4 * rc + j * rc + (h * 128) % rc + 128, :]
                     .rearrange("d (m p) -> d m p", p=128)) for j in range(4)] +
                   [("sp", vs[i][:, :, 0:128], vf[:, h * 128:(h + 1) * 128].rearrange("(kb p) d -> p kb d", p=128))] +
                   [("sp", augl[i][a:a + 1, :], t_[h:h + 1, :]) for a, t_ in enumerate((nhi, nmid, nlo))] +
                   [("sp", augr[i][32 + a:33 + a, :], t_[h:h + 1, :]) for a, t_ in enumerate((ohi, omid, olo))],
                   R=CK, W=[f"kts{i}", f"qhr{i}", f"sgh{i}", f"vs{i}", f"augl{i}", f"augr{i}"], sem=f"hl{i}")
        act = [m for m in grp if kk < 4 * m + 4]
        na = len(act)
        N = na * 128
        b = t % 4
        ml = act[-1]
        tri = kk >= 4 * ml
        kb.op("pe", lambda e: e.matmul(pp[:, b, :N], lhsT=kts[i][:, kk * 128:(kk + 1) * 128], rhs=qhr[i][:, g0:g0 + N],
                                       start=True, stop=False), R=[f"kts{i}", f"qhr{i}"], W=[f"pp{b}"])
        kb.op("pe", lambda e: e.matmul(pp[:, b, :N], lhsT=augl[i][:, kk * 128:(kk + 1) * 128], rhs=augr[i][:, g0:g0 + N],
                                       start=False, stop=(not tri)), R=[f"augl{i}", f"augr{i}"], W=[f"pp{b}"])
        if tri:
            kb.op("pe", lambda e: e.matmul(pp[:, b, (na - 1) * 128:na * 128], lhsT=identb[:], rhs=tri4b[:, kk - 4 * ml, :],
                                           start=False, stop=True), R=["identb", "tri4b"], W=[f"pp{b}"])

    def b_s2(it, t):
        h, gi, kk, first = it
        i = h % 2
        grp = groups[gi]
        act = [m for m in grp if kk < 4 * m + 4]
        na = len(act)
        N = na * 128
        b = t % 4
        sp_ = t % NP
        ml = act[-1]
        kb.op("act", lambda e: e.activation(out=pT[sp_][:, :N], in_=pp[:, b, :N], func=AF.Exp), R=[f"pp{b}"], W=[f"pT{sp_}"])
        for idx, m in enumerate(act):
            ob = 4 + idx
            kb.op("pe", lambda e: e.matmul(pp[:, ob, 0:129], lhsT=pT[sp_][:, idx * 128:(idx + 1) * 128], rhs=vs[i][:, kk, 0:129],
                                           start=(kk == 0), stop=(kk == 4 * m + 3)), R=[f"pT{sp_}", f"vs{i}"], W=[f"pp{ob}"])
        if kk == 4 * ml + 3:
            ob = 4 + (na - 1)
            e2 = st["e"] % 2
            st["e"] += 1
            kb.op("dve", lambda e: e.reciprocal(out=rs[:, e2:e2 + 1], in_=pp[:, ob, 128:129]), R=[f"pp{ob}"], W=[f"rs{e2}"])
            kb.op("dve", lambda e: e.scalar_tensor_tensor(out=ogs[e2][:], in0=pp[:, ob, 0:128], scalar=rs[:, e2:e2 + 1], in1=sgh[i][:, ml, :],
                                                          op0=ALU.mult, op1=ALU.mult), R=[f"pp{ob}", f"rs{e2}", f"sgh{i}"], W=[f"ogs{e2}"])
            pt = pp[:, ob, 256:320].bitcast(BF16)
            kb.op("pe", lambda e: e.transpose(pt, ogs[e2][:], identb[:]), R=[f"ogs{e2}", "identb"], W=[f"pp{ob}"])
            kb.op("act", lambda e: e.activation(out=ogTs[i][:, ml * 128:(ml + 1) * 128], in_=pt, func=AF.Copy), R=[f"pp{ob}"], W=[f"ogTs{i}"])
        if t == last_of_head[h]:
            kb.dma([("sp", ogT_o[h * 128:(h + 1) * 128, :], ogTs[i][:])], R=[f"ogTs{i}"], W=["ogT_o"], sem=f"ogTs{i}", indep=True)

    emit_skewed(bitems, b_s1, b_s2, 3)
    kb.finish(["ogT_o"])
    print("B2a inst", kb.n_inst, "waits", kb.n_wait)
    kb.pop()
    if own:
        kb.close()
    return nc


GROUPS = [[0, 1, 2, 3], [4, 5, 6, 7]]


def build_fused(nc, S):
    T = S // 4
    kb = KB(nc)

    def ext(name, shape, dt=F32):
        return nc.dram_tensor(name, list(shape), dt, kind="ExternalInput").ap()

    def scr(name, shape, dt):
        return nc.dram_tensor(name, list(shape), dt).ap()

    E = dict(xT=ext("xT", [D, T]), x=ext("x", [T, D]), a_w_in=ext("a_w_in", [25, 128, 8192]), a_w_uq=ext("a_w_uq", [12, 128, 8192]),
             a_w_out=ext("a_w_out", [16, 128, 8192]), b_w_in=ext("b_w_in", [32, 128, 8192]), b_w_vg=ext("b_w_vg", [17, 128, 16384]), b_w_out=ext("b_w_out", [16, 128, 8192]),
             qg=ext("qg", [128, 8]), kgb=ext("kgb", [128, 256]), fb=ext("fb", [128, 32]),
             lng0=ext("lng0", [128, D]), lnb0=ext("lnb0", [128, D]), lng1=ext("lng1", [128, D]), lnb1=ext("lnb1", [128, D]),
             cosF=ext("cosF", [128, T]), sinF=ext("sinF", [128, T]), cosT=ext("cosT", [T, 64]), sinT=ext("sinT", [T, 64]),
             perm=ext("perm", [128, 128]), ident=ext("ident", [128, 128]), cm4=ext("cm4", [128, 4, 128]),
             tri4=ext("tri4", [128, 4, 128]), sel=ext("sel", [128, 4]), selh=ext("selh", [32, 32, 128]))
    y_out = nc.dram_tensor("y", [T, D], F32, kind="ExternalOutput").ap()
    qT0 = scr("s_qT0", [32, 128, T], BF16)
    qiT0 = scr("s_qiT0", [64, 128, T], BF16)
    kT0 = scr("s_kT0", [512, T], BF16)
    v0 = scr("s_v0", [T, 512], BF16)
    kiT0 = scr("s_kiT0", [128, T], BF16)
    widx0 = scr("s_widx0", [T, 64], F32)
    sg0 = scr("s_sg0", [T, D], F32)
    kT0g = scr("s_kT0g", [4 * 512, T], BF16)
    v0g = scr("s_v0g", [4 * T, 512], BF16)
    kiT0g = scr("s_kiT0g", [4 * 128, T], BF16)
    ogT0 = scr("s_ogT0", [D, T], BF16)
    x1 = scr("s_x1", [T, D], F32)
    x1T = scr("s_x1T", [D, T], BF16)
    qT1 = scr("s_qT1", [32, 128, T], BF16)
    kT1 = scr("s_kT1", [D, T], BF16)
    v1 = scr("s_v1", [T, D], BF16)
    sg1 = scr("s_sg1", [T, D], F32)
    lf1 = scr("s_lf1", [T, 32], F32)
    kT1g = scr("s_kT1g", [4 * D, T], BF16)
    v1g = scr("s_v1g", [S, D], BF16)
    lf1g = scr("s_lf1g", [4 * T, 32], F32)
    ogT1 = scr("s_ogT1", [D, T], BF16)

    phase_a1(nc, S, kb=kb, io=dict(xT=E["xT"], a_w_in=E["a_w_in"], a_w_uq=E["a_w_uq"], qg=E["qg"], kgb=E["kgb"], cosF=E["cosF"],
                                   sinF=E["sinF"], cosT=E["cosT"], sinT=E["sinT"], perm=E["perm"], ident=E["ident"],
                                   qT=qT0, qiT=qiT0, kT=kT0.rearrange("(g d) t -> g d t", g=4), v=v0, kiT=kiT0, widx=widx0, sgate=sg0))
    kb.collective("AllGather", kT0, kT0g, GROUPS)
    kb.collective("AllGather", v0, v0g, GROUPS)
    kb.collective("AllGather", kiT0, kiT0g, GROUPS)
    kb.barrier()
    phase_a2a(nc, S, kb=kb, io=dict(qT=qT0, qiT=qiT0, widx=widx0, sgate=sg0, kTg=kT0g, vg=v0g, kiTg=kiT0g, ident=E["ident"],
                                    cm4=E["cm4"], ogT=ogT0))
    phase_out(nc, S, False, kb=kb, io=dict(ogT=ogT0, w_out=E["a_w_out"], x=E["x"], lng=E["lng0"], lnb=E["lnb0"], ident=E["ident"],
                                           y=x1, yT=x1T))
    phase_b1(nc, S, kb=kb, io=dict(x1T=x1T, b_w_in=E["b_w_in"], b_w_vg=E["b_w_vg"], fb=E["fb"], qT1=qT1, kT1=kT1.rearrange("(h d) t -> h d t", h=32),
                                   v1=v1, sgate1=sg1, lf=lf1,
                                   cc=dict(kT1=kT1, kT1g=kT1g, v1=v1, v1g=v1g, lf1=lf1, lf1g=lf1g, groups=GROUPS)))
    phase_b2a(nc, S, kb=kb, io=dict(qT1=qT1, kT1g=kT1g, v1g=v1g, lfg=lf1g, sgate1=sg1, tri4=E["tri4"], ident=E["ident"],
                                    sel=E["sel"], selh=E["selh"], ogT=ogT1))
    phase_out(nc, S, True, kb=kb, io=dict(ogT=ogT1, w_out=E["b_w_out"], x=x1, lng=E["lng1"], lnb=E["lnb1"], ident=E["ident"], y=y_out))
    print("FUSED inst", kb.n_inst, "waits", kb.n_wait)
    kb.close()
    return nc


def fused_inputs(inp, S, b, j):
    pos = own_pos(S, j)
    d = a1_inputs(inp, S, b, j)
    cm4, tri4 = causal_tables(j)
    sel = np.zeros((128, 4), np.float32)
    sel[:, j] = 1.0
    selh = np.zeros((32, 32, 128), np.float32)
    for h in range(32):
        selh[h, h, :] = 1.0
    bc = lambda v: np.ascontiguousarray(np.broadcast_to(v, (128, v.shape[-1])))
    d.update(x=np.ascontiguousarray(inp["x"][b, pos, :]), a_w_out=inp["a_w_out_t"], b_w_in=inp["b_w_in_t"], b_w_vg=inp["b_w_vg_t"], b_w_out=inp["b_w_out_t"],
             fb=bc(inp["b_forget_bias"][0]), lng0=bc(inp["ln_g"][0]), lnb0=bc(inp["ln_b"][0]), lng1=bc(inp["ln_g"][1]), lnb1=bc(inp["ln_b"][1]),
             cm4=cm4, tri4=tri4, sel=sel, selh=selh)
    return d


def tile_weights(inp):
    inp = dict(inp)
    inp["a_w_in_t"] = tile_w(inp["a_w_in"][0], A_GROUPS, 32)
    inp["a_w_uq_t"] = tile_w(inp["a_w_uq"][0], UQ_GROUPS, 8)
    inp["a_w_out_t"] = tile_w(inp["a_w_out"][0], O_GROUPS, 32)
    inp["b_w_in_t"] = tile_w(inp["b_w_in"][0], B_GROUPS, 32)
    inp["b_w_vg_t"] = tile_w(inp["b_w_in"][0], BV_GROUPS, 32, row=16384)
    inp["b_w_out_t"] = tile_w(inp["b_w_out"][0], O_GROUPS, 32)
    return inp


_S = 4096


def kernel(x, a_w_in, a_q_norm_g, a_w_uq, a_kidx_norm_g, a_kidx_norm_b, a_w_out,
           b_w_in, b_forget_bias, b_w_out, ln_g, ln_b):
    S = _S
    f = lambda a: np.asarray(a, np.float32)
    inp = dict(x=f(x), a_w_in=f(a_w_in), a_q_norm_g=f(a_q_norm_g), a_w_uq=f(a_w_uq), a_kidx_norm_g=f(a_kidx_norm_g),
               a_kidx_norm_b=f(a_kidx_norm_b), a_w_out=f(a_w_out), b_w_in=f(b_w_in), b_forget_bias=f(b_forget_bias),
               b_w_out=f(b_w_out), ln_g=f(ln_g), ln_b=f(ln_b))
    inp = tile_weights(inp)
    cores = [(b, j) for b in range(2) for j in range(4)]
    nc = bass.Bass("TRN2", target_bir_lowering=False)
    build_fused(nc, S)
    ims = [fused_inputs(inp, S, b, j) for (b, j) in cores]
    res = run_bass_kernel_spmd(nc, ims, core_ids=list(range(8))).results
    out = np.zeros((2, S, 4096), np.float32)
    for ci, (b, j) in enumerate(cores):
        out[b, own_pos(S, j), :] = res[ci]["y"]
    return out
```

```python
from concourse.bass_utils import run_bass_kernel_spmd
from contextlib import ExitStack
import numpy as np
import concourse.bass as bass
import concourse.mybir as mybir

F32 = mybir.dt.float32
BF16 = mybir.dt.bfloat16
AF = mybir.ActivationFunctionType
ALU = mybir.AluOpType
AX = mybir.AxisListType


class _St:
    __slots__ = ("w", "r", "wl")

    def __init__(self):
        self.w = None
        self.r = []
        self.wl = []


class KB:
    def __init__(self, nc, same_engine_sync=("act", "dve", "pool")):
        self.nc = nc
        self.es = ExitStack()
        self.engs = {"pe": nc.tensor, "dve": nc.vector, "act": nc.scalar,
                     "pool": nc.gpsimd, "sp": nc.sync}
        self.sem = {}
        self.cnt = {}
        self.waited = {k: {} for k in self.engs}
        for k in self.engs:
            self.sem[k] = self.es.enter_context(nc.semaphore(f"s_{k}"))
            self.cnt[k] = 0
        self.same = set(same_engine_sync)
        self.st = {}
        self.dsems = {}
        self.dcnt = {}
        self.n_inst = 0
        self.n_wait = 0
        self.scopes = []
        self.ncc = 0

    def _es(self):
        return self.scopes[-1] if self.scopes else self.es

    def sbuf(self, name, shape, dtype):
        self.nalloc = getattr(self, "nalloc", 0) + 1
        return self._es().enter_context(self.nc.sbuf_tensor(f"{name}_{self.nalloc}", list(shape), dtype))

    def psum(self, name, shape, dtype=F32):
        self.nalloc = getattr(self, "nalloc", 0) + 1
        return self._es().enter_context(self.nc.psum_tensor(f"{name}_{self.nalloc}", list(shape), dtype))

    def push(self):
        self.scopes.append(ExitStack())

    def pop(self):
        self.barrier()
        self.scopes.pop().close()
        self.st = {}

    def collective(self, kind, src, dst, groups, after=()):
        name = f"cc{self.ncc}"
        self.ncc += 1
        self.dsem(name)
        self._wait("pool", self._deps(after, []))
        inst = self.nc.gpsimd.collective_compute(kind, ALU.bypass, replica_groups=groups, ins=[src.opt()], outs=[dst.opt()])
        inst.then_inc(self.dsems[name], 1)
        self.dcnt[name] += 1
        self.n_inst += 1

    def dsem(self, name):
        if name not in self.dsems:
            self.dsems[name] = self.es.enter_context(self.nc.semaphore(f"d_{name}"))
            self.dcnt[name] = 0
        return name

    def _deps(self, R, W):
        deps = []
        for k in R:
            s = self.st.get(k)
            if s is not None and s.w is not None:
                deps.append(s.w)
            if s is not None:
                deps.extend(s.wl)
        for k in W:
            s = self.st.get(k)
            if s is not None:
                if s.w is not None:
                    deps.append(s.w)
                deps.extend(s.r)
        return deps

    def _wait(self, eng, deps):
        need = {}
        wt = self.waited[eng]
        for (sname, semh, val) in deps:
            if sname == eng and eng not in self.same:
                continue
            if wt.get(sname, 0) >= val:
                continue
            if need.get(sname, (None, 0))[1] < val:
                need[sname] = (semh, val)
        for sname, (semh, val) in need.items():
            self.engs[eng].wait_ge(semh, val)
            wt[sname] = val
            self.n_wait += 1

    def _commit(self, ev, R, W):
        for k in R:
            self.st.setdefault(k, _St()).r.append(ev)
        for k in W:
            s = self.st.setdefault(k, _St())
            s.w = ev
            s.r = []

    def op(self, eng, fn, R=(), W=()):
        W = list(W) + [k for k in R if k.startswith("pp")]
        R = [k for k in R if not k.startswith("pp")]
        self._wait(eng, self._deps(R, W))
        inst = fn(self.engs[eng])
        self.cnt[eng] += 1
        inst.then_inc(self.sem[eng], 1)
        self.n_inst += 1
        ev = (eng, self.sem[eng], self.cnt[eng])
        self._commit(ev, R, W)
        return inst

    def dma(self, parts, R=(), W=(), sem=None, indep=False):
        assert sem is not None
        self.dsem(sem)
        deps = self._deps(R, () if indep else W)
        if self.dcnt[sem] > 0:
            deps.append(("D" + sem, self.dsems[sem], self.dcnt[sem]))
        for q in dict.fromkeys(p[0] for p in parts):
            self._wait(q, deps)
        for (q, o, i) in parts:
            self.engs[q].dma_start(out=o, in_=i).then_inc(self.dsems[sem], 16)
            self.dcnt[sem] += 16
            self.n_inst += 1
        ev = ("D" + sem, self.dsems[sem], self.dcnt[sem])
        if indep:
            self._commit(ev, R, ())
            for k in W:
                self.st.setdefault(k, _St()).wl.append(ev)
        else:
            self._commit(ev, R, W)

    def finish(self, keys):
        self._wait("sp", self._deps(keys, ()))

    def barrier(self):
        deps = [(k, self.sem[k], self.cnt[k]) for k in self.engs if self.cnt[k] > 0]
        deps += [("D" + s, self.dsems[s], self.dcnt[s]) for s in self.dsems if self.dcnt[s] > 0]
        for e in self.engs:
            same = self.same
            self.same = set(self.engs)
            self._wait(e, deps)
            self.same = same

    def close(self):
        self.es.close()


import ml_dtypes

NPBF = ml_dtypes.bfloat16
D = 4096
A_IN = 6336
B_IN = 16416
SCALE = 128 ** -0.5
WSC = 64 ** -0.5 * 128 ** -0.5
NEG = -30000.0


def own_pos(S, j):
    NBc = S // 512
    return np.concatenate([np.arange(128) + (j + 4 * m) * 128 for m in range(NBc)])


def rope_tables(pos):
    inv = (10000.0 ** (-np.arange(64, dtype=np.float32) / 64)).astype(np.float32)
    ang = pos.astype(np.float32)[:, None] * inv[None, :]
    cos, sin = np.cos(ang).astype(np.float32), np.sin(ang).astype(np.float32)
    cosF = np.concatenate([cos.T, cos.T], 0)
    sinF = np.concatenate([-sin.T, sin.T], 0)
    return dict(cosF=np.ascontiguousarray(cosF), sinF=np.ascontiguousarray(sinF),
                cosT=np.ascontiguousarray(cos), sinT=np.ascontiguousarray(sin))


def tile_w(W, groups, KC, row=8192):
    out = np.zeros((len(groups), 128, row), np.float32)
    for g, (c0, width) in enumerate(groups):
        blk = W[:, c0:c0 + width].reshape(KC, 128, width).transpose(1, 0, 2).reshape(128, KC * width)
        out[g, :, :KC * width] = blk
    return out


A_GROUPS = [(g * 256, 256) for g in range(8)] + [(2048, 192)] + [(2240 + g * 256, 256) for g in range(16)]
UQ_GROUPS = [(g * 1024, 1024) for g in range(12)]
O_GROUPS = [(g * 256, 256) for g in range(16)]
B_GROUPS = [(g * 256, 256) for g in range(32)]
BV_GROUPS = [(8192 + g * 512, 512) for g in range(16)] + [(16384, 32)]


def consts():
    perm = np.zeros((128, 128), np.float32)
    for d in range(128):
        perm[(d + 64) % 128, d] = 1.0
    ident = np.eye(128, dtype=np.float32)
    return dict(perm=perm, ident=ident)


def emit_skewed(items, stage1, stage2, D):
    n = len(items)
    for t in range(n + D):
        if t < n:
            stage1(items[t], t)
        if t - D >= 0:
            stage2(items[t - D], t - D)


def dram_in(nc, name, shape, dt=F32, io=None):
    if io is not None and name in io:
        assert list(io[name].shape) == list(shape), (name, io[name].shape, shape)
        return io[name]
    return nc.dram_tensor(name, list(shape), dt, kind="ExternalInput").ap()


def dram_out(nc, name, shape, dt=F32, io=None):
    if io is not None and name in io:
        assert list(io[name].shape) == list(shape), (name, io[name].shape, shape)
        return io[name]
    return nc.dram_tensor(name, list(shape), dt, kind="ExternalOutput").ap()


def phase_a1(nc, S, stages=('ii', 'v', 'ki', 'gate', 'iv'), kb=None, io=None):
    NBc = S // 512
    T = NBc * 128
    TW = min(T, 512)
    TH = T // TW
    xT = dram_in(nc, "xT", [D, T], F32, io)
    w_in = dram_in(nc, "a_w_in", [25, 128, 8192], F32, io)
    w_uq = dram_in(nc, "a_w_uq", [12, 128, 8192], F32, io)
    qg = dram_in(nc, "qg", [128, 8], F32, io)
    kgb = dram_in(nc, "kgb", [128, 256], F32, io)
    cosF_d = dram_in(nc, "cosF", [128, T], F32, io)
    sinF_d = dram_in(nc, "sinF", [128, T], F32, io)
    cosT_d = dram_in(nc, "cosT", [T, 64], F32, io)
    sinT_d = dram_in(nc, "sinT", [T, 64], F32, io)
    perm_d = dram_in(nc, "perm", [128, 128], F32, io)
    ident_d = dram_in(nc, "ident", [128, 128], F32, io)
    qT_o = dram_out(nc, "qT", [32, 128, T], BF16, io)
    qiT_o = dram_out(nc, "qiT", [64, 128, T], BF16, io)
    kT_o = dram_out(nc, "kT", [4, 128, T], BF16, io)
    v_o = dram_out(nc, "v", [T, 512], BF16, io)
    kiT_o = dram_out(nc, "kiT", [128, T], BF16, io)
    widx_o = dram_out(nc, "widx", [T, 64], F32, io)
    sg_o = dram_out(nc, "sgate", [T, 4096], F32, io)

    own = kb is None
    kb = KB(nc) if own else kb
    kb.push()
    xb = kb.sbuf("xb", [128, 32, T], BF16)
    NWS = 2
    wslot = [kb.sbuf(f"wslot{i}", [128, 8192], BF16) for i in range(NWS)]
    cqg = kb.sbuf("cqg", [128, 8, T], BF16)
    cosF = kb.sbuf("cosFs", [128, T], F32)
    sinF = kb.sbuf("sinFs", [128, T], F32)
    cosT = kb.sbuf("cosTs", [128, NBc, 64], F32)
    sinT = kb.sbuf("sinTs", [128, NBc, 64], F32)
    crq = kb.sbuf("crq", [128, T], F32)
    srq = kb.sbuf("srq", [128, T], F32)
    cri = kb.sbuf("cri", [128, T], F32)
    sri = kb.sbuf("sri", [128, T], F32)
    rstd = kb.sbuf("rstd", [128, T], F32)
    qgs = kb.sbuf("qgs", [128, 8], F32)
    kgbs = kb.sbuf("kgbs", [128, 256], F32)
    permf = kb.sbuf("permf", [128, 128], F32)
    permb = kb.sbuf("permb", [128, 128], BF16)
    identf = kb.sbuf("identf", [128, 128], F32)
    identb = kb.sbuf("identb", [128, 128], BF16)
    onesf = kb.sbuf("onesf", [128, 128], F32)
    NR = 2
    sq = [kb.sbuf(f"sq{i}", [128, 512], F32) for i in range(NR)]
    xbr = [kb.sbuf(f"xbr{i}", [128, 512], BF16) for i in range(NR)]
    ra = [kb.sbuf(f"ra{i}", [128, 512], F32) for i in range(NR)]
    rb = [kb.sbuf(f"rb{i}", [128, 512], F32) for i in range(NR)]
    ro = [kb.sbuf(f"ro{i}", [128, 512], BF16) for i in range(NR)]
    go = [kb.sbuf(f"go{i}", [128, 256], F32) for i in range(NR)]
    vo = [kb.sbuf(f"vo{i}", [128, 256], BF16) for i in range(NR)]
    kis = kb.sbuf("kis", [128, 192], F32)
    kin = kb.sbuf("kin", [128, 128], F32)
    kir = kb.sbuf("kir", [128, 128], F32)
    kit = kb.sbuf("kit", [128, 64], F32)
    kib = kb.sbuf("kib", [128, 128], BF16)
    kiTs = kb.sbuf("kiTs", [128, T], BF16)
    wio = kb.sbuf("wio", [128, NBc, 64], F32)
    bst = kb.sbuf("bst", [128, 6], F32)
    mv = kb.sbuf("mv", [128, 4], F32)
    pp = kb.psum("pp", [128, 8, 512], F32)
    ptb = kb.psum("ptb", [128, 128], BF16) if False else None

    xTr = xT.rearrange("(k p) t -> p k t", p=128)
    for kq in range(4):
        kb.dma([("pool", xb[:, kq * 8:(kq + 1) * 8, :], xTr[:, kq * 8:(kq + 1) * 8, :])], W=[f"xb{kq}"], sem=f"xb{kq}")
    XB = [f"xb{kq}" for kq in range(4)]
    kb.dma([("sp", cosF[:], cosF_d), ("sp", sinF[:], sinF_d), ("sp", qgs[:], qg), ("sp", kgbs[:], kgb),
            ("sp", permf[:], perm_d), ("sp", identf[:], ident_d),
            ("sp", cosT[:], cosT_d.rearrange("(m p) c -> p m c", p=128)),
            ("sp", sinT[:], sinT_d.rearrange("(m p) c -> p m c", p=128))], W=["consts"], sem="consts")
    kb.op("dve", lambda e: e.tensor_copy(out=permb[:], in_=permf[:]), R=["consts"], W=["permb"])
    kb.op("dve", lambda e: e.tensor_copy(out=identb[:], in_=identf[:]), R=["consts"], W=["identb"])
    kb.op("dve", lambda e: e.memset(onesf[:], 1.0), W=["onesf"])

    wstate = {"n": 0}

    def load_w(wt, g, kchunks, ncols):
        i = wstate["n"] % NWS
        wstate["n"] += 1
        n = kchunks * ncols
        view = wslot[i][:, 0:n].rearrange("p (k c) -> p k c", k=kchunks)
        step = min(n, 2048)
        parts = [("pool", wslot[i][:, a:a + step], wt[g][:, a:a + step]) for a in range(0, n, step)]
        kb.dma(parts, W=[f"w{i}"], sem=f"w{i}")
        return view, f"w{i}"

    rr = {"n": 0}

    def rope_tile(src_ps, pskey, N, cr, sr, crkeys, dst_dram, dstkey):
        i = rr["n"] % NR
        rr["n"] += 1
        pb = 4 + (rr["n"] % 2)
        kb.op("act", lambda e: e.activation(out=xbr[i][:, :N], in_=src_ps, func=AF.Copy), R=[pskey], W=[f"xbr{i}"])
        kb.op("pe", lambda e: e.matmul(pp[:, pb, :N], lhsT=permb[:], rhs=xbr[i][:, :N], start=True, stop=True),
              R=[f"xbr{i}", "permb"], W=[f"pp{pb}"])
        kb.op("dve", lambda e: e.tensor_tensor(out=ra[i][:, :N], in0=src_ps, in1=cr, op=ALU.mult),
              R=[pskey] + crkeys, W=[f"ra{i}"])
        kb.op("dve", lambda e: e.tensor_tensor(out=rb[i][:, :N], in0=pp[:, pb, :N], in1=sr, op=ALU.mult),
              R=[f"pp{pb}"] + crkeys, W=[f"rb{i}"])
        kb.op("dve", lambda e: e.tensor_tensor(out=ro[i][:, :N], in0=ra[i][:, :N], in1=rb[i][:, :N], op=ALU.add),
              R=[f"ra{i}", f"rb{i}"], W=[f"ro{i}"])
        kb.dma([("sp", dst_dram, ro[i][:, :N])], R=[f"ro{i}"], W=[dstkey], sem=f"ro{i}")

    mm = {"n": 0}

    def next_bank():
        b = mm["n"] % 3
        mm["n"] += 1
        return b

    for g4 in range(4):
        wv, wk = load_w(w_in, g4, 32, 256)
        for fl in range(2):
            fc = g4 * 2 + fl
            for th in range(TH):
                b = next_bank()
                for k in range(32):
                    kb.op("pe", lambda e: e.matmul(pp[:, b, :TW], lhsT=wv[:, k, fl * 128:(fl + 1) * 128],
                                                   rhs=xb[:, k, th * TW:(th + 1) * TW], start=(k == 0), stop=(k == 31)),
                          R=[wk] + XB, W=[f"pp{b}"])
                kb.op("dve", lambda e: e.tensor_scalar(out=cqg[:, fc, th * TW:(th + 1) * TW], in0=pp[:, b, :TW],
                                                       scalar1=qgs[:, fc:fc + 1], scalar2=None, op0=ALU.mult),
                      R=[f"pp{b}", "consts"], W=[f"cqg{th}"])
                i = (fc * TH + th) % NR
                kb.op("act", lambda e: e.activation(out=sq[i][:, :TW], in_=pp[:, b, :TW], func=AF.Square),
                      R=[f"pp{b}"], W=[f"sq{i}"])
                kb.op("pe", lambda e: e.matmul(pp[:, 6 + th, :TW], lhsT=onesf[:], rhs=sq[i][:, :TW],
                                               start=(fc == 0), stop=(fc == 7)),
                      R=[f"sq{i}", "onesf"], W=[f"pp{6 + th}"])
    for th in range(TH):
        sl = slice(th * TW, (th + 1) * TW)
        kb.op("act", lambda e: e.activation(out=rstd[:, sl], in_=pp[:, 6 + th, :TW], func=AF.Sqrt, scale=1.0 / 1024, bias=1e-6),
              R=[f"pp{6 + th}"], W=["rstd"])
    kb.op("dve", lambda e: e.reciprocal(out=rstd[:], in_=rstd[:]), R=["rstd"], W=["rstd"])
    kb.op("dve", lambda e: e.tensor_tensor(out=cri[:], in0=cosF[:], in1=rstd[:], op=ALU.mult), R=["rstd", "consts"], W=["cri"])
    kb.op("dve", lambda e: e.tensor_tensor(out=sri[:], in0=sinF[:], in1=rstd[:], op=ALU.mult), R=["rstd", "consts"], W=["sri"])
    kb.op("pool", lambda e: e.tensor_scalar(out=crq[:], in0=cri[:], scalar1=SCALE, scalar2=None, op0=ALU.mult), R=["cri"], W=["crq"])
    kb.op("pool", lambda e: e.tensor_scalar(out=srq[:], in0=sri[:], scalar1=SCALE, scalar2=None, op0=ALU.mult), R=["sri"], W=["srq"])

    for g2 in (range(2) if 'ii' in stages else []):
        wv, wk = load_w(w_in, 4 + g2, 32, 256)
        for fl in range(2):
            g = g2 * 2 + fl
            for th in range(TH):
                b = next_bank()
                for k in range(32):
                    kb.op("pe", lambda e: e.matmul(pp[:, b, :TW], lhsT=wv[:, k, fl * 128:(fl + 1) * 128],
                                                   rhs=xb[:, k, th * TW:(th + 1) * TW], start=(k == 0), stop=(k == 31)),
                          R=[wk] + XB, W=[f"pp{b}"])
                sl = slice(th * TW, (th + 1) * TW)
                rope_tile(pp[:, b, :TW], f"pp{b}", TW, cosF[:, sl], sinF[:, sl], ["consts"], kT_o[g, :, sl], f"kT{g}")

    def tokmajor(gidx, ncols, handler):
        wv, wk = load_w(w_in, gidx, 32, ncols)
        for tb in range(NBc):
            b = next_bank()
            for k in range(32):
                kb.op("pe", lambda e: e.matmul(pp[:, b, :ncols], lhsT=xb[:, k, tb * 128:(tb + 1) * 128],
                                               rhs=wv[:, k, :], start=(k == 0), stop=(k == 31)),
                      R=[wk] + XB, W=[f"pp{b}"])
            handler(tb, b)

    tm = {"n": 0}

    def h_v(c0v):
        def f(tb, b):
            i = tm["n"] % NR
            tm["n"] += 1
            kb.op("act", lambda e: e.activation(out=vo[i][:], in_=pp[:, b, :256], func=AF.Copy), R=[f"pp{b}"], W=[f"vo{i}"])
            kb.dma([("sp", v_o[tb * 128:(tb + 1) * 128, c0v:c0v + 256], vo[i][:])], R=[f"vo{i}"], W=["v_o"], sem=f"vo{i}", indep=True)
        return f

    if 'v' in stages:
        tokmajor(6, 256, h_v(0))
        tokmajor(7, 256, h_v(256))

    def h_ki(tb, b):
        kb.op("act", lambda e: e.activation(out=kis[:], in_=pp[:, b, :192], func=AF.Copy), R=[f"pp{b}"], W=["kis"])
        kb.op("dve", lambda e: e.tensor_scalar(out=wio[:, tb, :], in0=kis[:, 128:192], scalar1=WSC, scalar2=None, op0=ALU.mult),
              R=["kis"], W=["wio"])
        kb.op("dve", lambda e: e.bn_stats(out=bst[:], in_=kis[:, 0:128]), R=["kis"], W=["bst"])
        kb.op("dve", lambda e: e.bn_aggr(out=mv[:, 0:2], in_=bst[:]), R=["bst"], W=["mv"])
        kb.op("act", lambda e: e.activation(out=mv[:, 2:3], in_=mv[:, 1:2], func=AF.Sqrt, scale=1.0, bias=1e-5), R=["mv"], W=["mv"])
        kb.op("dve", lambda e: e.reciprocal(out=mv[:, 3:4], in_=mv[:, 2:3]), R=["mv"], W=["mv"])
        kb.op("dve", lambda e: e.tensor_scalar(out=kin[:], in0=kis[:, 0:128], scalar1=mv[:, 0:1], scalar2=mv[:, 3:4],
                                               op0=ALU.subtract, op1=ALU.mult), R=["kis", "mv"], W=["kin"])
        kb.op("dve", lambda e: e.tensor_tensor(out=kin[:], in0=kin[:], in1=kgbs[:, 0:128], op=ALU.mult), R=["kin", "consts"], W=["kin"])
        kb.op("dve", lambda e: e.tensor_tensor(out=kin[:], in0=kin[:], in1=kgbs[:, 128:256], op=ALU.add), R=["kin", "consts"], W=["kin"])
        c, s = cosT[:, tb, :], sinT[:, tb, :]
        kb.op("dve", lambda e: e.tensor_tensor(out=kir[:, 0:64], in0=kin[:, 0:64], in1=c, op=ALU.mult), R=["kin", "consts"], W=["kir"])
        kb.op("dve", lambda e: e.tensor_tensor(out=kit[:], in0=kin[:, 64:128], in1=s, op=ALU.mult), R=["kin", "consts"], W=["kit"])
        kb.op("dve", lambda e: e.tensor_tensor(out=kir[:, 0:64], in0=kir[:, 0:64], in1=kit[:], op=ALU.subtract), R=["kir", "kit"], W=["kir"])
        kb.op("dve", lambda e: e.tensor_tensor(out=kir[:, 64:128], in0=kin[:, 64:128], in1=c, op=ALU.mult), R=["kin", "consts"], W=["kir"])
        kb.op("dve", lambda e: e.tensor_tensor(out=kit[:], in0=kin[:, 0:64], in1=s, op=ALU.mult), R=["kin", "consts", "kir"], W=["kit"])
        kb.op("dve", lambda e: e.tensor_tensor(out=kib[:, 64:128], in0=kir[:, 64:128], in1=kit[:], op=ALU.add), R=["kir", "kit"], W=["kib"])
        kb.op("dve", lambda e: e.tensor_copy(out=kib[:, 0:64], in_=kir[:, 0:64]), R=["kir"], W=["kib"])
        pt = pp[:, 7, 0:64].bitcast(BF16)
        kb.op("pe", lambda e: e.transpose(pt, kib[:], identb[:]), R=["kib", "identb"], W=["pp7"])
        kb.op("act", lambda e: e.activation(out=kiTs[:, tb * 128:(tb + 1) * 128], in_=pt, func=AF.Copy), R=["pp7"], W=["kiTs"])

    if 'ki' in stages:
        tokmajor(8, 192, h_ki)
        kb.dma([("sp", kiT_o, kiTs[:])], R=["kiTs"], W=["kiT_o"], sem="kiTo")
        kb.dma([("sp", widx_o.rearrange("(m p) c -> p m c", p=128), wio[:])], R=["wio"], W=["widx_o"], sem="wio")

    def h_gate(c0g):
        def f(tb, b):
            i = tm["n"] % NR
            tm["n"] += 1
            kb.op("act", lambda e: e.activation(out=go[i][:], in_=pp[:, b, :256], func=AF.Silu), R=[f"pp{b}"], W=[f"go{i}"])
            kb.dma([("sp", sg_o[tb * 128:(tb + 1) * 128, c0g:c0g + 256], go[i][:])], R=[f"go{i}"], W=["sg_o"], sem=f"go{i}", indep=True)
        return f

    for gg in (range(16) if 'gate' in stages else []):
        tokmajor(9 + gg, 256, h_gate(gg * 256))

    for g12 in (range(12) if 'iv' in stages else []):
        wv, wk = load_w(w_uq, g12, 8, 1024)
        for hl in range(8):
            hh = g12 * 8 + hl
            for th in range(TH):
                b = next_bank()
                for k in range(8):
                    kb.op("pe", lambda e: e.matmul(pp[:, b, :TW], lhsT=wv[:, k, hl * 128:(hl + 1) * 128],
                                                   rhs=cqg[:, k, th * TW:(th + 1) * TW], start=(k == 0), stop=(k == 7)),
                          R=[wk] + [f"cqg{t}" for t in range(TH)], W=[f"pp{b}"])
                sl = slice(th * TW, (th + 1) * TW)
                if hh < 32:
                    rope_tile(pp[:, b, :TW], f"pp{b}", TW, crq[:, sl], srq[:, sl], ["crq", "srq"], qT_o[hh, :, sl], f"qT{hh}")
                else:
                    rope_tile(pp[:, b, :TW], f"pp{b}", TW, cri[:, sl], sri[:, sl], ["cri", "sri"], qiT_o[hh - 32, :, sl], f"qiT{hh}")
    outs = [f"kT{g}" for g in range(4)] + [f"qT{h}" for h in range(32)] + [f"qiT{h}" for h in range(32, 96)] + \
           ["v_o", "kiT_o", "widx_o", "sg_o"]
    kb.finish([o for o in outs if o in kb.st])
    print("A1 inst", kb.n_inst, "waits", kb.n_wait)
    kb.pop()
    if own:
        kb.close()
    return nc


def a1_inputs(inp, S, b, j):
    pos = own_pos(S, j)
    rt = rope_tables(pos)
    c = consts()
    qg = np.ascontiguousarray(inp["a_q_norm_g"][0].reshape(8, 128).T)
    kgb = np.concatenate([np.broadcast_to(inp["a_kidx_norm_g"][0][None, :], (128, 128)),
                          np.broadcast_to(inp["a_kidx_norm_b"][0][None, :], (128, 128))], 1)
    return dict(xT=np.ascontiguousarray(inp["x"][b, pos, :].T), a_w_in=inp["a_w_in_t"], a_w_uq=inp["a_w_uq_t"],
                qg=qg, kgb=np.ascontiguousarray(kgb), perm=c["perm"], ident=c["ident"], **rt)


def phase_a2a(nc, S, kb=None, io=None):
    NB = S // 128
    NBc = S // 512
    T = NBc * 128
    qT = dram_in(nc, "qT", [32, 128, T], BF16, io)
    qiT = dram_in(nc, "qiT", [64, 128, T], BF16, io)
    widx = dram_in(nc, "widx", [T, 64], F32, io)
    sgate = dram_in(nc, "sgate", [T, 4096], F32, io)
    kTf = dram_in(nc, "kTg", [4 * 512, T], BF16, io)
    vf = dram_in(nc, "vg", [4 * T, 512], BF16, io)
    kiTf = dram_in(nc, "kiTg", [4 * 128, T], BF16, io)
    ident_d = dram_in(nc, "ident", [128, 128], F32, io)
    cm4_d = dram_in(nc, "cm4", [128, 4, 128], F32, io)
    ogT_o = dram_out(nc, "ogT", [4096, T], BF16, io)

    own = kb is None
    kb = KB(nc) if own else kb
    kb.push()
    kis = kb.sbuf("kiTs", [128, S], BF16)
    kts = kb.sbuf("kTs", [128, 4, S], BF16)
    vs = kb.sbuf("vs", [128, NB, 4, 130], BF16)
    qib = kb.sbuf("qib", [128, 64, 128], BF16)
    Dg = kb.sbuf("Dg", [128, 64, 128], BF16)
    wix = kb.sbuf("wix", [128, 64], F32)
    score2 = [kb.sbuf(f"score{i}", [128, S], F32) for i in range(2)]
    mb = kb.sbuf("mb", [128, S], BF16)
    mbT = kb.sbuf("mbT", [128, NB, 128], BF16)
    qblk = kb.sbuf("qblk", [128, 32, 128], BF16)
    identf = kb.sbuf("identf", [128, 128], F32)
    identb = kb.sbuf("identb", [128, 128], BF16)
    cm4 = kb.sbuf("cm4s", [128, 4, 128], F32)
    half = kb.sbuf("half", [128, 1], F32)
    bs = kb.sbuf("bs", [128, 8], F32)
    KIT = 26
    pw2 = kb.sbuf("pw2", [128, KIT + 1], F32)
    hht = kb.sbuf("hht", [128, KIT + 1], F32)
    rl = [kb.sbuf(f"rl{i}", [128, 512], BF16) for i in range(6)]
    sm = [kb.sbuf(f"sm{i}", [128, 512], F32) for i in range(4)]
    pT = [kb.sbuf(f"pT{i}", [128, 512], BF16) for i in range(4)]
    sgb = [kb.sbuf(f"sgb{i}", [128, 512], F32) for i in range(2)]
    og4 = kb.sbuf("og4", [128, 512], BF16)
    ogT4 = [kb.sbuf(f"ogT4{i}", [128, 4, 128], BF16) for i in range(2)]
    rs = kb.sbuf("rs", [128, 4], F32)
    pp = kb.psum("pp", [128, 8, 512], F32)

    kb.dma([("sp", identf[:], ident_d), ("sp", cm4[:], cm4_d)] +
           [("sp", kis[:].rearrange("d (m j p) -> d m j p", j=4, p=128)[:, :, j, :],
             kiTf[j * 128:(j + 1) * 128, :].rearrange("d (m p) -> d m p", p=128)) for j in range(4)], W=["consts"], sem="consts")
    kb.dma([("sp", kts[:, g, :].rearrange("d (m j p) -> d m j p", j=4, p=128)[:, :, j, :],
             kTf[j * 512 + g * 128:j * 512 + (g + 1) * 128, :].rearrange("d (m p) -> d m p", p=128))
            for g in range(4) for j in range(4)], W=["kts"], sem="kts")
    kb.op("pool", lambda e: e.memset(vs[:], 1.0), W=["vs"])
    kb.dma([("sp", vs[:, :, g, 0:128].rearrange("p (m j) d -> p m j d", j=4)[:, :, j, :],
             vf[j * T:(j + 1) * T, g * 128:(g + 1) * 128].rearrange("(m p) d -> p m d", p=128))
            for g in range(4) for j in range(4)], W=["vs"], sem="vs")
    kb.op("dve", lambda e: e.tensor_copy(out=identb[:], in_=identf[:]), R=["consts"], W=["identb"])
    kb.op("dve", lambda e: e.memset(half[:], 0.5), W=["half"])
    for k in range(KIT + 1):
        kb.op("pool", lambda e: e.memset(pw2[:, k:k + 1], 2.0 ** -(k + 1)), W=["pw2"])
    LO, HI, MID, CNT, GE, D1, D2 = [slice(i, i + 1) for i in range(7)]
    ev = {"n": 0}

    def stage_I(m):
        nk = 4 * m + 4
        SK = nk * 128
        qs = slice(m * 128, (m + 1) * 128)
        score = score2[m % 2]
        skey = f"score{m % 2}"
        junk = mb
        kb.dma([("sp", qib[:], qiT[:, :, qs].rearrange("h d q -> d h q")), ("sp", wix[:], widx[qs, :])], W=["qib", "wix"], sem="qload")
        for h in range(64):
            eng = "dve" if h % 2 == 0 else "pool"
            kb.op(eng, lambda e: e.tensor_scalar(out=Dg[:, h, :], in0=identf[:], scalar1=wix[:, h:h + 1], scalar2=None, op0=ALU.mult),
                  R=["consts", "wix"], W=[f"Dg{h}"])
        DB = (0, 1, 2, 5, 6, 7)
        items = [(c0, min(512, SK - c0), h) for c0 in range(0, SK, 512) for h in range(64)]

        def idx_s1(it, t):
            c0, N, h = it
            b = DB[t % 6]
            kb.op("pe", lambda e: e.matmul(pp[:, b, :N], lhsT=qib[:, h, :], rhs=kis[:, c0:c0 + N], start=True, stop=True),
                  R=["qib", "consts"], W=[f"pp{b}"])

        def idx_s2(it, t):
            c0, N, h = it
            b = DB[t % 6]
            i = t % 6
            if True:
                kb.op("act", lambda e: e.activation(out=rl[i][:, :N], in_=pp[:, b, :N], func=AF.Relu), R=[f"pp{b}"], W=[f"rl{i}"])
            else:
                kb.op("dve", lambda e: e.tensor_scalar(out=rl[i][:, :N], in0=pp[:, b, :N], scalar1=0.0, scalar2=None, op0=ALU.max),
                      R=[f"pp{b}"], W=[f"rl{i}"])
            kb.op("pe", lambda e: e.matmul(pp[:, 3, :N], lhsT=Dg[:, h, :], rhs=rl[i][:, :N], start=(h == 0), stop=(h == 63)),
                  R=[f"rl{i}", f"Dg{h}"], W=["pp3"])
            if h == 63:
                kb.op("act", lambda e: e.activation(out=score[:, c0:c0 + N], in_=pp[:, 3, :N], func=AF.Copy), R=["pp3"], W=[skey])

        emit_skewed(items, idx_s1, idx_s2, 4)

    def stage_B(m):
        nk = 4 * m + 4
        SK = nk * 128
        qs = slice(m * 128, (m + 1) * 128)
        score = score2[m % 2]
        skey = f"score{m % 2}"
        junk = mb
        W0 = slice(5, 6)
        SG = slice(4, 5)
        kb.op("dve", lambda e: e.tensor_reduce(out=bs[:, LO], in_=score[:, :SK], op=ALU.min, axis=AX.X), R=[skey], W=["bs"])
        kb.op("dve", lambda e: e.tensor_reduce(out=bs[:, HI], in_=score[:, :SK], op=ALU.max, axis=AX.X), R=[skey], W=["bs"])
        kb.op("dve", lambda e: e.tensor_tensor(out=score[:, SK - 512:SK], in0=score[:, SK - 512:SK],
                                               in1=cm4[:].rearrange("p a b -> p (a b)"), op=ALU.add), R=[skey, "consts"], W=[skey])
        kb.op("dve", lambda e: e.tensor_tensor(out=bs[:, W0], in0=bs[:, HI], in1=bs[:, LO], op=ALU.subtract), R=["bs"], W=["bs"])
        kb.op("dve", lambda e: e.tensor_scalar(out=bs[:, W0], in0=bs[:, W0], scalar1=1.01, scalar2=1e-5, op0=ALU.mult, op1=ALU.add), R=["bs"], W=["bs"])
        kb.op("dve", lambda e: e.tensor_scalar(out=hht[:], in0=pw2[:], scalar1=bs[:, W0], scalar2=None, op0=ALU.mult), R=["bs", "pw2"], W=["hh"])
        kb.op("dve", lambda e: e.tensor_tensor(out=bs[:, MID], in0=bs[:, HI], in1=hht[:, 0:1], op=ALU.subtract), R=["bs", "hh"], W=["bs"])
        for it in range(KIT):
            kb.op("dve", lambda e: e.tensor_scalar(out=junk[:, :SK], in0=score[:, :SK], scalar1=bs[:, MID], scalar2=None,
                                                   op0=ALU.is_ge, op1=ALU.add, accum_out=bs[:, CNT]), R=[skey, "bs"], W=["mb", "bs"])
            kb.op("dve", lambda e: e.tensor_scalar(out=bs[:, SG], in0=bs[:, CNT], scalar1=255.5, scalar2=0.5, op0=ALU.is_ge, op1=ALU.subtract),
                  R=["bs"], W=["bs"])
            kb.op("dve", lambda e: e.scalar_tensor_tensor(out=bs[:, MID], in0=bs[:, SG], scalar=hht[:, it:it + 1], in1=bs[:, MID],
                                                          op0=ALU.mult, op1=ALU.add), R=["bs", "hh"], W=["bs"])
        kb.op("dve", lambda e: e.tensor_tensor(out=bs[:, LO], in0=bs[:, MID], in1=hht[:, KIT:KIT + 1], op=ALU.subtract), R=["bs", "hh"], W=["bs"])
        kb.op("dve", lambda e: e.tensor_scalar(out=mb[:, :SK], in0=score[:, :SK], scalar1=bs[:, LO], scalar2=NEG, op0=ALU.is_lt, op1=ALU.mult),
              R=[skey, "bs"], W=["mb"])
        for k4 in range(0, nk, 4):
            pt = pp[:, 4, 0:256].bitcast(BF16).rearrange("p (a b) -> p a b", a=4)
            for a in range(4):
                kb.op("pe", lambda e: e.transpose(pt[:, a, :], mb[:, (k4 + a) * 128:(k4 + a + 1) * 128], identb[:]), R=["mb", "identb"], W=["pp4"])
            kb.op("act", lambda e: e.activation(out=mbT[:, k4:k4 + 4, :], in_=pt, func=AF.Copy), R=["pp4"], W=["mbT"])

    def stage_A(m):
        nk = 4 * m + 4
        SK = nk * 128
        qs = slice(m * 128, (m + 1) * 128)
        score = score2[m % 2]
        skey = f"score{m % 2}"
        junk = mb
        kb.dma([("sp", qblk[:], qT[:, :, qs].rearrange("h d q -> d h q"))], W=["qblk"], sem="qblk")
        aitems = [(g, hp, kk) for g in range(4) for hp in range(2) for kk in range(nk)]

        def att_s1(it, t):
            g, hp, kk = it
            h0 = g * 8 + hp * 4
            b = t % 4
            if kk == 0:
                sgi = (g * 2 + hp) % 2
                kb.dma([("sp", sgb[sgi][:], sgate[qs, h0 * 128:(h0 + 4) * 128])], W=[f"sgb{sgi}"], sem=f"sgb{sgi}")
            kb.op("pe", lambda e: e.matmul(pp[:, b, :], lhsT=kts[:, g, kk * 128:(kk + 1) * 128],
                                           rhs=qblk[:, h0:h0 + 4, :].rearrange("p h q -> p (h q)"), start=True, stop=True),
                  R=["kts", "qblk"], W=[f"pp{b}"])

        def att_s2(it, t):
            g, hp, kk = it
            h0 = g * 8 + hp * 4
            b = t % 4
            i = t % 4
            sgi = (g * 2 + hp) % 2
            kb.op("dve", lambda e: e.tensor_tensor(out=sm[i][:].rearrange("p (h q) -> p h q", h=4),
                                                   in0=pp[:, b, :].rearrange("p (h q) -> p h q", h=4),
                                                   in1=mbT[:, kk:kk + 1, :].to_broadcast([128, 4, 128]), op=ALU.add),
                  R=[f"pp{b}", "mbT"], W=[f"sm{i}"])
            kb.op("act", lambda e: e.activation(out=pT[i][:], in_=sm[i][:], func=AF.Exp), R=[f"sm{i}"], W=[f"pT{i}"])
            for hh in range(4):
                kb.op("pe", lambda e: e.matmul(pp[:, 4 + hh, 0:129], lhsT=pT[i][:, hh * 128:(hh + 1) * 128], rhs=vs[:, kk, g, 0:129],
                                               start=(kk == 0), stop=(kk == nk - 1)), R=[f"pT{i}", "vs"], W=[f"pp{4 + hh}"])
            if kk == nk - 1:
                for hh in range(4):
                    kb.op("dve", lambda e: e.reciprocal(out=rs[:, hh:hh + 1], in_=pp[:, 4 + hh, 128:129]), R=[f"pp{4 + hh}"], W=["rs"])
                    kb.op("dve", lambda e: e.scalar_tensor_tensor(out=og4[:, hh * 128:(hh + 1) * 128], in0=pp[:, 4 + hh, 0:128],
                                                                  scalar=rs[:, hh:hh + 1], in1=sgb[sgi][:, hh * 128:(hh + 1) * 128],
                                                                  op0=ALU.mult, op1=ALU.mult), R=[f"pp{4 + hh}", "rs", f"sgb{sgi}"], W=["og4"])
                i2 = ev["n"] % 2
                ev["n"] += 1
                pt = pp[:, 4, 256:512].bitcast(BF16).rearrange("p (a b) -> p a b", a=4)
                for hh in range(4):
                    kb.op("pe", lambda e: e.transpose(pt[:, hh, :], og4[:, hh * 128:(hh + 1) * 128], identb[:]), R=["og4", "identb"], W=["pp4"])
                kb.op("act", lambda e: e.activation(out=ogT4[i2][:], in_=pt, func=AF.Copy), R=["pp4"], W=[f"ogT4{i2}"])
                kb.dma([("sp", ogT_o[h0 * 128:(h0 + 4) * 128, qs].rearrange("(h d) q -> d h q", h=4), ogT4[i2][:])],
                       R=[f"ogT4{i2}"], W=["ogT_o"], sem=f"ogT4{i2}", indep=True)

        emit_skewed(aitems, att_s1, att_s2, 3)

    stage_I(0)
    for m in range(NBc):
        if m + 1 < NBc:
            stage_I(m + 1)
        stage_B(m)
        stage_A(m)
    kb.finish(["ogT_o"])
    print("A2a inst", kb.n_inst, "waits", kb.n_wait)
    kb.pop()
    if own:
        kb.close()
    return nc


def causal_tables(j):
    q = np.arange(128)[:, None]
    s = np.arange(128)[None, :]
    cm4 = np.zeros((128, 4, 128), np.float32)
    tri4 = np.zeros((128, 4, 128), np.float32)
    for r in range(4):
        if r == j:
            cm4[:, r, :] = np.where(s <= q, 0.0, -1e30)
            tri4[:, r, :] = np.where(s.T <= q.T, 0.0, NEG)
        elif r > j:
            cm4[:, r, :] = -1e30
            tri4[:, r, :] = NEG
    return cm4, tri4


def phase_out(nc, S, final, kb=None, io=None):
    NBc = S // 512
    T = NBc * 128
    HB = min(4, NBc)
    ogT = dram_in(nc, "ogT", [4096, T], BF16, io)
    w_out = dram_in(nc, "w_out", [16, 128, 8192], F32, io)
    x_d = dram_in(nc, "x", [T, 4096], F32, io)
    lng_d = dram_in(nc, "lng", [128, 4096], F32, io)
    lnb_d = dram_in(nc, "lnb", [128, 4096], F32, io)
    ident_d = dram_in(nc, "ident", [128, 128], F32, io)
    y_o = dram_out(nc, "y", [T, 4096], F32, io)
    yT_o = None if final else dram_out(nc, "yT", [4096, T], BF16, io)
    ALPHA = 4 ** 0.25

    own = kb is None
    kb = KB(nc) if own else kb
    kb.push()
    ogb = kb.sbuf("ogb", [128, 32, HB * 128], BF16)
    wslot = [kb.sbuf(f"wslot{i}", [128, 32, 256], BF16) for i in range(2)]
    ybuf = kb.sbuf("ybuf", [128, HB, 4096], F32)
    lng = kb.sbuf("lngs", [128, 4096], F32)
    lnb = kb.sbuf("lnbs", [128, 4096], F32)
    identf = kb.sbuf("identf", [128, 128], F32)
    bst = kb.sbuf("bst", [128, 8, 6], F32)
    mv = kb.sbuf("mv", [128, 4], F32)
    yTs = kb.sbuf("yTs", [128, 32, 128], BF16)
    pp = kb.psum("pp", [128, 8, 512], F32)
    kb.dma([("sp", lng[:], lng_d), ("sp", lnb[:], lnb_d), ("sp", identf[:], ident_d)], W=["consts"], sem="consts")
    ogr = ogT.rearrange("(c p) t -> p c t", p=128)
    wn = {"n": 0}
    for hf in range(NBc // HB):
        t0 = hf * HB * 128
        kb.dma([("sp", ogb[:, a * 8:(a + 1) * 8, :], ogr[:, a * 8:(a + 1) * 8, t0:t0 + HB * 128]) for a in range(4)], W=["ogb"], sem="ogb")
        kb.dma([("sp", ybuf[:, tb, :], x_d[t0 + tb * 128:t0 + (tb + 1) * 128, :]) for tb in range(HB)], W=["ybuf"], sem="ybuf")
        for cg in range(16):
            i = wn["n"] % 2
            wn["n"] += 1
            wfl = wslot[i][:].rearrange("p k c -> p (k c)")
            kb.dma([("pool", wfl[:, a * 2048:(a + 1) * 2048], w_out[cg][:, a * 2048:(a + 1) * 2048]) for a in range(4)], W=[f"w{i}"], sem=f"w{i}")
            for tb in range(HB):
                b = (cg * HB + tb) % 4
                for c in range(32):
                    kb.op("pe", lambda e: e.matmul(pp[:, b, :256], lhsT=ogb[:, c, tb * 128:(tb + 1) * 128], rhs=wslot[i][:, c, :],
                                                   start=(c == 0), stop=(c == 31)), R=["ogb", f"w{i}"], W=[f"pp{b}"])
                ysl = ybuf[:, tb, cg * 256:(cg + 1) * 256]
                kb.op("dve", lambda e: e.scalar_tensor_tensor(out=ysl, in0=ysl, scalar=ALPHA, in1=pp[:, b, :256], op0=ALU.mult, op1=ALU.add),
                      R=[f"pp{b}"], W=["ybuf"])
        for tb in range(HB):
            for c8 in range(8):
                kb.op("dve", lambda e: e.bn_stats(out=bst[:, c8, :], in_=ybuf[:, tb, c8 * 512:(c8 + 1) * 512]), R=["ybuf"], W=["bst"])
            kb.op("dve", lambda e: e.bn_aggr(out=mv[:, 0:2], in_=bst[:].rearrange("p a b -> p (a b)")), R=["bst"], W=["mv"])
            kb.op("act", lambda e: e.activation(out=mv[:, 2:3], in_=mv[:, 1:2], func=AF.Sqrt, scale=1.0, bias=1e-5), R=["mv"], W=["mv"])
            kb.op("dve", lambda e: e.reciprocal(out=mv[:, 3:4], in_=mv[:, 2:3]), R=["mv"], W=["mv"])
            kb.op("dve", lambda e: e.tensor_scalar(out=ybuf[:, tb, :], in0=ybuf[:, tb, :], scalar1=mv[:, 0:1], scalar2=mv[:, 3:4],
                                                   op0=ALU.subtract, op1=ALU.mult), R=["mv"], W=["ybuf"])
            kb.op("dve", lambda e: e.tensor_tensor(out=ybuf[:, tb, :], in0=ybuf[:, tb, :], in1=lng[:], op=ALU.mult), R=["consts"], W=["ybuf"])
            kb.op("dve", lambda e: e.tensor_tensor(out=ybuf[:, tb, :], in0=ybuf[:, tb, :], in1=lnb[:], op=ALU.add), R=["consts"], W=["ybuf"])
            kb.dma([("sp", y_o[t0 + tb * 128:t0 + (tb + 1) * 128, :], ybuf[:, tb, :])], R=["ybuf"], W=["y_o"], sem="yo", indep=True)
            if not final:
                for c4 in range(8):
                    b = 4 + c4 % 2
                    for a in range(4):
                        c = c4 * 4 + a
                        kb.op("pe", lambda e: e.transpose(pp[:, b, a * 128:(a + 1) * 128], ybuf[:, tb, c * 128:(c + 1) * 128], identf[:]),
                              R=["ybuf", "consts"], W=[f"pp{b}"])
                    kb.op("act", lambda e: e.activation(out=yTs[:, c4 * 4:(c4 + 1) * 4, :].rearrange("p a b -> p (a b)"), in_=pp[:, b, :],
                                                        func=AF.Copy), R=[f"pp{b}"], W=["yTs"])
                kb.dma([("sp", yT_o.rearrange("(c p) t -> p c t", p=128)[:, :, t0 + tb * 128:t0 + (tb + 1) * 128], yTs[:])],
                       R=["yTs"], W=["yT_o"], sem="yTo", indep=True)
    kb.finish(["y_o", "yT_o"])
    print("OUT inst", kb.n_inst, "waits", kb.n_wait)
    kb.pop()
    if own:
        kb.close()
    return nc


def phase_b1(nc, S, kb=None, io=None):
    NBc = S // 512
    T = NBc * 128
    TW = min(T, 512)
    TH = T // TW
    x1T = dram_in(nc, "x1T", [4096, T], BF16, io)
    w_in = dram_in(nc, "b_w_in", [32, 128, 8192], F32, io)
    w_vg = dram_in(nc, "b_w_vg", [17, 128, 16384], F32, io)
    fb_d = dram_in(nc, "fb", [128, 32], F32, io)
    qT_o = dram_out(nc, "qT1", [32, 128, T], BF16, io)
    kT_o = dram_out(nc, "kT1", [32, 128, T], BF16, io)
    v_o = dram_out(nc, "v1", [T, 4096], BF16, io)
    sg_o = dram_out(nc, "sgate1", [T, 4096], F32, io)
    lf_o = dram_out(nc, "lf", [T, 32], F32, io)

    own = kb is None
    kb = KB(nc) if own else kb
    kb.push()
    xb = kb.sbuf("xb", [128, 32, T], BF16)
    wslot = [kb.sbuf(f"wslot{i}", [128, 32, 512], BF16) for i in range(2)]
    wstage = kb.sbuf("wstage", [128, 8192], F32)
    fbs = kb.sbuf("fbs", [128, 32], F32)
    NR = 4
    ro = [kb.sbuf(f"ro{i}", [128, 512], BF16) for i in range(NR)]
    go = [kb.sbuf(f"go{i}", [128, 512], F32) for i in range(NR)]
    vo = [kb.sbuf(f"vo{i}", [128, 512], BF16) for i in range(NR)]
    zz = kb.sbuf("zz", [128, 32], F32)
    lfo = kb.sbuf("lfo", [128, NBc, 32], F32)
    pp = kb.psum("pp", [128, 8, 512], F32)
    xr = x1T.rearrange("(k p) t -> p k t", p=128)
    kb.dma([("sp", xb[:, a * 8:(a + 1) * 8, :], xr[:, a * 8:(a + 1) * 8, :]) for a in range(4)], W=["xb"], sem="xb")
    kb.dma([("sp", fbs[:], fb_d)], W=["fbs"], sem="fbs")
    wn = {"n": 0}

    def load_w(gidx, ncols, wt=None):
        wt = w_in if wt is None else wt
        i = wn["n"] % 2
        wn["n"] += 1
        n = 32 * ncols
        step = min(n, 2048)
        wfl = wslot[i][:].rearrange("p k c -> p (k c)")
        if True:
            kb.dma([("pool", wfl[:, a:a + step], wt[gidx][:, a:a + step]) for a in range(0, n, step)], W=[f"w{i}a", f"w{i}b"], sem=f"w{i}")
        else:
            kb.dma([("sp", wstage[:, a:a + step], w_in[gidx][:, a:a + step]) for a in range(0, n, step)], W=["wstage"], sem="wstage")
            for a in range(0, n, 4096):
                e_ = min(n, a + 4096)
                kb.op("dve", lambda e: e.tensor_copy(out=wfl[:, a:e_], in_=wstage[:, a:e_]), R=["wstage"], W=[f"w{i}a" if a == 0 else f"w{i}b"])
        return wfl[:, 0:n].rearrange("p (k c) -> p k c", k=32), f"w{i}"

    mm = {"n": 0}
    cc = io.get("cc") if io is not None else None
    if cc is not None:
        nck = max(1, (4096 * T * 2) >> 20)
        rc = 4096 // nck
    for gi in range(32):
        wv, wk = load_w(gi, 256)
        for fl in range(2):
            hh = gi * 2 + fl
            for th in range(TH):
                b = mm["n"] % 4
                mm["n"] += 1
                for k in range(32):
                    kb.op("pe", lambda e: e.matmul(pp[:, b, :TW], lhsT=wv[:, k, fl * 128:(fl + 1) * 128], rhs=xb[:, k, th * TW:(th + 1) * TW],
                                                   start=(k == 0), stop=(k == 31)), R=[wk + "a", wk + "b", "xb"], W=[f"pp{b}"])
                i = mm["n"] % NR
                sl = slice(th * TW, (th + 1) * TW)
                if hh < 32:
                    kb.op("act", lambda e: e.activation(out=ro[i][:, :TW], in_=pp[:, b, :TW], func=AF.Copy, scale=SCALE), R=[f"pp{b}"], W=[f"ro{i}"])
                    kb.dma([("sp", qT_o[hh, :, sl], ro[i][:, :TW])], R=[f"ro{i}"], W=["qT_o"], sem=f"ro{i}", indep=True)
                else:
                    kb.op("dve", lambda e: e.tensor_copy(out=ro[i][:, :TW], in_=pp[:, b, :TW]), R=[f"pp{b}"], W=[f"ro{i}"])
                    kb.dma([("sp", kT_o[hh - 32, :, sl], ro[i][:, :TW])], R=[f"ro{i}"], W=["kT_o"], sem=f"ro{i}", indep=True)
        if cc is not None and gi >= 16:
            rows_done = (gi - 16 + 1) * 256
            if rows_done % rc == 0:
                c = rows_done // rc - 1
                kb.collective("AllGather", cc["kT1"][c * rc:(c + 1) * rc, :], cc["kT1g"][c * 4 * rc:(c + 1) * 4 * rc, :], cc["groups"], after=["kT_o"])

    def tokmajor(gidx, ncols, handler):
        wv, wk = load_w(gidx, ncols, w_vg)
        for tb in range(NBc):
            b = mm["n"] % 4
            mm["n"] += 1
            for k in range(32):
                kb.op("pe", lambda e: e.matmul(pp[:, b, :ncols], lhsT=xb[:, k, tb * 128:(tb + 1) * 128], rhs=wv[:, k, :],
                                               start=(k == 0), stop=(k == 31)), R=[wk + "a", wk + "b", "xb"], W=[f"pp{b}"])
            handler(tb, b)

    tm = {"n": 0}

    def h_v(c0v):
        def f(tb, b):
            i = tm["n"] % NR
            tm["n"] += 1
            kb.op("dve", lambda e: e.tensor_copy(out=vo[i][:], in_=pp[:, b, :512]), R=[f"pp{b}"], W=[f"vo{i}"])
            kb.dma([("sp", v_o[tb * 128:(tb + 1) * 128, c0v:c0v + 512], vo[i][:])], R=[f"vo{i}"], W=["v_o"], sem=f"vo{i}", indep=True)
        return f

    def h_gate(c0g):
        def f(tb, b):
            i = tm["n"] % NR
            tm["n"] += 1
            kb.op("act", lambda e: e.activation(out=go[i][:], in_=pp[:, b, :512], func=AF.Silu), R=[f"pp{b}"], W=[f"go{i}"])
            kb.dma([("sp", sg_o[tb * 128:(tb + 1) * 128, c0g:c0g + 512], go[i][:])], R=[f"go{i}"], W=["sg_o"], sem=f"go{i}", indep=True)
        return f

    for gi in range(8):
        tokmajor(gi, 512, h_v(gi * 512))
    if cc is not None:
        for m in range(NBc):
            kb.collective("AllGather", cc["v1"][m * 128:(m + 1) * 128, :], cc["v1g"][m * 512:(m + 1) * 512, :], cc["groups"], after=["v_o"])
    for gi in range(8):
        tokmajor(8 + gi, 512, h_gate(gi * 512))

    def h_f(tb, b):
        kb.op("dve", lambda e: e.tensor_tensor(out=zz[:], in0=pp[:, b, :32], in1=fbs[:], op=ALU.add), R=[f"pp{b}", "fbs"], W=["zz"])
        kb.op("act", lambda e: e.activation(out=zz[:], in_=zz[:], func=AF.Exp, scale=-1.0), R=["zz"], W=["zz"])
        kb.op("act", lambda e: e.activation(out=zz[:], in_=zz[:], func=AF.Ln, scale=1.0, bias=1.0), R=["zz"], W=["zz"])
        kb.op("dve", lambda e: e.tensor_scalar(out=lfo[:, tb, :], in0=zz[:], scalar1=-1.0, scalar2=None, op0=ALU.mult), R=["zz"], W=["lfo"])

    tokmajor(16, 32, h_f)
    kb.dma([("sp", lf_o.rearrange("(m p) h -> p m h", p=128), lfo[:])], R=["lfo"], W=["lf_o"], sem="lfo")
    if cc is not None:
        kb.collective("AllGather", cc["lf1"], cc["lf1g"], cc["groups"], after=["lf_o"])
    kb.finish(["qT_o", "kT_o", "v_o", "sg_o", "lf_o"])
    print("B1 inst", kb.n_inst, "waits", kb.n_wait)
    kb.pop()
    if own:
        kb.close()
    return nc


def phase_b2a(nc, S, kb=None, io=None):
    NB = S // 128
    NBc = S // 512
    T = NBc * 128
    qT = dram_in(nc, "qT1", [32, 128, T], BF16, io)
    kTf = dram_in(nc, "kT1g", [4 * 4096, T], BF16, io)
    vf = dram_in(nc, "v1g", [S, 4096], BF16, io)
    nck = max(1, (4096 * T * 2) >> 20)
    rc = 4096 // nck
    lff = dram_in(nc, "lfg", [4 * T, 32], F32, io)
    sgate = dram_in(nc, "sgate1", [T, 4096], F32, io)
    tri4_d = dram_in(nc, "tri4", [128, 4, 128], F32, io)
    ident_d = dram_in(nc, "ident", [128, 128], F32, io)
    sel_d = dram_in(nc, "sel", [128, 4], F32, io)
    selh_d = dram_in(nc, "selh", [32, 32, 128], F32, io)
    ogT_o = dram_out(nc, "ogT", [4096, T], BF16, io)

    own = kb is None
    kb = KB(nc) if own else kb
    kb.push()
    identf = kb.sbuf("identf", [128, 128], F32)
    identb = kb.sbuf("identb", [128, 128], BF16)
    tri4 = kb.sbuf("tri4s", [128, 4, 128], F32)
    tri4b = kb.sbuf("tri4b", [128, 4, 128], BF16)
    sel = kb.sbuf("sels", [128, 4], F32)
    lfs = kb.sbuf("lfs", [128, NB, 32], F32)
    lfT = kb.sbuf("lfT", [32, S], F32)
    onesT = kb.sbuf("onesT", [32, S], F32)
    cT = kb.sbuf("cT", [32, S], F32)
    ocT = kb.sbuf("ocT", [32, NBc, 128], F32)
    ocR = kb.sbuf("ocR", [32, NBc, 128], F32)
    r1 = kb.sbuf("r1", [32, S], F32)
    nhi = kb.sbuf("nhi", [32, S], BF16)
    nmid = kb.sbuf("nmid", [32, S], BF16)
    nlo = kb.sbuf("nlo", [32, S], BF16)
    ohi = kb.sbuf("ohi", [32, T], BF16)
    omid = kb.sbuf("omid", [32, T], BF16)
    olo = kb.sbuf("olo", [32, T], BF16)
    augl = [kb.sbuf(f"augl{i}", [128, S], BF16) for i in range(2)]
    augr = [kb.sbuf(f"augr{i}", [128, T], BF16) for i in range(2)]
    kts = [kb.sbuf(f"kts{i}", [128, S], BF16) for i in range(2)]
    vs = [kb.sbuf(f"vs{i}", [128, NB, 130], BF16) for i in range(2)]
    qhr = [kb.sbuf(f"qhr{i}", [128, T], BF16) for i in range(2)]
    sgh = [kb.sbuf(f"sgh{i}", [128, NBc, 128], F32) for i in range(2)]
    NP = 4
    pT = [kb.sbuf(f"pT{i}", [128, 512], BF16) for i in range(NP)]
    ogs = [kb.sbuf(f"ogs{i}", [128, 128], BF16) for i in range(2)]
    ogTs = [kb.sbuf(f"ogTs{i}", [128, T], BF16) for i in range(2)]
    rs = kb.sbuf("rs", [128, 2], F32)
    pp = kb.psum("pp", [128, 8, 512], F32)

    groups = [list(range(min(g0 + 4, NBc) - 1, g0 - 1, -1)) for g0 in range(0, NBc, 4)]
    slot = {}
    for gi, grp in enumerate(groups):
        for idx, m in enumerate(grp):
            slot[m] = gi * 4 + idx

    kb.dma([("sp", identf[:], ident_d), ("sp", tri4[:], tri4_d), ("sp", sel[:], sel_d)] +
           [("sp", lfs[:].rearrange("p (m j) h -> p m j h", j=4)[:, :, j, :],
             lff[j * T:(j + 1) * T, :].rearrange("(m p) h -> p m h", p=128)) for j in range(4)], W=["consts"], sem="consts")
    kb.op("dve", lambda e: e.tensor_copy(out=identb[:], in_=identf[:]), R=["consts"], W=["identb"])
    kb.op("dve", lambda e: e.tensor_copy(out=tri4b[:], in_=tri4[:]), R=["consts"], W=["tri4b"])
    kb.op("dve", lambda e: e.memset(onesT[:], 1.0), W=["onesT"])
    for i in range(2):
        kb.op("pool", lambda e: e.memset(vs[i][:], 1.0), W=[f"vs{i}"])
        kb.op("pool", lambda e: e.memset(augl[i][:], 0.0), W=[f"augl{i}"])
        kb.op("pool", lambda e: e.memset(augr[i][:], 0.0), W=[f"augr{i}"])
        kb.op("pool", lambda e: e.memset(augl[i][32:35, :], 1.0), W=[f"augl{i}"])
        kb.op("pool", lambda e: e.memset(augr[i][0:3, :], 1.0), W=[f"augr{i}"])
    for k4 in range(0, NB, 4):
        for a in range(4):
            kb.op("pe", lambda e: e.transpose(pp[0:32, 0, a * 128:(a + 1) * 128], lfs[:, k4 + a, :], identf[:]), R=["consts"], W=["pp0"])
        kb.op("act", lambda e: e.activation(out=lfT[:, k4 * 128:(k4 + 4) * 128], in_=pp[0:32, 0, :], func=AF.Copy), R=["pp0"], W=["lfT"])
    kb.op("dve", lambda e: e.tensor_tensor_scan(out=cT[:], data0=onesT[:], data1=lfT[:], initial=0.0, op0=ALU.mult, op1=ALU.add),
          R=["onesT", "lfT"], W=["cT"])
    cT4 = cT[:].rearrange("p (m r q) -> p m r q", r=4, q=128)
    kb.op("dve", lambda e: e.tensor_scalar(out=ocT[:], in0=cT4[:, :, 0, :], scalar1=sel[0:32, 0:1], scalar2=None, op0=ALU.mult), R=["cT", "consts"], W=["ocT"])
    for r in range(1, 4):
        kb.op("dve", lambda e: e.scalar_tensor_tensor(out=ocT[:], in0=cT4[:, :, r, :], scalar=sel[0:32, r:r + 1], in1=ocT[:], op0=ALU.mult, op1=ALU.add),
              R=["cT", "consts"], W=["ocT"])
    for m in range(NBc):
        kb.op("dve", lambda e: e.tensor_copy(out=ocR[:, slot[m], :], in_=ocT[:, m, :]), R=["ocT"], W=["ocR"])

    def split3(src, hi, mid, lo, n, neg, key):
        sg = -1.0 if neg else 1.0
        kb.op("dve", lambda e: e.tensor_scalar(out=hi, in0=src, scalar1=sg, scalar2=None, op0=ALU.mult), R=[key], W=[key + "hi"])
        kb.op("dve", lambda e: e.scalar_tensor_tensor(out=r1[:, :n], in0=src, scalar=sg, in1=hi, op0=ALU.mult, op1=ALU.subtract),
              R=[key, key + "hi"], W=["r1"])
        kb.op("dve", lambda e: e.tensor_copy(out=mid, in_=r1[:, :n]), R=["r1"], W=[key + "mid"])
        kb.op("dve", lambda e: e.tensor_tensor(out=r1[:, :n], in0=r1[:, :n], in1=mid, op=ALU.subtract), R=[key + "mid"], W=["r1"])
        kb.op("dve", lambda e: e.tensor_copy(out=lo, in_=r1[:, :n]), R=["r1"], W=[key + "lo"])

    split3(cT[:], nhi[:], nmid[:], nlo[:], S, True, "cT")
    split3(ocR[:].rearrange("p a b -> p (a b)"), ohi[:], omid[:], olo[:], T, False, "ocR")
    CK = ["cThi", "cTmid", "cTlo", "ocRhi", "ocRmid", "ocRlo"]

    st = {"e": 0}
    bitems = []
    for h in range(32):
        first = True
        for gi, grp in enumerate(groups):
            for kk in range(4 * grp[0] + 4):
                bitems.append((h, gi, kk, first))
                first = False
    last_of_head = {}
    for t, it in enumerate(bitems):
        last_of_head[it[0]] = t

    def b_s1(it, t):
        h, gi, kk, first = it
        i = h % 2
        grp = groups[gi]
        g0 = gi * 4 * 128
        if first:
            kb.dma([("sp", sgh[i][:], sgate[:, h * 128:(h + 1) * 128].rearrange("(m p) d -> p m d", p=128))] +
                   [("sp", qhr[i][:, slot[m] * 128:(slot[m] + 1) * 128], qT[h][:, m * 128:(m + 1) * 128]) for m in range(NBc)] +
                   [("sp", kts[i][:].rearrange("d (m j p) -> d m j p", j=4, p=128)[:, :, j, :],
                     kTf[((h * 128) // rc) * 4 * rc + j * rc + (h * 128) % rc:((h * 128) // rc) * 4 * rc + j * rc + (h * 128) % rc + 128, :]
                     .rearrange("d (m p) -> d m p", p=128)) for j in range(4)] +
                   [("sp", vs[i][:, :, 0:128], vf[:, h * 128:(h + 1) * 128].rearrange("(kb p) d -> p kb d", p=128))] +
                   [("sp", augl[i][a:a + 1, :], t_[h:h + 1, :]) for a, t_ in enumerate((nhi, nmid, nlo))] +
                   [("sp", augr[i][32 + a:33 + a, :], t_[h:h + 1, :]) for a, t_ in enumerate((ohi, omid, olo))],
                   R=CK, W=[f"kts{i}", f"qhr{i}", f"sgh{i}", f"vs{i}", f"augl{i}", f"augr{i}"], sem=f"hl{i}")
        act = [m for m in grp if kk < 4 * m + 4]
        na = len(act)
        N = na * 128
        b = t % 4
        ml = act[-1]
        tri = kk >= 4 * ml
        kb.op("pe", lambda e: e.matmul(pp[:, b, :N], lhsT=kts[i][:, kk * 128:(kk + 1) * 128], rhs=qhr[i][:, g0:g0 + N],
                                       start=True, stop=False), R=[f"kts{i}", f"qhr{i}"], W=[f"pp{b}"])
        kb.op("pe", lambda e: e.matmul(pp[:, b, :N], lhsT=augl[i][:, kk * 128:(kk + 1) * 128], rhs=augr[i][:, g0:g0 + N],
                                       start=False, stop=(not tri)), R=[f"augl{i}", f"augr{i}"], W=[f"pp{b}"])
        if tri:
            kb.op("pe", lambda e: e.matmul(pp[:, b, (na - 1) * 128:na * 128], lhsT=identb[:], rhs=tri4b[:, kk - 4 * ml, :],
                                           start=False, stop=True), R=["identb", "tri4b"], W=[f"pp{b}"])

    def b_s2(it, t):
        h, gi, kk, first = it
        i = h % 2
        grp = groups[gi]
        act = [m for m in grp if kk < 4 * m + 4]
        na = len(act)
        N = na * 128
        b = t % 4
        sp_ = t % NP
        ml = act[-1]
        kb.op("act", lambda e: e.activation(out=pT[sp_][:, :N], in_=pp[:, b, :N], func=AF.Exp), R=[f"pp{b}"], W=[f"pT{sp_}"])
        for idx, m in enumerate(act):
            ob = 4 + idx
            kb.op("pe", lambda e: e.matmul(pp[:, ob, 0:129], lhsT=pT[sp_][:, idx * 128:(idx + 1) * 128], rhs=vs[i][:, kk, 0:129],
                                           start=(kk == 0), stop=(kk == 4 * m + 3)), R=[f"pT{sp_}", f"vs{i}"], W=[f"pp{ob}"])
        if kk == 4 * ml + 3:
            ob = 4 + (na - 1)
            e2 = st["e"] % 2
            st["e"] += 1
            kb.op("dve", lambda e: e.reciprocal(out=rs[:, e2:e2 + 1], in_=pp[:, ob, 128:129]), R=[f"pp{ob}"], W=[f"rs{e2}"])
            kb.op("dve", lambda e: e.scalar_tensor_tensor(out=ogs[e2][:], in0=pp[:, ob, 0:128], scalar=rs[:, e2:e2 + 1], in1=sgh[i][:, ml, :],
                                                          op0=ALU.mult, op1=ALU.mult), R=[f"pp{ob}", f"rs{e2}", f"sgh{i}"], W=[f"ogs{e2}"])
            pt = pp[:, ob, 256:320].bitcast(BF16)
            kb.op("pe", lambda e: e.transpose(pt, ogs[e2][:], identb[:]), R=[f"ogs{e2}", "identb"], W=[f"pp{ob}"])
            kb.op("act", lambda e: e.activation(out=ogTs[i][:, ml * 128:(ml + 1) * 128], in_=pt, func=AF.Copy), R=[f"pp{ob}"], W=[f"ogTs{i}"])
        if t == last_of_head[h]:
            kb.dma([("sp", ogT_o[h * 128:(h + 1) * 128, :], ogTs[i][:])], R=[f"ogTs{i}"], W=["ogT_o"], sem=f"ogTs{i}", indep=True)

    emit_skewed(bitems, b_s1, b_s2, 3)
    kb.finish(["ogT_o"])
    print("B2a inst", kb.n_inst, "waits", kb.n_wait)
    kb.pop()
    if own:
        kb.close()
    return nc


GROUPS = [[0, 1, 2, 3], [4, 5, 6, 7]]


def build_fused(nc, S):
    T = S // 4
    kb = KB(nc)

    def ext(name, shape, dt=F32):
        return nc.dram_tensor(name, list(shape), dt, kind="ExternalInput").ap()

    def scr(name, shape, dt):
        return nc.dram_tensor(name, list(shape), dt).ap()

    E = dict(xT=ext("xT", [D, T]), x=ext("x", [T, D]), a_w_in=ext("a_w_in", [25, 128, 8192]), a_w_uq=ext("a_w_uq", [12, 128, 8192]),
             a_w_out=ext("a_w_out", [16, 128, 8192]), b_w_in=ext("b_w_in", [32, 128, 8192]), b_w_vg=ext("b_w_vg", [17, 128, 16384]), b_w_out=ext("b_w_out", [16, 128, 8192]),
             qg=ext("qg", [128, 8]), kgb=ext("kgb", [128, 256]), fb=ext("fb", [128, 32]),
             lng0=ext("lng0", [128, D]), lnb0=ext("lnb0", [128, D]), lng1=ext("lng1", [128, D]), lnb1=ext("lnb1", [128, D]),
             cosF=ext("cosF", [128, T]), sinF=ext("sinF", [128, T]), cosT=ext("cosT", [T, 64]), sinT=ext("sinT", [T, 64]),
             perm=ext("perm", [128, 128]), ident=ext("ident", [128, 128]), cm4=ext("cm4", [128, 4, 128]),
             tri4=ext("tri4", [128, 4, 128]), sel=ext("sel", [128, 4]), selh=ext("selh", [32, 32, 128]))
    y_out = nc.dram_tensor("y", [T, D], F32, kind="ExternalOutput").ap()
    qT0 = scr("s_qT0", [32, 128, T], BF16)
    qiT0 = scr("s_qiT0", [64, 128, T], BF16)
    kT0 = scr("s_kT0", [512, T], BF16)
    v0 = scr("s_v0", [T, 512], BF16)
    kiT0 = scr("s_kiT0", [128, T], BF16)
    widx0 = scr("s_widx0", [T, 64], F32)
    sg0 = scr("s_sg0", [T, D], F32)
    kT0g = scr("s_kT0g", [4 * 512, T], BF16)
    v0g = scr("s_v0g", [4 * T, 512], BF16)
    kiT0g = scr("s_kiT0g", [4 * 128, T], BF16)
    ogT0 = scr("s_ogT0", [D, T], BF16)
    x1 = scr("s_x1", [T, D], F32)
    x1T = scr("s_x1T", [D, T], BF16)
    qT1 = scr("s_qT1", [32, 128, T], BF16)
    kT1 = scr("s_kT1", [D, T], BF16)
    v1 = scr("s_v1", [T, D], BF16)
    sg1 = scr("s_sg1", [T, D], F32)
    lf1 = scr("s_lf1", [T, 32], F32)
    kT1g = scr("s_kT1g", [4 * D, T], BF16)
    v1g = scr("s_v1g", [S, D], BF16)
    lf1g = scr("s_lf1g", [4 * T, 32], F32)
    ogT1 = scr("s_ogT1", [D, T], BF16)

    phase_a1(nc, S, kb=kb, io=dict(xT=E["xT"], a_w_in=E["a_w_in"], a_w_uq=E["a_w_uq"], qg=E["qg"], kgb=E["kgb"], cosF=E["cosF"],
                                   sinF=E["sinF"], cosT=E["cosT"], sinT=E["sinT"], perm=E["perm"], ident=E["ident"],
                                   qT=qT0, qiT=qiT0, kT=kT0.rearrange("(g d) t -> g d t", g=4), v=v0, kiT=kiT0, widx=widx0, sgate=sg0))
    kb.collective("AllGather", kT0, kT0g, GROUPS)
    kb.collective("AllGather", v0, v0g, GROUPS)
    kb.collective("AllGather", kiT0, kiT0g, GROUPS)
    kb.barrier()
    phase_a2a(nc, S, kb=kb, io=dict(qT=qT0, qiT=qiT0, widx=widx0, sgate=sg0, kTg=kT0g, vg=v0g, kiTg=kiT0g, ident=E["ident"],
                                    cm4=E["cm4"], ogT=ogT0))
    phase_out(nc, S, False, kb=kb, io=dict(ogT=ogT0, w_out=E["a_w_out"], x=E["x"], lng=E["lng0"], lnb=E["lnb0"], ident=E["ident"],
                                           y=x1, yT=x1T))
    phase_b1(nc, S, kb=kb, io=dict(x1T=x1T, b_w_in=E["b_w_in"], b_w_vg=E["b_w_vg"], fb=E["fb"], qT1=qT1, kT1=kT1.rearrange("(h d) t -> h d t", h=32),
                                   v1=v1, sgate1=sg1, lf=lf1,
                                   cc=dict(kT1=kT1, kT1g=kT1g, v1=v1, v1g=v1g, lf1=lf1, lf1g=lf1g, groups=GROUPS)))
    phase_b2a(nc, S, kb=kb, io=dict(qT1=qT1, kT1g=kT1g, v1g=v1g, lfg=lf1g, sgate1=sg1, tri4=E["tri4"], ident=E["ident"],
                                    sel=E["sel"], selh=E["selh"], ogT=ogT1))
    phase_out(nc, S, True, kb=kb, io=dict(ogT=ogT1, w_out=E["b_w_out"], x=x1, lng=E["lng1"], lnb=E["lnb1"], ident=E["ident"], y=y_out))
    print("FUSED inst", kb.n_inst, "waits", kb.n_wait)
    kb.close()
    return nc


def fused_inputs(inp, S, b, j):
    pos = own_pos(S, j)
    d = a1_inputs(inp, S, b, j)
    cm4, tri4 = causal_tables(j)
    sel = np.zeros((128, 4), np.float32)
    sel[:, j] = 1.0
    selh = np.zeros((32, 32, 128), np.float32)
    for h in range(32):
        selh[h, h, :] = 1.0
    bc = lambda v: np.ascontiguousarray(np.broadcast_to(v, (128, v.shape[-1])))
    d.update(x=np.ascontiguousarray(inp["x"][b, pos, :]), a_w_out=inp["a_w_out_t"], b_w_in=inp["b_w_in_t"], b_w_vg=inp["b_w_vg_t"], b_w_out=inp["b_w_out_t"],
             fb=bc(inp["b_forget_bias"][0]), lng0=bc(inp["ln_g"][0]), lnb0=bc(inp["ln_b"][0]), lng1=bc(inp["ln_g"][1]), lnb1=bc(inp["ln_b"][1]),
             cm4=cm4, tri4=tri4, sel=sel, selh=selh)
    return d


def tile_weights(inp):
    inp = dict(inp)
    inp["a_w_in_t"] = tile_w(inp["a_w_in"][0], A_GROUPS, 32)
    inp["a_w_uq_t"] = tile_w(inp["a_w_uq"][0], UQ_GROUPS, 8)
    inp["a_w_out_t"] = tile_w(inp["a_w_out"][0], O_GROUPS, 32)
    inp["b_w_in_t"] = tile_w(inp["b_w_in"][0], B_GROUPS, 32)
    inp["b_w_vg_t"] = tile_w(inp["b_w_in"][0], BV_GROUPS, 32, row=16384)
    inp["b_w_out_t"] = tile_w(inp["b_w_out"][0], O_GROUPS, 32)
    return inp


_S = 4096


def kernel(x, a_w_in, a_q_norm_g, a_w_uq, a_kidx_norm_g, a_kidx_norm_b, a_w_out,
           b_w_in, b_forget_bias, b_w_out, ln_g, ln_b):
    S = _S
    f = lambda a: np.asarray(a, np.float32)
    inp = dict(x=f(x), a_w_in=f(a_w_in), a_q_norm_g=f(a_q_norm_g), a_w_uq=f(a_w_uq), a_kidx_norm_g=f(a_kidx_norm_g),
               a_kidx_norm_b=f(a_kidx_norm_b), a_w_out=f(a_w_out), b_w_in=f(b_w_in), b_forget_bias=f(b_forget_bias),
               b_w_out=f(b_w_out), ln_g=f(ln_g), ln_b=f(ln_b))
    inp = tile_weights(inp)
    cores = [(b, j) for b in range(2) for j in range(4)]
    nc = bass.Bass("TRN2", target_bir_lowering=False)
    build_fused(nc, S)
    ims = [fused_inputs(inp, S, b, j) for (b, j) in cores]
    res = run_bass_kernel_spmd(nc, ims, core_ids=list(range(8))).results
    out = np.zeros((2, S, 4096), np.float32)
    for ci, (b, j) in enumerate(cores):
        out[b, own_pos(S, j), :] = res[ci]["y"]
    return out
```

```python
from concourse.bass_utils import run_bass_kernel_spmd
from contextlib import ExitStack
import numpy as np
import concourse.bass as bass
import concourse.mybir as mybir

F32 = mybir.dt.float32
BF16 = mybir.dt.bfloat16
AF = mybir.ActivationFunctionType
ALU = mybir.AluOpType
AX = mybir.AxisListType


class _St:
    __slots__ = ("w", "r", "wl")

    def __init__(self):
        self.w = None
        self.r = []
        self.wl = []


class KB:
    def __init__(self, nc, same_engine_sync=("act", "dve", "pool")):
        self.nc = nc
        self.es = ExitStack()
        self.engs = {"pe": nc.tensor, "dve": nc.vector, "act": nc.scalar,
                     "pool": nc.gpsimd, "sp": nc.sync}
        self.sem = {}
        self.cnt = {}
        self.waited = {k: {} for k in self.engs}
        for k in self.engs:
            self.sem[k] = self.es.enter_context(nc.semaphore(f"s_{k}"))
            self.cnt[k] = 0
        self.same = set(same_engine_sync)
        self.st = {}
        self.dsems = {}
        self.dcnt = {}
        self.n_inst = 0
        self.n_wait = 0
        self.scopes = []
        self.ncc = 0

    def _es(self):
        return self.scopes[-1] if self.scopes else self.es

    def sbuf(self, name, shape, dtype):
        self.nalloc = getattr(self, "nalloc", 0) + 1
        return self._es().enter_context(self.nc.sbuf_tensor(f"{name}_{self.nalloc}", list(shape), dtype))

    def psum(self, name, shape, dtype=F32):
        self.nalloc = getattr(self, "nalloc", 0) + 1
        return self._es().enter_context(self.nc.psum_tensor(f"{name}_{self.nalloc}", list(shape), dtype))

    def push(self):
        self.scopes.append(ExitStack())

    def pop(self):
        self.barrier()
        self.scopes.pop().close()
        self.st = {}

    def collective(self, kind, src, dst, groups, after=()):
        name = f"cc{self.ncc}"
        self.ncc += 1
        self.dsem(name)
        self._wait("pool", self._deps(after, []))
        inst = self.nc.gpsimd.collective_compute(kind, ALU.bypass, replica_groups=groups, ins=[src.opt()], outs=[dst.opt()])
        inst.then_inc(self.dsems[name], 1)
        self.dcnt[name] += 1
        self.n_inst += 1

    def dsem(self, name):
        if name not in self.dsems:
            self.dsems[name] = self.es.enter_context(self.nc.semaphore(f"d_{name}"))
            self.dcnt[name] = 0
        return name

    def _deps(self, R, W):
        deps = []
        for k in R:
            s = self.st.get(k)
            if s is not None and s.w is not None:
                deps.append(s.w)
            if s is not None:
                deps.extend(s.wl)
        for k in W:
            s = self.st.get(k)
            if s is not None:
                if s.w is not None:
                    deps.append(s.w)
                deps.extend(s.r)
        return deps

    def _wait(self, eng, deps):
        need = {}
        wt = self.waited[eng]
        for (sname, semh, val) in deps:
            if sname == eng and eng not in self.same:
                continue
            if wt.get(sname, 0) >= val:
                continue
            if need.get(sname, (None, 0))[1] < val:
                need[sname] = (semh, val)
        for sname, (semh, val) in need.items():
            self.engs[eng].wait_ge(semh, val)
            wt[sname] = val
            self.n_wait += 1

    def _commit(self, ev, R, W):
        for k in R:
            self.st.setdefault(k, _St()).r.append(ev)
        for k in W:
            s = self.st.setdefault(k, _St())
            s.w = ev
            s.r = []

    def op(self, eng, fn, R=(), W=()):
        W = list(W) + [k for k in R if k.startswith("pp")]
        R = [k for k in R if not k.startswith("pp")]
        self._wait(eng, self._deps(R, W))
        inst = fn(self.engs[eng])
        self.cnt[eng] += 1
        inst.then_inc(self.sem[eng], 1)
        self.n_inst += 1
        ev = (eng, self.sem[eng], self.cnt[eng])
        self._commit(ev, R, W)
        return inst

    def dma(self, parts, R=(), W=(), sem=None, indep=False):
        assert sem is not None
        self.dsem(sem)
        deps = self._deps(R, () if indep else W)
        if self.dcnt[sem] > 0:
            deps.append(("D" + sem, self.dsems[sem], self.dcnt[sem]))
        for q in dict.fromkeys(p[0] for p in parts):
            self._wait(q, deps)
        for (q, o, i) in parts:
            self.engs[q].dma_start(out=o, in_=i).then_inc(self.dsems[sem], 16)
            self.dcnt[sem] += 16
            self.n_inst += 1
        ev = ("D" + sem, self.dsems[sem], self.dcnt[sem])
        if indep:
            self._commit(ev, R, ())
            for k in W:
                self.st.setdefault(k, _St()).wl.append(ev)
        else:
            self._commit(ev, R, W)

    def finish(self, keys):
        self._wait("sp", self._deps(keys, ()))

    def barrier(self):
        deps = [(k, self.sem[k], self.cnt[k]) for k in self.engs if self.cnt[k] > 0]
        deps += [("D" + s, self.dsems[s], self.dcnt[s]) for s in self.dsems if self.dcnt[s] > 0]
        for e in self.engs:
            same = self.same
            self.same = set(self.engs)
            self._wait(e, deps)
            self.same = same

    def close(self):
        self.es.close()


import ml_dtypes

NPBF = ml_dtypes.bfloat16
D = 4096
A_IN = 6336
B_IN = 16416
SCALE = 128 ** -0.5
WSC = 64 ** -0.5 * 128 ** -0.5
NEG = -30000.0


def own_pos(S, j):
    NBc = S // 512
    return np.concatenate([np.arange(128) + (j + 4 * m) * 128 for m in range(NBc)])


def rope_tables(pos):
    inv = (10000.0 ** (-np.arange(64, dtype=np.float32) / 64)).astype(np.float32)
    ang = pos.astype(np.float32)[:, None] * inv[None, :]
    cos, sin = np.cos(ang).astype(np.float32), np.sin(ang).astype(np.float32)
    cosF = np.concatenate([cos.T, cos.T], 0)
    sinF = np.concatenate([-sin.T, sin.T], 0)
    return dict(cosF=np.ascontiguousarray(cosF), sinF=np.ascontiguousarray(sinF),
                cosT=np.ascontiguousarray(cos), sinT=np.ascontiguousarray(sin))


def tile_w(W, groups, KC, row=8192):
    out = np.zeros((len(groups), 128, row), np.float32)
    for g, (c0, width) in enumerate(groups):
        blk = W[:, c0:c0 + width].reshape(KC, 128, width).transpose(1, 0, 2).reshape(128, KC * width)
        out[g, :, :KC * width] = blk
    return out


A_GROUPS = [(g * 256, 256) for g in range(8)] + [(2048, 192)] + [(2240 + g * 256, 256) for g in range(16)]
UQ_GROUPS = [(g * 1024, 1024) for g in range(12)]
O_GROUPS = [(g * 256, 256) for g in range(16)]
B_GROUPS = [(g * 256, 256) for g in range(32)]
BV_GROUPS = [(8192 + g * 512, 512) for g in range(16)] + [(16384, 32)]


def consts():
    perm = np.zeros((128, 128), np.float32)
    for d in range(128):
        perm[(d + 64) % 128, d] = 1.0
    ident = np.eye(128, dtype=np.float32)
    return dict(perm=perm, ident=ident)


def emit_skewed(items, stage1, stage2, D):
    n = len(items)
    for t in range(n + D):
        if t < n:
            stage1(items[t], t)
        if t - D >= 0:
            stage2(items[t - D], t - D)


def dram_in(nc, name, shape, dt=F32, io=None):
    if io is not None and name in io:
        assert list(io[name].shape) == list(shape), (name, io[name].shape, shape)
        return io[name]
    return nc.dram_tensor(name, list(shape), dt, kind="ExternalInput").ap()


def dram_out(nc, name, shape, dt=F32, io=None):
    if io is not None and name in io:
        assert list(io[name].shape) == list(shape), (name, io[name].shape, shape)
        return io[name]
    return nc.dram_tensor(name, list(shape), dt, kind="ExternalOutput").ap()


def phase_a1(nc, S, stages=('ii', 'v', 'ki', 'gate', 'iv'), kb=None, io=None):
    NBc = S // 512
    T = NBc * 128
    TW = min(T, 512)
    TH = T // TW
    xT = dram_in(nc, "xT", [D, T], F32, io)
    w_in = dram_in(nc, "a_w_in", [25, 128, 8192], F32, io)
    w_uq = dram_in(nc, "a_w_uq", [12, 128, 8192], F32, io)
    qg = dram_in(nc, "qg", [128, 8], F32, io)
    kgb = dram_in(nc, "kgb", [128, 256], F32, io)
    cosF_d = dram_in(nc, "cosF", [128, T], F32, io)
    sinF_d = dram_in(nc, "sinF", [128, T], F32, io)
    cosT_d = dram_in(nc, "cosT", [T, 64], F32, io)
    sinT_d = dram_in(nc, "sinT", [T, 64], F32, io)
    perm_d = dram_in(nc, "perm", [128, 128], F32, io)
    ident_d = dram_in(nc, "ident", [128, 128], F32, io)
    qT_o = dram_out(nc, "qT", [32, 128, T], BF16, io)
    qiT_o = dram_out(nc, "qiT", [64, 128, T], BF16, io)
    kT_o = dram_out(nc, "kT", [4, 128, T], BF16, io)
    v_o = dram_out(nc, "v", [T, 512], BF16, io)
    kiT_o = dram_out(nc, "kiT", [128, T], BF16, io)
    widx_o = dram_out(nc, "widx", [T, 64], F32, io)
    sg_o = dram_out(nc, "sgate", [T, 4096], F32, io)

    own = kb is None
    kb = KB(nc) if own else kb
    kb.push()
    xb = kb.sbuf("xb", [128, 32, T], BF16)
    NWS = 2
    wslot = [kb.sbuf(f"wslot{i}", [128, 8192], BF16) for i in range(NWS)]
    cqg = kb.sbuf("cqg", [128, 8, T], BF16)
    cosF = kb.sbuf("cosFs", [128, T], F32)
    sinF = kb.sbuf("sinFs", [128, T], F32)
    cosT = kb.sbuf("cosTs", [128, NBc, 64], F32)
    sinT = kb.sbuf("sinTs", [128, NBc, 64], F32)
    crq = kb.sbuf("crq", [128, T], F32)
    srq = kb.sbuf("srq", [128, T], F32)
    cri = kb.sbuf("cri", [128, T], F32)
    sri = kb.sbuf("sri", [128, T], F32)
    rstd = kb.sbuf("rstd", [128, T], F32)
    qgs = kb.sbuf("qgs", [128, 8], F32)
    kgbs = kb.sbuf("kgbs", [128, 256], F32)
    permf = kb.sbuf("permf", [128, 128], F32)
    permb = kb.sbuf("permb", [128, 128], BF16)
    identf = kb.sbuf("identf", [128, 128], F32)
    identb = kb.sbuf("identb", [128, 128], BF16)
    onesf = kb.sbuf("onesf", [128, 128], F32)
    NR = 2
    sq = [kb.sbuf(f"sq{i}", [128, 512], F32) for i in range(NR)]
    xbr = [kb.sbuf(f"xbr{i}", [128, 512], BF16) for i in range(NR)]
    ra = [kb.sbuf(f"ra{i}", [128, 512], F32) for i in range(NR)]
    rb = [kb.sbuf(f"rb{i}", [128, 512], F32) for i in range(NR)]
    ro = [kb.sbuf(f"ro{i}", [128, 512], BF16) for i in range(NR)]
    go = [kb.sbuf(f"go{i}", [128, 256], F32) for i in range(NR)]
    vo = [kb.sbuf(f"vo{i}", [128, 256], BF16) for i in range(NR)]
    kis = kb.sbuf("kis", [128, 192], F32)
    kin = kb.sbuf("kin", [128, 128], F32)
    kir = kb.sbuf("kir", [128, 128], F32)
    kit = kb.sbuf("kit", [128, 64], F32)
    kib = kb.sbuf("kib", [128, 128], BF16)
    kiTs = kb.sbuf("kiTs", [128, T], BF16)
    wio = kb.sbuf("wio", [128, NBc, 64], F32)
    bst = kb.sbuf("bst", [128, 6], F32)
    mv = kb.sbuf("mv", [128, 4], F32)
    pp = kb.psum("pp", [128, 8, 512], F32)
    ptb = kb.psum("ptb", [128, 128], BF16) if False else None

    xTr = xT.rearrange("(k p) t -> p k t", p=128)
    for kq in range(4):
        kb.dma([("pool", xb[:, kq * 8:(kq + 1) * 8, :], xTr[:, kq * 8:(kq + 1) * 8, :])], W=[f"xb{kq}"], sem=f"xb{kq}")
    XB = [f"xb{kq}" for kq in range(4)]
    kb.dma([("sp", cosF[:], cosF_d), ("sp", sinF[:], sinF_d), ("sp", qgs[:], qg), ("sp", kgbs[:], kgb),
            ("sp", permf[:], perm_d), ("sp", identf[:], ident_d),
            ("sp", cosT[:], cosT_d.rearrange("(m p) c -> p m c", p=128)),
            ("sp", sinT[:], sinT_d.rearrange("(m p) c -> p m c", p=128))], W=["consts"], sem="consts")
    kb.op("dve", lambda e: e.tensor_copy(out=permb[:], in_=permf[:]), R=["consts"], W=["permb"])
    kb.op("dve", lambda e: e.tensor_copy(out=identb[:], in_=identf[:]), R=["consts"], W=["identb"])
    kb.op("dve", lambda e: e.memset(onesf[:], 1.0), W=["onesf"])

    wstate = {"n": 0}

    def load_w(wt, g, kchunks, ncols):
        i = wstate["n"] % NWS
        wstate["n"] += 1
        n = kchunks * ncols
        view = wslot[i][:, 0:n].rearrange("p (k c) -> p k c", k=kchunks)
        step = min(n, 2048)
        parts = [("pool", wslot[i][:, 0:n].rearrange("p (a e) -> p a e", e=step), wt[g][:, 0:n].rearrange("p (a e) -> p a e", e=step))]
        kb.dma(parts, W=[f"w{i}"], sem=f"w{i}")
        return view, f"w{i}"

    rr = {"n": 0}

    def rope_tile(src_ps, pskey, N, cr, sr, crkeys, dst_dram, dstkey):
        i = rr["n"] % NR
        rr["n"] += 1
        pb = 4 + (rr["n"] % 2)
        kb.op("act", lambda e: e.activation(out=xbr[i][:, :N], in_=src_ps, func=AF.Copy), R=[pskey], W=[f"xbr{i}"])
        kb.op("pe", lambda e: e.matmul(pp[:, pb, :N], lhsT=permb[:], rhs=xbr[i][:, :N], start=True, stop=True),
              R=[f"xbr{i}", "permb"], W=[f"pp{pb}"])
        kb.op("dve", lambda e: e.tensor_tensor(out=ra[i][:, :N], in0=src_ps, in1=cr, op=ALU.mult),
              R=[pskey] + crkeys, W=[f"ra{i}"])
        kb.op("dve", lambda e: e.tensor_tensor(out=rb[i][:, :N], in0=pp[:, pb, :N], in1=sr, op=ALU.mult),
              R=[f"pp{pb}"] + crkeys, W=[f"rb{i}"])
        kb.op("dve", lambda e: e.tensor_tensor(out=ro[i][:, :N], in0=ra[i][:, :N], in1=rb[i][:, :N], op=ALU.add),
              R=[f"ra{i}", f"rb{i}"], W=[f"ro{i}"])
        kb.dma([("sp", dst_dram, ro[i][:, :N])], R=[f"ro{i}"], W=[dstkey], sem=f"ro{i}")

    mm = {"n": 0}

    def next_bank():
        b = mm["n"] % 3
        mm["n"] += 1
        return b

    for g4 in range(4):
        wv, wk = load_w(w_in, g4, 32, 256)
        for fl in range(2):
            fc = g4 * 2 + fl
            for th in range(TH):
                b = next_bank()
                for k in range(32):
                    kb.op("pe", lambda e: e.matmul(pp[:, b, :TW], lhsT=wv[:, k, fl * 128:(fl + 1) * 128],
                                                   rhs=xb[:, k, th * TW:(th + 1) * TW], start=(k == 0), stop=(k == 31)),
                          R=[wk] + XB, W=[f"pp{b}"])
                kb.op("dve", lambda e: e.tensor_scalar(out=cqg[:, fc, th * TW:(th + 1) * TW], in0=pp[:, b, :TW],
                                                       scalar1=qgs[:, fc:fc + 1], scalar2=None, op0=ALU.mult),
                      R=[f"pp{b}", "consts"], W=[f"cqg{th}"])
                i = (fc * TH + th) % NR
                kb.op("act", lambda e: e.activation(out=sq[i][:, :TW], in_=pp[:, b, :TW], func=AF.Square),
                      R=[f"pp{b}"], W=[f"sq{i}"])
                kb.op("pe", lambda e: e.matmul(pp[:, 6 + th, :TW], lhsT=onesf[:], rhs=sq[i][:, :TW],
                                               start=(fc == 0), stop=(fc == 7)),
                      R=[f"sq{i}", "onesf"], W=[f"pp{6 + th}"])
    for th in range(TH):
        sl = slice(th * TW, (th + 1) * TW)
        kb.op("act", lambda e: e.activation(out=rstd[:, sl], in_=pp[:, 6 + th, :TW], func=AF.Sqrt, scale=1.0 / 1024, bias=1e-6),
              R=[f"pp{6 + th}"], W=["rstd"])
    kb.op("dve", lambda e: e.reciprocal(out=rstd[:], in_=rstd[:]), R=["rstd"], W=["rstd"])
    kb.op("dve", lambda e: e.tensor_tensor(out=cri[:], in0=cosF[:], in1=rstd[:], op=ALU.mult), R=["rstd", "consts"], W=["cri"])
    kb.op("dve", lambda e: e.tensor_tensor(out=sri[:], in0=sinF[:], in1=rstd[:], op=ALU.mult), R=["rstd", "consts"], W=["sri"])
    kb.op("pool", lambda e: e.tensor_scalar(out=crq[:], in0=cri[:], scalar1=SCALE, scalar2=None, op0=ALU.mult), R=["cri"], W=["crq"])
    kb.op("pool", lambda e: e.tensor_scalar(out=srq[:], in0=sri[:], scalar1=SCALE, scalar2=None, op0=ALU.mult), R=["sri"], W=["srq"])

    for g2 in (range(2) if 'ii' in stages else []):
        wv, wk = load_w(w_in, 4 + g2, 32, 256)
        for fl in range(2):
            g = g2 * 2 + fl
            for th in range(TH):
                b = next_bank()
                for k in range(32):
                    kb.op("pe", lambda e: e.matmul(pp[:, b, :TW], lhsT=wv[:, k, fl * 128:(fl + 1) * 128],
                                                   rhs=xb[:, k, th * TW:(th + 1) * TW], start=(k == 0), stop=(k == 31)),
                          R=[wk] + XB, W=[f"pp{b}"])
                sl = slice(th * TW, (th + 1) * TW)
                rope_tile(pp[:, b, :TW], f"pp{b}", TW, cosF[:, sl], sinF[:, sl], ["consts"], kT_o[g, :, sl], f"kT{g}")

    def tokmajor(gidx, ncols, handler):
        wv, wk = load_w(w_in, gidx, 32, ncols)
        for tb in range(NBc):
            b = next_bank()
            for k in range(32):
                kb.op("pe", lambda e: e.matmul(pp[:, b, :ncols], lhsT=xb[:, k, tb * 128:(tb + 1) * 128],
                                               rhs=wv[:, k, :], start=(k == 0), stop=(k == 31)),
                      R=[wk] + XB, W=[f"pp{b}"])
            handler(tb, b)

    tm = {"n": 0}

    def h_v(c0v):
        def f(tb, b):
            i = tm["n"] % NR
            tm["n"] += 1
            kb.op("act", lambda e: e.activation(out=vo[i][:], in_=pp[:, b, :256], func=AF.Copy), R=[f"pp{b}"], W=[f"vo{i}"])
            kb.dma([("sp", v_o[tb * 128:(tb + 1) * 128, c0v:c0v + 256], vo[i][:])], R=[f"vo{i}"], W=["v_o"], sem=f"vo{i}", indep=True)
        return f

    if 'v' in stages:
        tokmajor(6, 256, h_v(0))
        tokmajor(7, 256, h_v(256))

    def h_ki(tb, b):
        kb.op("act", lambda e: e.activation(out=kis[:], in_=pp[:, b, :192], func=AF.Copy), R=[f"pp{b}"], W=["kis"])
        kb.op("dve", lambda e: e.tensor_scalar(out=wio[:, tb, :], in0=kis[:, 128:192], scalar1=WSC, scalar2=None, op0=ALU.mult),
              R=["kis"], W=["wio"])
        kb.op("dve", lambda e: e.bn_stats(out=bst[:], in_=kis[:, 0:128]), R=["kis"], W=["bst"])
        kb.op("dve", lambda e: e.bn_aggr(out=mv[:, 0:2], in_=bst[:]), R=["bst"], W=["mv"])
        kb.op("act", lambda e: e.activation(out=mv[:, 2:3], in_=mv[:, 1:2], func=AF.Sqrt, scale=1.0, bias=1e-5), R=["mv"], W=["mv"])
        kb.op("dve", lambda e: e.reciprocal(out=mv[:, 3:4], in_=mv[:, 2:3]), R=["mv"], W=["mv"])
        kb.op("dve", lambda e: e.tensor_scalar(out=kin[:], in0=kis[:, 0:128], scalar1=mv[:, 0:1], scalar2=mv[:, 3:4],
                                               op0=ALU.subtract, op1=ALU.mult), R=["kis", "mv"], W=["kin"])
        kb.op("dve", lambda e: e.tensor_tensor(out=kin[:], in0=kin[:], in1=kgbs[:, 0:128], op=ALU.mult), R=["kin", "consts"], W=["kin"])
        kb.op("dve", lambda e: e.tensor_tensor(out=kin[:], in0=kin[:], in1=kgbs[:, 128:256], op=ALU.add), R=["kin", "consts"], W=["kin"])
        c, s = cosT[:, tb, :], sinT[:, tb, :]
        kb.op("dve", lambda e: e.tensor_tensor(out=kir[:, 0:64], in0=kin[:, 0:64], in1=c, op=ALU.mult), R=["kin", "consts"], W=["kir"])
        kb.op("dve", lambda e: e.tensor_tensor(out=kit[:], in0=kin[:, 64:128], in1=s, op=ALU.mult), R=["kin", "consts"], W=["kit"])
        kb.op("dve", lambda e: e.tensor_tensor(out=kir[:, 0:64], in0=kir[:, 0:64], in1=kit[:], op=ALU.subtract), R=["kir", "kit"], W=["kir"])
        kb.op("dve", lambda e: e.tensor_tensor(out=kir[:, 64:128], in0=kin[:, 64:128], in1=c, op=ALU.mult), R=["kin", "consts"], W=["kir"])
        kb.op("dve", lambda e: e.tensor_tensor(out=kit[:], in0=kin[:, 0:64], in1=s, op=ALU.mult), R=["kin", "consts", "kir"], W=["kit"])
        kb.op("dve", lambda e: e.tensor_tensor(out=kib[:, 64:128], in0=kir[:, 64:128], in1=kit[:], op=ALU.add), R=["kir", "kit"], W=["kib"])
        kb.op("dve", lambda e: e.tensor_copy(out=kib[:, 0:64], in_=kir[:, 0:64]), R=["kir"], W=["kib"])
        pt = pp[:, 7, 0:64].bitcast(BF16)
        kb.op("pe", lambda e: e.transpose(pt, kib[:], identb[:]), R=["kib", "identb"], W=["pp7"])
        kb.op("act", lambda e: e.activation(out=kiTs[:, tb * 128:(tb + 1) * 128], in_=pt, func=AF.Copy), R=["pp7"], W=["kiTs"])

    if 'ki' in stages:
        tokmajor(8, 192, h_ki)
        kb.dma([("sp", kiT_o, kiTs[:])], R=["kiTs"], W=["kiT_o"], sem="kiTo")
        kb.dma([("sp", widx_o.rearrange("(m p) c -> p m c", p=128), wio[:])], R=["wio"], W=["widx_o"], sem="wio")

    def h_gate(c0g):
        def f(tb, b):
            i = tm["n"] % NR
            tm["n"] += 1
            kb.op("act", lambda e: e.activation(out=go[i][:], in_=pp[:, b, :256], func=AF.Silu), R=[f"pp{b}"], W=[f"go{i}"])
            kb.dma([("sp", sg_o[tb * 128:(tb + 1) * 128, c0g:c0g + 256], go[i][:])], R=[f"go{i}"], W=["sg_o"], sem=f"go{i}", indep=True)
        return f

    for gg in (range(16) if 'gate' in stages else []):
        tokmajor(9 + gg, 256, h_gate(gg * 256))

    for g12 in (range(12) if 'iv' in stages else []):
        wv, wk = load_w(w_uq, g12, 8, 1024)
        for hl in range(8):
            hh = g12 * 8 + hl
            for th in range(TH):
                b = next_bank()
                for k in range(8):
                    kb.op("pe", lambda e: e.matmul(pp[:, b, :TW], lhsT=wv[:, k, hl * 128:(hl + 1) * 128],
                                                   rhs=cqg[:, k, th * TW:(th + 1) * TW], start=(k == 0), stop=(k == 7)),
                          R=[wk] + [f"cqg{t}" for t in range(TH)], W=[f"pp{b}"])
                sl = slice(th * TW, (th + 1) * TW)
                if hh < 32:
                    rope_tile(pp[:, b, :TW], f"pp{b}", TW, crq[:, sl], srq[:, sl], ["crq", "srq"], qT_o[hh, :, sl], f"qT{hh}")
                else:
                    rope_tile(pp[:, b, :TW], f"pp{b}", TW, cri[:, sl], sri[:, sl], ["cri", "sri"], qiT_o[hh - 32, :, sl], f"qiT{hh}")
    outs = [f"kT{g}" for g in range(4)] + [f"qT{h}" for h in range(32)] + [f"qiT{h}" for h in range(32, 96)] + \
           ["v_o", "kiT_o", "widx_o", "sg_o"]
    kb.finish([o for o in outs if o in kb.st])
    print("A1 inst", kb.n_inst, "waits", kb.n_wait)
    kb.pop()
    if own:
        kb.close()
    return nc


def a1_inputs(inp, S, b, j):
    pos = own_pos(S, j)
    rt = rope_tables(pos)
    c = consts()
    qg = np.ascontiguousarray(inp["a_q_norm_g"][0].reshape(8, 128).T)
    kgb = np.concatenate([np.broadcast_to(inp["a_kidx_norm_g"][0][None, :], (128, 128)),
                          np.broadcast_to(inp["a_kidx_norm_b"][0][None, :], (128, 128))], 1)
    return dict(xT=np.ascontiguousarray(inp["x"][b, pos, :].T), a_w_in=inp["a_w_in_t"], a_w_uq=inp["a_w_uq_t"],
                qg=qg, kgb=np.ascontiguousarray(kgb), perm=c["perm"], ident=c["ident"], **rt)


def phase_a2a(nc, S, kb=None, io=None):
    NB = S // 128
    NBc = S // 512
    T = NBc * 128
    qT = dram_in(nc, "qT", [32, 128, T], BF16, io)
    qiT = dram_in(nc, "qiT", [64, 128, T], BF16, io)
    widx = dram_in(nc, "widx", [T, 64], F32, io)
    sgate = dram_in(nc, "sgate", [T, 4096], F32, io)
    kTf = dram_in(nc, "kTg", [4 * 512, T], BF16, io)
    vf = dram_in(nc, "vg", [4 * T, 512], BF16, io)
    kiTf = dram_in(nc, "kiTg", [4 * 128, T], BF16, io)
    ident_d = dram_in(nc, "ident", [128, 128], F32, io)
    cm4_d = dram_in(nc, "cm4", [128, 4, 128], F32, io)
    ogT_o = dram_out(nc, "ogT", [4096, T], BF16, io)

    own = kb is None
    kb = KB(nc) if own else kb
    kb.push()
    kis = kb.sbuf("kiTs", [128, S], BF16)
    kts = kb.sbuf("kTs", [128, 4, S], BF16)
    vs = kb.sbuf("vs", [128, NB, 4, 130], BF16)
    qib = kb.sbuf("qib", [128, 64, 128], BF16)
    Dg = kb.sbuf("Dg", [128, 64, 128], BF16)
    wix = kb.sbuf("wix", [128, 64], F32)
    score2 = [kb.sbuf(f"score{i}", [128, S], F32) for i in range(2)]
    mb = kb.sbuf("mb", [128, S], BF16)
    mbT = kb.sbuf("mbT", [128, NB, 128], BF16)
    qblk = kb.sbuf("qblk", [128, 32, 128], BF16)
    identf = kb.sbuf("identf", [128, 128], F32)
    identb = kb.sbuf("identb", [128, 128], BF16)
    cm4 = kb.sbuf("cm4s", [128, 4, 128], F32)
    half = kb.sbuf("half", [128, 1], F32)
    bs = kb.sbuf("bs", [128, 8], F32)
    KIT = 26
    pw2 = kb.sbuf("pw2", [128, KIT + 1], F32)
    hht = kb.sbuf("hht", [128, KIT + 1], F32)
    rl = [kb.sbuf(f"rl{i}", [128, 512], BF16) for i in range(6)]
    sm = [kb.sbuf(f"sm{i}", [128, 512], F32) for i in range(4)]
    pT = [kb.sbuf(f"pT{i}", [128, 512], BF16) for i in range(4)]
    sgb = [kb.sbuf(f"sgb{i}", [128, 512], F32) for i in range(2)]
    og4 = kb.sbuf("og4", [128, 512], BF16)
    ogT4 = [kb.sbuf(f"ogT4{i}", [128, 4, 128], BF16) for i in range(2)]
    rs = kb.sbuf("rs", [128, 4], F32)
    pp = kb.psum("pp", [128, 8, 512], F32)

    kb.dma([("sp", identf[:], ident_d), ("sp", cm4[:], cm4_d)] +
           [("sp", kis[:].rearrange("d (m j p) -> d m j p", j=4, p=128)[:, :, j, :],
             kiTf[j * 128:(j + 1) * 128, :].rearrange("d (m p) -> d m p", p=128)) for j in range(4)], W=["consts"], sem="consts")
    kb.dma([("sp", kts[:, g, :].rearrange("d (m j p) -> d m j p", j=4, p=128)[:, :, j, :],
             kTf[j * 512 + g * 128:j * 512 + (g + 1) * 128, :].rearrange("d (m p) -> d m p", p=128))
            for g in range(4) for j in range(4)], W=["kts"], sem="kts")
    kb.op("pool", lambda e: e.memset(vs[:], 1.0), W=["vs"])
    kb.dma([("sp", vs[:, :, g, 0:128].rearrange("p (m j) d -> p m j d", j=4)[:, :, j, :],
             vf[j * T:(j + 1) * T, g * 128:(g + 1) * 128].rearrange("(m p) d -> p m d", p=128))
            for g in range(4) for j in range(4)], W=["vs"], sem="vs")
    kb.op("dve", lambda e: e.tensor_copy(out=identb[:], in_=identf[:]), R=["consts"], W=["identb"])
    kb.op("dve", lambda e: e.memset(half[:], 0.5), W=["half"])
    for k in range(KIT + 1):
        kb.op("pool", lambda e: e.memset(pw2[:, k:k + 1], 2.0 ** -(k + 1)), W=["pw2"])
    LO, HI, MID, CNT, GE, D1, D2 = [slice(i, i + 1) for i in range(7)]
    ev = {"n": 0}

    def stage_I(m):
        nk = 4 * m + 4
        SK = nk * 128
        qs = slice(m * 128, (m + 1) * 128)
        score = score2[m % 2]
        skey = f"score{m % 2}"
        junk = mb
        kb.dma([("sp", qib[:], qiT[:, :, qs].rearrange("h d q -> d h q")), ("sp", wix[:], widx[qs, :])], W=["qib", "wix"], sem="qload")
        for h in range(64):
            eng = "dve" if h % 2 == 0 else "pool"
            kb.op(eng, lambda e: e.tensor_scalar(out=Dg[:, h, :], in0=identf[:], scalar1=wix[:, h:h + 1], scalar2=None, op0=ALU.mult),
                  R=["consts", "wix"], W=[f"Dg{h}"])
        DB = (0, 1, 2, 5, 6, 7)
        items = [(c0, min(512, SK - c0), h) for c0 in range(0, SK, 512) for h in range(64)]

        def idx_s1(it, t):
            c0, N, h = it
            b = DB[t % 6]
            kb.op("pe", lambda e: e.matmul(pp[:, b, :N], lhsT=qib[:, h, :], rhs=kis[:, c0:c0 + N], start=True, stop=True),
                  R=["qib", "consts"], W=[f"pp{b}"])

        def idx_s2(it, t):
            c0, N, h = it
            b = DB[t % 6]
            i = t % 6
            if True:
                kb.op("act", lambda e: e.activation(out=rl[i][:, :N], in_=pp[:, b, :N], func=AF.Relu), R=[f"pp{b}"], W=[f"rl{i}"])
            else:
                kb.op("dve", lambda e: e.tensor_scalar(out=rl[i][:, :N], in0=pp[:, b, :N], scalar1=0.0, scalar2=None, op0=ALU.max),
                      R=[f"pp{b}"], W=[f"rl{i}"])
            kb.op("pe", lambda e: e.matmul(pp[:, 3, :N], lhsT=Dg[:, h, :], rhs=rl[i][:, :N], start=(h == 0), stop=(h == 63)),
                  R=[f"rl{i}", f"Dg{h}"], W=["pp3"])
            if h == 63:
                kb.op("act", lambda e: e.activation(out=score[:, c0:c0 + N], in_=pp[:, 3, :N], func=AF.Copy), R=["pp3"], W=[skey])

        emit_skewed(items, idx_s1, idx_s2, 4)

    def stage_B(m):
        nk = 4 * m + 4
        SK = nk * 128
        qs = slice(m * 128, (m + 1) * 128)
        score = score2[m % 2]
        skey = f"score{m % 2}"
        junk = mb
        W0 = slice(5, 6)
        SG = slice(4, 5)
        kb.op("dve", lambda e: e.tensor_reduce(out=bs[:, LO], in_=score[:, :SK], op=ALU.min, axis=AX.X), R=[skey], W=["bs"])
        kb.op("dve", lambda e: e.tensor_reduce(out=bs[:, HI], in_=score[:, :SK], op=ALU.max, axis=AX.X), R=[skey], W=["bs"])
        kb.op("dve", lambda e: e.tensor_tensor(out=score[:, SK - 512:SK], in0=score[:, SK - 512:SK],
                                               in1=cm4[:].rearrange("p a b -> p (a b)"), op=ALU.add), R=[skey, "consts"], W=[skey])
        kb.op("dve", lambda e: e.tensor_tensor(out=bs[:, W0], in0=bs[:, HI], in1=bs[:, LO], op=ALU.subtract), R=["bs"], W=["bs"])
        kb.op("dve", lambda e: e.tensor_scalar(out=bs[:, W0], in0=bs[:, W0], scalar1=1.01, scalar2=1e-5, op0=ALU.mult, op1=ALU.add), R=["bs"], W=["bs"])
        kb.op("dve", lambda e: e.tensor_scalar(out=hht[:], in0=pw2[:], scalar1=bs[:, W0], scalar2=None, op0=ALU.mult), R=["bs", "pw2"], W=["hh"])
        kb.op("dve", lambda e: e.tensor_tensor(out=bs[:, MID], in0=bs[:, HI], in1=hht[:, 0:1], op=ALU.subtract), R=["bs", "hh"], W=["bs"])
        for it in range(KIT):
            kb.op("dve", lambda e: e.tensor_scalar(out=junk[:, :SK], in0=score[:, :SK], scalar1=bs[:, MID], scalar2=None,
                                                   op0=ALU.is_ge, op1=ALU.add, accum_out=bs[:, CNT]), R=[skey, "bs"], W=["mb", "bs"])
            kb.op("dve", lambda e: e.tensor_scalar(out=bs[:, SG], in0=bs[:, CNT], scalar1=255.5, scalar2=0.5, op0=ALU.is_ge, op1=ALU.subtract),
                  R=["bs"], W=["bs"])
            kb.op("dve", lambda e: e.scalar_tensor_tensor(out=bs[:, MID], in0=bs[:, SG], scalar=hht[:, it:it + 1], in1=bs[:, MID],
                                                          op0=ALU.mult, op1=ALU.add), R=["bs", "hh"], W=["bs"])
        kb.op("dve", lambda e: e.tensor_tensor(out=bs[:, LO], in0=bs[:, MID], in1=hht[:, KIT:KIT + 1], op=ALU.subtract), R=["bs", "hh"], W=["bs"])
        kb.op("dve", lambda e: e.tensor_scalar(out=mb[:, :SK], in0=score[:, :SK], scalar1=bs[:, LO], scalar2=NEG, op0=ALU.is_lt, op1=ALU.mult),
              R=[skey, "bs"], W=["mb"])
        for k4 in range(0, nk, 4):
            pt = pp[:, 4, 0:256].bitcast(BF16).rearrange("p (a b) -> p a b", a=4)
            for a in range(4):
                kb.op("pe", lambda e: e.transpose(pt[:, a, :], mb[:, (k4 + a) * 128:(k4 + a + 1) * 128], identb[:]), R=["mb", "identb"], W=["pp4"])
            kb.op("act", lambda e: e.activation(out=mbT[:, k4:k4 + 4, :], in_=pt, func=AF.Copy), R=["pp4"], W=["mbT"])

    def stage_A(m):
        nk = 4 * m + 4
        SK = nk * 128
        qs = slice(m * 128, (m + 1) * 128)
        score = score2[m % 2]
        skey = f"score{m % 2}"
        junk = mb
        kb.dma([("sp", qblk[:], qT[:, :, qs].rearrange("h d q -> d h q"))], W=["qblk"], sem="qblk")
        aitems = [(g, hp, kk) for g in range(4) for hp in range(2) for kk in range(nk)]

        def att_s1(it, t):
            g, hp, kk = it
            h0 = g * 8 + hp * 4
            b = t % 4
            if kk == 0:
                sgi = (g * 2 + hp) % 2
                kb.dma([("sp", sgb[sgi][:], sgate[qs, h0 * 128:(h0 + 4) * 128])], W=[f"sgb{sgi}"], sem=f"sgb{sgi}")
            kb.op("pe", lambda e: e.matmul(pp[:, b, :], lhsT=kts[:, g, kk * 128:(kk + 1) * 128],
                                           rhs=qblk[:, h0:h0 + 4, :].rearrange("p h q -> p (h q)"), start=True, stop=True),
                  R=["kts", "qblk"], W=[f"pp{b}"])

        def att_s2(it, t):
            g, hp, kk = it
            h0 = g * 8 + hp * 4
            b = t % 4
            i = t % 4
            sgi = (g * 2 + hp) % 2
            kb.op("dve", lambda e: e.tensor_tensor(out=sm[i][:].rearrange("p (h q) -> p h q", h=4),
                                                   in0=pp[:, b, :].rearrange("p (h q) -> p h q", h=4),
                                                   in1=mbT[:, kk:kk + 1, :].to_broadcast([128, 4, 128]), op=ALU.add),
                  R=[f"pp{b}", "mbT"], W=[f"sm{i}"])
            kb.op("act", lambda e: e.activation(out=pT[i][:], in_=sm[i][:], func=AF.Exp), R=[f"sm{i}"], W=[f"pT{i}"])
            for hh in range(4):
                kb.op("pe", lambda e: e.matmul(pp[:, 4 + hh, 0:129], lhsT=pT[i][:, hh * 128:(hh + 1) * 128], rhs=vs[:, kk, g, 0:129],
                                               start=(kk == 0), stop=(kk == nk - 1)), R=[f"pT{i}", "vs"], W=[f"pp{4 + hh}"])
            if kk == nk - 1:
                for hh in range(4):
                    kb.op("dve", lambda e: e.reciprocal(out=rs[:, hh:hh + 1], in_=pp[:, 4 + hh, 128:129]), R=[f"pp{4 + hh}"], W=["rs"])
                    kb.op("dve", lambda e: e.scalar_tensor_tensor(out=og4[:, hh * 128:(hh + 1) * 128], in0=pp[:, 4 + hh, 0:128],
                                                                  scalar=rs[:, hh:hh + 1], in1=sgb[sgi][:, hh * 128:(hh + 1) * 128],
                                                                  op0=ALU.mult, op1=ALU.mult), R=[f"pp{4 + hh}", "rs", f"sgb{sgi}"], W=["og4"])
                i2 = ev["n"] % 2
                ev["n"] += 1
                pt = pp[:, 4, 256:512].bitcast(BF16).rearrange("p (a b) -> p a b", a=4)
                for hh in range(4):
                    kb.op("pe", lambda e: e.transpose(pt[:, hh, :], og4[:, hh * 128:(hh + 1) * 128], identb[:]), R=["og4", "identb"], W=["pp4"])
                kb.op("act", lambda e: e.activation(out=ogT4[i2][:], in_=pt, func=AF.Copy), R=["pp4"], W=[f"ogT4{i2}"])
                kb.dma([("sp", ogT_o[h0 * 128:(h0 + 4) * 128, qs].rearrange("(h d) q -> d h q", h=4), ogT4[i2][:])],
                       R=[f"ogT4{i2}"], W=["ogT_o"], sem=f"ogT4{i2}", indep=True)

        emit_skewed(aitems, att_s1, att_s2, 3)

    stage_I(0)
    for m in range(NBc):
        if m + 1 < NBc:
            stage_I(m + 1)
        stage_B(m)
        stage_A(m)
    kb.finish(["ogT_o"])
    print("A2a inst", kb.n_inst, "waits", kb.n_wait)
    kb.pop()
    if own:
        kb.close()
    return nc


def causal_tables(j):
    q = np.arange(128)[:, None]
    s = np.arange(128)[None, :]
    cm4 = np.zeros((128, 4, 128), np.float32)
    tri4 = np.zeros((128, 4, 128), np.float32)
    for r in range(4):
        if r == j:
            cm4[:, r, :] = np.where(s <= q, 0.0, -1e30)
            tri4[:, r, :] = np.where(s.T <= q.T, 0.0, NEG)
        elif r > j:
            cm4[:, r, :] = -1e30
            tri4[:, r, :] = NEG
    return cm4, tri4


def phase_out(nc, S, final, kb=None, io=None):
    NBc = S // 512
    T = NBc * 128
    HB = min(4, NBc)
    ogT = dram_in(nc, "ogT", [4096, T], BF16, io)
    w_out = dram_in(nc, "w_out", [16, 128, 8192], F32, io)
    x_d = dram_in(nc, "x", [T, 4096], F32, io)
    lng_d = dram_in(nc, "lng", [128, 4096], F32, io)
    lnb_d = dram_in(nc, "lnb", [128, 4096], F32, io)
    ident_d = dram_in(nc, "ident", [128, 128], F32, io)
    y_o = dram_out(nc, "y", [T, 4096], F32, io)
    yT_o = None if final else dram_out(nc, "yT", [4096, T], BF16, io)
    ALPHA = 4 ** 0.25

    own = kb is None
    kb = KB(nc) if own else kb
    kb.push()
    ogb = kb.sbuf("ogb", [128, 32, HB * 128], BF16)
    wslot = [kb.sbuf(f"wslot{i}", [128, 32, 256], BF16) for i in range(2)]
    ybuf = kb.sbuf("ybuf", [128, HB, 4096], F32)
    lng = kb.sbuf("lngs", [128, 4096], F32)
    lnb = kb.sbuf("lnbs", [128, 4096], F32)
    identf = kb.sbuf("identf", [128, 128], F32)
    bst = kb.sbuf("bst", [128, 8, 6], F32)
    mv = kb.sbuf("mv", [128, 4], F32)
    yTs = kb.sbuf("yTs", [128, 32, 128], BF16)
    pp = kb.psum("pp", [128, 8, 512], F32)
    kb.dma([("sp", lng[:], lng_d), ("sp", lnb[:], lnb_d), ("sp", identf[:], ident_d)], W=["consts"], sem="consts")
    ogr = ogT.rearrange("(c p) t -> p c t", p=128)
    wn = {"n": 0}
    for hf in range(NBc // HB):
        t0 = hf * HB * 128
        kb.dma([("sp", ogb[:, a * 8:(a + 1) * 8, :], ogr[:, a * 8:(a + 1) * 8, t0:t0 + HB * 128]) for a in range(4)], W=["ogb"], sem="ogb")
        kb.dma([("sp", ybuf[:, tb, :], x_d[t0 + tb * 128:t0 + (tb + 1) * 128, :]) for tb in range(HB)], W=["ybuf"], sem="ybuf")
        for cg in range(16):
            i = wn["n"] % 2
            wn["n"] += 1
            wfl = wslot[i][:].rearrange("p k c -> p (k c)")
            kb.dma([("pool", wfl.rearrange("p (a e) -> p a e", e=2048), w_out[cg].rearrange("p (a e) -> p a e", e=2048))], W=[f"w{i}"], sem=f"w{i}")
            for tb in range(HB):
                b = (cg * HB + tb) % 4
                for c in range(32):
                    kb.op("pe", lambda e: e.matmul(pp[:, b, :256], lhsT=ogb[:, c, tb * 128:(tb + 1) * 128], rhs=wslot[i][:, c, :],
                                                   start=(c == 0), stop=(c == 31)), R=["ogb", f"w{i}"], W=[f"pp{b}"])
                ysl = ybuf[:, tb, cg * 256:(cg + 1) * 256]
                kb.op("dve", lambda e: e.scalar_tensor_tensor(out=ysl, in0=ysl, scalar=ALPHA, in1=pp[:, b, :256], op0=ALU.mult, op1=ALU.add),
                      R=[f"pp{b}"], W=["ybuf"])
        for tb in range(HB):
            for c8 in range(8):
                kb.op("dve", lambda e: e.bn_stats(out=bst[:, c8, :], in_=ybuf[:, tb, c8 * 512:(c8 + 1) * 512]), R=["ybuf"], W=["bst"])
            kb.op("dve", lambda e: e.bn_aggr(out=mv[:, 0:2], in_=bst[:].rearrange("p a b -> p (a b)")), R=["bst"], W=["mv"])
            kb.op("act", lambda e: e.activation(out=mv[:, 2:3], in_=mv[:, 1:2], func=AF.Sqrt, scale=1.0, bias=1e-5), R=["mv"], W=["mv"])
            kb.op("dve", lambda e: e.reciprocal(out=mv[:, 3:4], in_=mv[:, 2:3]), R=["mv"], W=["mv"])
            kb.op("dve", lambda e: e.tensor_scalar(out=ybuf[:, tb, :], in0=ybuf[:, tb, :], scalar1=mv[:, 0:1], scalar2=mv[:, 3:4],
                                                   op0=ALU.subtract, op1=ALU.mult), R=["mv"], W=["ybuf"])
            kb.op("dve", lambda e: e.tensor_tensor(out=ybuf[:, tb, :], in0=ybuf[:, tb, :], in1=lng[:], op=ALU.mult), R=["consts"], W=["ybuf"])
            kb.op("dve", lambda e: e.tensor_tensor(out=ybuf[:, tb, :], in0=ybuf[:, tb, :], in1=lnb[:], op=ALU.add), R=["consts"], W=["ybuf"])
            kb.dma([("sp", y_o[t0 + tb * 128:t0 + (tb + 1) * 128, :], ybuf[:, tb, :])], R=["ybuf"], W=["y_o"], sem="yo", indep=True)
            if not final:
                for c4 in range(8):
                    b = 4 + c4 % 2
                    for a in range(4):
                        c = c4 * 4 + a
                        kb.op("pe", lambda e: e.transpose(pp[:, b, a * 128:(a + 1) * 128], ybuf[:, tb, c * 128:(c + 1) * 128], identf[:]),
                              R=["ybuf", "consts"], W=[f"pp{b}"])
                    kb.op("act", lambda e: e.activation(out=yTs[:, c4 * 4:(c4 + 1) * 4, :].rearrange("p a b -> p (a b)"), in_=pp[:, b, :],
                                                        func=AF.Copy), R=[f"pp{b}"], W=["yTs"])
                kb.dma([("sp", yT_o.rearrange("(c p) t -> p c t", p=128)[:, :, t0 + tb * 128:t0 + (tb + 1) * 128], yTs[:])],
                       R=["yTs"], W=["yT_o"], sem="yTo", indep=True)
    kb.finish(["y_o", "yT_o"])
    print("OUT inst", kb.n_inst, "waits", kb.n_wait)
    kb.pop()
    if own:
        kb.close()
    return nc


def phase_b1(nc, S, kb=None, io=None):
    NBc = S // 512
    T = NBc * 128
    TW = min(T, 512)
    TH = T // TW
    x1T = dram_in(nc, "x1T", [4096, T], BF16, io)
    w_in = dram_in(nc, "b_w_in", [32, 128, 8192], F32, io)
    w_vg = dram_in(nc, "b_w_vg", [17, 128, 16384], F32, io)
    fb_d = dram_in(nc, "fb", [128, 32], F32, io)
    qT_o = dram_out(nc, "qT1", [32, 128, T], BF16, io)
    kT_o = dram_out(nc, "kT1", [32, 128, T], BF16, io)
    v_o = dram_out(nc, "v1", [T, 4096], BF16, io)
    sg_o = dram_out(nc, "sgate1", [T, 4096], F32, io)
    lf_o = dram_out(nc, "lf", [T, 32], F32, io)

    own = kb is None
    kb = KB(nc) if own else kb
    kb.push()
    xb = kb.sbuf("xb", [128, 32, T], BF16)
    wslot = [kb.sbuf(f"wslot{i}", [128, 32, 512], BF16) for i in range(2)]
    wstage = kb.sbuf("wstage", [128, 8192], F32)
    fbs = kb.sbuf("fbs", [128, 32], F32)
    NR = 4
    ro = [kb.sbuf(f"ro{i}", [128, 512], BF16) for i in range(NR)]
    go = [kb.sbuf(f"go{i}", [128, 512], F32) for i in range(NR)]
    vo = [kb.sbuf(f"vo{i}", [128, 512], BF16) for i in range(NR)]
    zz = kb.sbuf("zz", [128, 32], F32)
    lfo = kb.sbuf("lfo", [128, NBc, 32], F32)
    pp = kb.psum("pp", [128, 8, 512], F32)
    xr = x1T.rearrange("(k p) t -> p k t", p=128)
    kb.dma([("sp", xb[:, a * 8:(a + 1) * 8, :], xr[:, a * 8:(a + 1) * 8, :]) for a in range(4)], W=["xb"], sem="xb")
    kb.dma([("sp", fbs[:], fb_d)], W=["fbs"], sem="fbs")
    wn = {"n": 0}

    def load_w(gidx, ncols, wt=None):
        wt = w_in if wt is None else wt
        i = wn["n"] % 2
        wn["n"] += 1
        n = 32 * ncols
        step = min(n, 2048)
        wfl = wslot[i][:].rearrange("p k c -> p (k c)")
        if True:
            kb.dma([("pool", wfl[:, 0:n].rearrange("p (a e) -> p a e", e=step), wt[gidx][:, 0:n].rearrange("p (a e) -> p a e", e=step))],
                   W=[f"w{i}a", f"w{i}b"], sem=f"w{i}")
        else:
            kb.dma([("sp", wstage[:, a:a + step], w_in[gidx][:, a:a + step]) for a in range(0, n, step)], W=["wstage"], sem="wstage")
            for a in range(0, n, 4096):
                e_ = min(n, a + 4096)
                kb.op("dve", lambda e: e.tensor_copy(out=wfl[:, a:e_], in_=wstage[:, a:e_]), R=["wstage"], W=[f"w{i}a" if a == 0 else f"w{i}b"])
        return wfl[:, 0:n].rearrange("p (k c) -> p k c", k=32), f"w{i}"

    mm = {"n": 0}
    cc = io.get("cc") if io is not None else None
    if cc is not None:
        nck = max(1, (4096 * T * 2) >> 20)
        rc = 4096 // nck
    for gi in range(32):
        wv, wk = load_w(gi, 256)
        for fl in range(2):
            hh = gi * 2 + fl
            for th in range(TH):
                b = mm["n"] % 4
                mm["n"] += 1
                for k in range(32):
                    kb.op("pe", lambda e: e.matmul(pp[:, b, :TW], lhsT=wv[:, k, fl * 128:(fl + 1) * 128], rhs=xb[:, k, th * TW:(th + 1) * TW],
                                                   start=(k == 0), stop=(k == 31)), R=[wk + "a", wk + "b", "xb"], W=[f"pp{b}"])
                i = mm["n"] % NR
                sl = slice(th * TW, (th + 1) * TW)
                if hh < 32:
                    kb.op("act", lambda e: e.activation(out=ro[i][:, :TW], in_=pp[:, b, :TW], func=AF.Copy, scale=SCALE), R=[f"pp{b}"], W=[f"ro{i}"])
                    kb.dma([("sp", qT_o[hh, :, sl], ro[i][:, :TW])], R=[f"ro{i}"], W=["qT_o"], sem=f"ro{i}", indep=True)
                else:
                    kb.op("dve", lambda e: e.tensor_copy(out=ro[i][:, :TW], in_=pp[:, b, :TW]), R=[f"pp{b}"], W=[f"ro{i}"])
                    kb.dma([("sp", kT_o[hh - 32, :, sl], ro[i][:, :TW])], R=[f"ro{i}"], W=["kT_o"], sem=f"ro{i}", indep=True)
        if cc is not None and gi >= 16:
            rows_done = (gi - 16 + 1) * 256
            if rows_done % rc == 0:
                c = rows_done // rc - 1
                kb.collective("AllGather", cc["kT1"][c * rc:(c + 1) * rc, :], cc["kT1g"][c * 4 * rc:(c + 1) * 4 * rc, :], cc["groups"], after=["kT_o"])

    def tokmajor(gidx, ncols, handler):
        wv, wk = load_w(gidx, ncols, w_vg)
        for tb in range(NBc):
            b = mm["n"] % 4
            mm["n"] += 1
            for k in range(32):
                kb.op("pe", lambda e: e.matmul(pp[:, b, :ncols], lhsT=xb[:, k, tb * 128:(tb + 1) * 128], rhs=wv[:, k, :],
                                               start=(k == 0), stop=(k == 31)), R=[wk + "a", wk + "b", "xb"], W=[f"pp{b}"])
            handler(tb, b)

    tm = {"n": 0}

    def h_v(c0v):
        def f(tb, b):
            i = tm["n"] % NR
            tm["n"] += 1
            kb.op("dve", lambda e: e.tensor_copy(out=vo[i][:], in_=pp[:, b, :512]), R=[f"pp{b}"], W=[f"vo{i}"])
            kb.dma([("sp", v_o[tb * 128:(tb + 1) * 128, c0v:c0v + 512], vo[i][:])], R=[f"vo{i}"], W=["v_o"], sem=f"vo{i}", indep=True)
        return f

    def h_gate(c0g):
        def f(tb, b):
            i = tm["n"] % NR
            tm["n"] += 1
            kb.op("act", lambda e: e.activation(out=go[i][:], in_=pp[:, b, :512], func=AF.Silu), R=[f"pp{b}"], W=[f"go{i}"])
            kb.dma([("sp", sg_o[tb * 128:(tb + 1) * 128, c0g:c0g + 512], go[i][:])], R=[f"go{i}"], W=["sg_o"], sem=f"go{i}", indep=True)
        return f

    for gi in range(8):
        tokmajor(gi, 512, h_v(gi * 512))
    if cc is not None:
        for m in range(NBc):
            kb.collective("AllGather", cc["v1"][m * 128:(m + 1) * 128, :], cc["v1g"][m * 512:(m + 1) * 512, :], cc["groups"], after=["v_o"])
    for gi in range(8):
        tokmajor(8 + gi, 512, h_gate(gi * 512))

    def h_f(tb, b):
        kb.op("dve", lambda e: e.tensor_tensor(out=zz[:], in0=pp[:, b, :32], in1=fbs[:], op=ALU.add), R=[f"pp{b}", "fbs"], W=["zz"])
        kb.op("act", lambda e: e.activation(out=zz[:], in_=zz[:], func=AF.Exp, scale=-1.0), R=["zz"], W=["zz"])
        kb.op("act", lambda e: e.activation(out=zz[:], in_=zz[:], func=AF.Ln, scale=1.0, bias=1.0), R=["zz"], W=["zz"])
        kb.op("dve", lambda e: e.tensor_scalar(out=lfo[:, tb, :], in0=zz[:], scalar1=-1.0, scalar2=None, op0=ALU.mult), R=["zz"], W=["lfo"])

    tokmajor(16, 32, h_f)
    kb.dma([("sp", lf_o.rearrange("(m p) h -> p m h", p=128), lfo[:])], R=["lfo"], W=["lf_o"], sem="lfo")
    if cc is not None:
        kb.collective("AllGather", cc["lf1"], cc["lf1g"], cc["groups"], after=["lf_o"])
    kb.finish(["qT_o", "kT_o", "v_o", "sg_o", "lf_o"])
    print("B1 inst", kb.n_inst, "waits", kb.n_wait)
    kb.pop()
    if own:
        kb.close()
    return nc


def phase_b2a(nc, S, kb=None, io=None):
    NB = S // 128
    NBc = S // 512
    T = NBc * 128
    qT = dram_in(nc, "qT1", [32, 128, T], BF16, io)
    kTf = dram_in(nc, "kT1g", [4 * 4096, T], BF16, io)
    vf = dram_in(nc, "v1g", [S, 4096], BF16, io)
    nck = max(1, (4096 * T * 2) >> 20)
    rc = 4096 // nck
    lff = dram_in(nc, "lfg", [4 * T, 32], F32, io)
    sgate = dram_in(nc, "sgate1", [T, 4096], F32, io)
    tri4_d = dram_in(nc, "tri4", [128, 4, 128], F32, io)
    ident_d = dram_in(nc, "ident", [128, 128], F32, io)
    sel_d = dram_in(nc, "sel", [128, 4], F32, io)
    selh_d = dram_in(nc, "selh", [32, 32, 128], F32, io)
    ogT_o = dram_out(nc, "ogT", [4096, T], BF16, io)

    own = kb is None
    kb = KB(nc) if own else kb
    kb.push()
    identf = kb.sbuf("identf", [128, 128], F32)
    identb = kb.sbuf("identb", [128, 128], BF16)
    tri4 = kb.sbuf("tri4s", [128, 4, 128], F32)
    tri4b = kb.sbuf("tri4b", [128, 4, 128], BF16)
    sel = kb.sbuf("sels", [128, 4], F32)
    lfs = kb.sbuf("lfs", [128, NB, 32], F32)
    lfT = kb.sbuf("lfT", [32, S], F32)
    onesT = kb.sbuf("onesT", [32, S], F32)
    cT = kb.sbuf("cT", [32, S], F32)
    ocT = kb.sbuf("ocT", [32, NBc, 128], F32)
    ocR = kb.sbuf("ocR", [32, NBc, 128], F32)
    r1 = kb.sbuf("r1", [32, S], F32)
    nhi = kb.sbuf("nhi", [32, S], BF16)
    nmid = kb.sbuf("nmid", [32, S], BF16)
    nlo = kb.sbuf("nlo", [32, S], BF16)
    ohi = kb.sbuf("ohi", [32, T], BF16)
    omid = kb.sbuf("omid", [32, T], BF16)
    olo = kb.sbuf("olo", [32, T], BF16)
    augl = [kb.sbuf(f"augl{i}", [128, S], BF16) for i in range(2)]
    augr = [kb.sbuf(f"augr{i}", [128, T], BF16) for i in range(2)]
    kts = [kb.sbuf(f"kts{i}", [128, S], BF16) for i in range(2)]
    vs = [kb.sbuf(f"vs{i}", [128, NB, 130], BF16) for i in range(2)]
    qhr = [kb.sbuf(f"qhr{i}", [128, T], BF16) for i in range(2)]
    sgh = [kb.sbuf(f"sgh{i}", [128, NBc, 128], F32) for i in range(2)]
    NP = 4
    pT = [kb.sbuf(f"pT{i}", [128, 512], BF16) for i in range(NP)]
    ogs = [kb.sbuf(f"ogs{i}", [128, 128], BF16) for i in range(2)]
    ogTs = [kb.sbuf(f"ogTs{i}", [128, T], BF16) for i in range(2)]
    rs = kb.sbuf("rs", [128, 2], F32)
    pp = kb.psum("pp", [128, 8, 512], F32)

    groups = [list(range(min(g0 + 4, NBc) - 1, g0 - 1, -1)) for g0 in range(0, NBc, 4)]
    slot = {}
    for gi, grp in enumerate(groups):
        for idx, m in enumerate(grp):
            slot[m] = gi * 4 + idx

    kb.dma([("sp", identf[:], ident_d), ("sp", tri4[:], tri4_d), ("sp", sel[:], sel_d)] +
           [("sp", lfs[:].rearrange("p (m j) h -> p m j h", j=4)[:, :, j, :],
             lff[j * T:(j + 1) * T, :].rearrange("(m p) h -> p m h", p=128)) for j in range(4)], W=["consts"], sem="consts")
    kb.op("dve", lambda e: e.tensor_copy(out=identb[:], in_=identf[:]), R=["consts"], W=["identb"])
    kb.op("dve", lambda e: e.tensor_copy(out=tri4b[:], in_=tri4[:]), R=["consts"], W=["tri4b"])
    kb.op("dve", lambda e: e.memset(onesT[:], 1.0), W=["onesT"])
    for i in range(2):
        kb.op("pool", lambda e: e.memset(vs[i][:], 1.0), W=[f"vs{i}"])
        kb.op("pool", lambda e: e.memset(augl[i][:], 0.0), W=[f"augl{i}"])
        kb.op("pool", lambda e: e.memset(augr[i][:], 0.0), W=[f"augr{i}"])
        kb.op("pool", lambda e: e.memset(augl[i][32:35, :], 1.0), W=[f"augl{i}"])
        kb.op("pool", lambda e: e.memset(augr[i][0:3, :], 1.0), W=[f"augr{i}"])
    for k4 in range(0, NB, 4):
        for a in range(4):
            kb.op("pe", lambda e: e.transpose(pp[0:32, 0, a * 128:(a + 1) * 128], lfs[:, k4 + a, :], identf[:]), R=["consts"], W=["pp0"])
        kb.op("act", lambda e: e.activation(out=lfT[:, k4 * 128:(k4 + 4) * 128], in_=pp[0:32, 0, :], func=AF.Copy), R=["pp0"], W=["lfT"])
    kb.op("dve", lambda e: e.tensor_tensor_scan(out=cT[:], data0=onesT[:], data1=lfT[:], initial=0.0, op0=ALU.mult, op1=ALU.add),
          R=["onesT", "lfT"], W=["cT"])
    cT4 = cT[:].rearrange("p (m r q) -> p m r q", r=4, q=128)
    kb.op("dve", lambda e: e.tensor_scalar(out=ocT[:], in0=cT4[:, :, 0, :], scalar1=sel[0:32, 0:1], scalar2=None, op0=ALU.mult), R=["cT", "consts"], W=["ocT"])
    for r in range(1, 4):
        kb.op("dve", lambda e: e.scalar_tensor_tensor(out=ocT[:], in0=cT4[:, :, r, :], scalar=sel[0:32, r:r + 1], in1=ocT[:], op0=ALU.mult, op1=ALU.add),
              R=["cT", "consts"], W=["ocT"])
    for m in range(NBc):
        kb.op("dve", lambda e: e.tensor_copy(out=ocR[:, slot[m], :], in_=ocT[:, m, :]), R=["ocT"], W=["ocR"])

    def split3(src, hi, mid, lo, n, neg, key):
        sg = -1.0 if neg else 1.0
        kb.op("dve", lambda e: e.tensor_scalar(out=hi, in0=src, scalar1=sg, scalar2=None, op0=ALU.mult), R=[key], W=[key + "hi"])
        kb.op("dve", lambda e: e.scalar_tensor_tensor(out=r1[:, :n], in0=src, scalar=sg, in1=hi, op0=ALU.mult, op1=ALU.subtract),
              R=[key, key + "hi"], W=["r1"])
        kb.op("dve", lambda e: e.tensor_copy(out=mid, in_=r1[:, :n]), R=["r1"], W=[key + "mid"])
        kb.op("dve", lambda e: e.tensor_tensor(out=r1[:, :n], in0=r1[:, :n], in1=mid, op=ALU.subtract), R=[key + "mid"], W=["r1"])
        kb.op("dve", lambda e: e.tensor_copy(out=lo, in_=r1[:, :n]), R=["r1"], W=[key + "lo"])

    split3(cT[:], nhi[:], nmid[:], nlo[:], S, True, "cT")
    split3(ocR[:].rearrange("p a b -> p (a b)"), ohi[:], omid[:], olo[:], T, False, "ocR")
    CK = ["cThi", "cTmid", "cTlo", "ocRhi", "ocRmid", "ocRlo"]

    st = {"e": 0}
    bitems = []
    for h in range(32):
        first = True
        for gi, grp in enumerate(groups):
            for kk in range(4 * grp[0] + 4):
                bitems.append((h, gi, kk, first))
                first = False
    last_of_head = {}
    for t, it in enumerate(bitems):
        last_of_head[it[0]] = t

    def b_s1(it, t):
        h, gi, kk, first = it
        i = h % 2
        grp = groups[gi]
        g0 = gi * 4 * 128
        if first:
            kb.dma([("sp", sgh[i][:], sgate[:, h * 128:(h + 1) * 128].rearrange("(m p) d -> p m d", p=128))] +
                   [("sp", qhr[i][:, slot[m] * 128:(slot[m] + 1) * 128], qT[h][:, m * 128:(m + 1) * 128]) for m in range(NBc)] +
                   [("sp", kts[i][:].rearrange("d (m j p) -> d m j p", j=4, p=128)[:, :, j, :],
                     kTf[((h * 128) // rc) * 4 * rc + j * rc + (h * 128) % rc:((h * 128) // rc) * 4 * rc + j * rc + (h * 128) % rc + 128, :]
                     .rearrange("d (m p) -> d m p", p=128)) for j in range(4)] +
                   [("sp", vs[i][:, :, 0:128], vf[:, h * 128:(h + 1) * 128].rearrange("(kb p) d -> p kb d", p=128))] +
                   [("sp", augl[i][a:a + 1, :], t_[h:h + 1, :]) for a, t_ in enumerate((nhi, nmid, nlo))] +
                   [("sp", augr[i][32 + a:33 + a, :], t_[h:h + 1, :]) for a, t_ in enumerate((ohi, omid, olo))],
                   R=CK, W=[f"kts{i}", f"qhr{i}", f"sgh{i}", f"vs{i}", f"augl{i}", f"augr{i}"], sem=f"hl{i}")
        act = [m for m in grp if kk < 4 * m + 4]
        na = len(act)
        N = na * 128
        b = t % 4
        ml = act[-1]
        tri = kk >= 4 * ml
        kb.op("pe", lambda e: e.matmul(pp[:, b, :N], lhsT=kts[i][:, kk * 128:(kk + 1) * 128], rhs=qhr[i][:, g0:g0 + N],
                                       start=True, stop=False), R=[f"kts{i}", f"qhr{i}"], W=[f"pp{b}"])
        kb.op("pe", lambda e: e.matmul(pp[:, b, :N], lhsT=augl[i][:, kk * 128:(kk + 1) * 128], rhs=augr[i][:, g0:g0 + N],
                                       start=False, stop=(not tri)), R=[f"augl{i}", f"augr{i}"], W=[f"pp{b}"])
        if tri:
            kb.op("pe", lambda e: e.matmul(pp[:, b, (na - 1) * 128:na * 128], lhsT=identb[:], rhs=tri4b[:, kk - 4 * ml, :],
                                           start=False, stop=True), R=["identb", "tri4b"], W=[f"pp{b}"])

    def b_s2(it, t):
        h, gi, kk, first = it
        i = h % 2
        grp = groups[gi]
        act = [m for m in grp if kk < 4 * m + 4]
        na = len(act)
        N = na * 128
        b = t % 4
        sp_ = t % NP
        ml = act[-1]
        kb.op("act", lambda e: e.activation(out=pT[sp_][:, :N], in_=pp[:, b, :N], func=AF.Exp), R=[f"pp{b}"], W=[f"pT{sp_}"])
        for idx, m in enumerate(act):
            ob = 4 + idx
            kb.op("pe", lambda e: e.matmul(pp[:, ob, 0:129], lhsT=pT[sp_][:, idx * 128:(idx + 1) * 128], rhs=vs[i][:, kk, 0:129],
                                           start=(kk == 0), stop=(kk == 4 * m + 3)), R=[f"pT{sp_}", f"vs{i}"], W=[f"pp{ob}"])
        if kk == 4 * ml + 3:
            ob = 4 + (na - 1)
            e2 = st["e"] % 2
            st["e"] += 1
            kb.op("dve", lambda e: e.reciprocal(out=rs[:, e2:e2 + 1], in_=pp[:, ob, 128:129]), R=[f"pp{ob}"], W=[f"rs{e2}"])
            kb.op("dve", lambda e: e.scalar_tensor_tensor(out=ogs[e2][:], in0=pp[:, ob, 0:128], scalar=rs[:, e2:e2 + 1], in1=sgh[i][:, ml, :],
                                                          op0=ALU.mult, op1=ALU.mult), R=[f"pp{ob}", f"rs{e2}", f"sgh{i}"], W=[f"ogs{e2}"])
            pt = pp[:, ob, 256:320].bitcast(BF16)
            kb.op("pe", lambda e: e.transpose(pt, ogs[e2][:], identb[:]), R=[f"ogs{e2}", "identb"], W=[f"pp{ob}"])
            kb.op("act", lambda e: e.activation(out=ogTs[i][:, ml * 128:(ml + 1) * 128], in_=pt, func=AF.Copy), R=[f"pp{ob}"], W=[f"ogTs{i}"])
        if t == last_of_head[h]:
            kb.dma([("sp", ogT_o[h * 128:(h + 1) * 128, :], ogTs[i][:])], R=[f"ogTs{i}"], W=["ogT_o"], sem=f"ogTs{i}", indep=True)

    emit_skewed(bitems, b_s1, b_s2, 3)
    kb.finish(["ogT_o"])
    print("B2a inst", kb.n_inst, "waits", kb.n_wait)
    kb.pop()
    if own:
        kb.close()
    return nc


GROUPS = [[0, 1, 2, 3], [4, 5, 6, 7]]


def build_fused(nc, S):
    T = S // 4
    kb = KB(nc)

    def ext(name, shape, dt=F32):
        return nc.dram_tensor(name, list(shape), dt, kind="ExternalInput").ap()

    def scr(name, shape, dt):
        return nc.dram_tensor(name, list(shape), dt).ap()

    E = dict(xT=ext("xT", [D, T]), x=ext("x", [T, D]), a_w_in=ext("a_w_in", [25, 128, 8192]), a_w_uq=ext("a_w_uq", [12, 128, 8192]),
             a_w_out=ext("a_w_out", [16, 128, 8192]), b_w_in=ext("b_w_in", [32, 128, 8192]), b_w_vg=ext("b_w_vg", [17, 128, 16384]), b_w_out=ext("b_w_out", [16, 128, 8192]),
             qg=ext("qg", [128, 8]), kgb=ext("kgb", [128, 256]), fb=ext("fb", [128, 32]),
             lng0=ext("lng0", [128, D]), lnb0=ext("lnb0", [128, D]), lng1=ext("lng1", [128, D]), lnb1=ext("lnb1", [128, D]),
             cosF=ext("cosF", [128, T]), sinF=ext("sinF", [128, T]), cosT=ext("cosT", [T, 64]), sinT=ext("sinT", [T, 64]),
             perm=ext("perm", [128, 128]), ident=ext("ident", [128, 128]), cm4=ext("cm4", [128, 4, 128]),
             tri4=ext("tri4", [128, 4, 128]), sel=ext("sel", [128, 4]), selh=ext("selh", [32, 32, 128]))
    y_out = nc.dram_tensor("y", [T, D], F32, kind="ExternalOutput").ap()
    qT0 = scr("s_qT0", [32, 128, T], BF16)
    qiT0 = scr("s_qiT0", [64, 128, T], BF16)
    kT0 = scr("s_kT0", [512, T], BF16)
    v0 = scr("s_v0", [T, 512], BF16)
    kiT0 = scr("s_kiT0", [128, T], BF16)
    widx0 = scr("s_widx0", [T, 64], F32)
    sg0 = scr("s_sg0", [T, D], F32)
    kT0g = scr("s_kT0g", [4 * 512, T], BF16)
    v0g = scr("s_v0g", [4 * T, 512], BF16)
    kiT0g = scr("s_kiT0g", [4 * 128, T], BF16)
    ogT0 = scr("s_ogT0", [D, T], BF16)
    x1 = scr("s_x1", [T, D], F32)
    x1T = scr("s_x1T", [D, T], BF16)
    qT1 = scr("s_qT1", [32, 128, T], BF16)
    kT1 = scr("s_kT1", [D, T], BF16)
    v1 = scr("s_v1", [T, D], BF16)
    sg1 = scr("s_sg1", [T, D], F32)
    lf1 = scr("s_lf1", [T, 32], F32)
    kT1g = scr("s_kT1g", [4 * D, T], BF16)
    v1g = scr("s_v1g", [S, D], BF16)
    lf1g = scr("s_lf1g", [4 * T, 32], F32)
    ogT1 = scr("s_ogT1", [D, T], BF16)

    phase_a1(nc, S, kb=kb, io=dict(xT=E["xT"], a_w_in=E["a_w_in"], a_w_uq=E["a_w_uq"], qg=E["qg"], kgb=E["kgb"], cosF=E["cosF"],
                                   sinF=E["sinF"], cosT=E["cosT"], sinT=E["sinT"], perm=E["perm"], ident=E["ident"],
                                   qT=qT0, qiT=qiT0, kT=kT0.rearrange("(g d) t -> g d t", g=4), v=v0, kiT=kiT0, widx=widx0, sgate=sg0))
    kb.collective("AllGather", kT0, kT0g, GROUPS)
    kb.collective("AllGather", v0, v0g, GROUPS)
    kb.collective("AllGather", kiT0, kiT0g, GROUPS)
    kb.barrier()
    phase_a2a(nc, S, kb=kb, io=dict(qT=qT0, qiT=qiT0, widx=widx0, sgate=sg0, kTg=kT0g, vg=v0g, kiTg=kiT0g, ident=E["ident"],
                                    cm4=E["cm4"], ogT=ogT0))
    phase_out(nc, S, False, kb=kb, io=dict(ogT=ogT0, w_out=E["a_w_out"], x=E["x"], lng=E["lng0"], lnb=E["lnb0"], ident=E["ident"],
                                           y=x1, yT=x1T))
    phase_b1(nc, S, kb=kb, io=dict(x1T=x1T, b_w_in=E["b_w_in"], b_w_vg=E["b_w_vg"], fb=E["fb"], qT1=qT1, kT1=kT1.rearrange("(h d) t -> h d t", h=32),
                                   v1=v1, sgate1=sg1, lf=lf1,
                                   cc=dict(kT1=kT1, kT1g=kT1g, v1=v1, v1g=v1g, lf1=lf1, lf1g=lf1g, groups=GROUPS)))
    phase_b2a(nc, S, kb=kb, io=dict(qT1=qT1, kT1g=kT1g, v1g=v1g, lfg=lf1g, sgate1=sg1, tri4=E["tri4"], ident=E["ident"],
                                    sel=E["sel"], selh=E["selh"], ogT=ogT1))
    phase_out(nc, S, True, kb=kb, io=dict(ogT=ogT1, w_out=E["b_w_out"], x=x1, lng=E["lng1"], lnb=E["lnb1"], ident=E["ident"], y=y_out))
    print("FUSED inst", kb.n_inst, "waits", kb.n_wait)
    kb.close()
    return nc


def fused_inputs(inp, S, b, j):
    pos = own_pos(S, j)
    d = a1_inputs(inp, S, b, j)
    cm4, tri4 = causal_tables(j)
    sel = np.zeros((128, 4), np.float32)
    sel[:, j] = 1.0
    selh = np.zeros((32, 32, 128), np.float32)
    for h in range(32):
        selh[h, h, :] = 1.0
    bc = lambda v: np.ascontiguousarray(np.broadcast_to(v, (128, v.shape[-1])))
    d.update(x=np.ascontiguousarray(inp["x"][b, pos, :]), a_w_out=inp["a_w_out_t"], b_w_in=inp["b_w_in_t"], b_w_vg=inp["b_w_vg_t"], b_w_out=inp["b_w_out_t"],
             fb=bc(inp["b_forget_bias"][0]), lng0=bc(inp["ln_g"][0]), lnb0=bc(inp["ln_b"][0]), lng1=bc(inp["ln_g"][1]), lnb1=bc(inp["ln_b"][1]),
             cm4=cm4, tri4=tri4, sel=sel, selh=selh)
    return d


def tile_weights(inp):
    inp = dict(inp)
    inp["a_w_in_t"] = tile_w(inp["a_w_in"][0], A_GROUPS, 32)
    inp["a_w_uq_t"] = tile_w(inp["a_w_uq"][0], UQ_GROUPS, 8)
    inp["a_w_out_t"] = tile_w(inp["a_w_out"][0], O_GROUPS, 32)
    inp["b_w_in_t"] = tile_w(inp["b_w_in"][0], B_GROUPS, 32)
    inp["b_w_vg_t"] = tile_w(inp["b_w_in"][0], BV_GROUPS, 32, row=16384)
    inp["b_w_out_t"] = tile_w(inp["b_w_out"][0], O_GROUPS, 32)
    return inp


_S = 4096


def kernel(x, a_w_in, a_q_norm_g, a_w_uq, a_kidx_norm_g, a_kidx_norm_b, a_w_out,
           b_w_in, b_forget_bias, b_w_out, ln_g, ln_b):
    S = _S
    f = lambda a: np.asarray(a, np.float32)
    inp = dict(x=f(x), a_w_in=f(a_w_in), a_q_norm_g=f(a_q_norm_g), a_w_uq=f(a_w_uq), a_kidx_norm_g=f(a_kidx_norm_g),
               a_kidx_norm_b=f(a_kidx_norm_b), a_w_out=f(a_w_out), b_w_in=f(b_w_in), b_forget_bias=f(b_forget_bias),
               b_w_out=f(b_w_out), ln_g=f(ln_g), ln_b=f(ln_b))
    inp = tile_weights(inp)
    cores = [(b, j) for b in range(2) for j in range(4)]
    nc = bass.Bass("TRN2", target_bir_lowering=False)
    build_fused(nc, S)
    ims = [fused_inputs(inp, S, b, j) for (b, j) in cores]
    res = run_bass_kernel_spmd(nc, ims, core_ids=list(range(8))).results
    out = np.zeros((2, S, 4096), np.float32)
    for ci, (b, j) in enumerate(cores):
        out[b, own_pos(S, j), :] = res[ci]["y"]
    return out
```
